# Optimizing a Trainium2 kernel written in Bass

```python
import jax
import jax.numpy as jnp
from jax import lax
import numpy as np

D_MODEL = 1024
BATCH = 32
SEQ = 2048
DEPTH = 4

GRID_W = 64
CTX_LEN = 256
EPS = 1e-6
NEG = -1e30
ROPE_THETA = 10000.0
Q_BLOCK = 128
CHUNK = 64

A_HEADS = 4
A_DK = 128
A_DV = 128
B_HEADS = 4
B_DQK = 64
B_DV = 128
B_CONV = 3
C_HEADS = 8
C_KV_HEADS = 2
C_DH = 64
D_HEADS = 4
D_Q_LORA = 256
D_KV_LORA = 128
D_NOPE = 128
D_ROPE = 64
D_DV = 128
N_BRANCH = 4
BRANCH_W = 512
D_FF = 4 * D_MODEL

KEY_COLS = (
    ('a_i', A_HEADS * A_DV),
    ('a_f_fwd', A_HEADS * A_DK),
    ('a_f_bwd', A_HEADS * A_DK),
    ('b_k', B_HEADS * B_DQK),
    ('b_v', B_HEADS * B_DV),
    ('b_gates', 4 * B_HEADS),
    ('c_k', C_KV_HEADS * C_DH),
    ('c_v', C_KV_HEADS * C_DH),
    ('d_ckv', D_KV_LORA),
    ('d_krope', D_ROPE),
)
QUERY_COLS = (
    ('a_q', A_HEADS * A_DK),
    ('a_g', A_HEADS * A_DV),
    ('b_q', B_HEADS * B_DQK),
    ('b_o', B_HEADS * B_DV),
    ('c_q', C_HEADS * C_DH),
    ('d_cq', D_Q_LORA),
    ('gates', N_BRANCH * D_MODEL),
)
KEY_WIDTH = sum(w for _, w in KEY_COLS)
IN_WIDTH = KEY_WIDTH + sum(w for _, w in QUERY_COLS)

kernel_name = 'hybrid_parallel_mixer_dit_trunk'


def rms_norm(x, g):
    xf = x.astype(jnp.float32)
    y = xf * lax.rsqrt(jnp.mean(xf * xf, axis=-1, keepdims=True) + EPS)
    return (y * g.astype(jnp.float32)).astype(x.dtype)


def modulate(h, shift, scale):
    return h * (1.0 + scale) + shift


def split_cols(p, layout):
    sizes = [w for _, w in layout]
    parts = jnp.split(p, np.cumsum(sizes)[:-1].tolist(), axis=-1)
    return {name: part for (name, _), part in zip(layout, parts)}


def split_heads(x, h):
    return x.reshape(x.shape[0], x.shape[1], h, -1)


def to_heads(x, h):
    return split_heads(x, h).transpose(0, 2, 1, 3)


def from_heads(x):
    b, h, n, d = x.shape
    return x.transpose(0, 2, 1, 3).reshape(b, n, h * d)


def flip_time(t, rev):
    return jnp.flip(t, axis=2) if rev else t


def to_chunks(x):
    b, h, n = x.shape[:3]
    x = x.reshape(b, h, n // CHUNK, CHUNK, *x.shape[3:])
    return jnp.moveaxis(x, 2, 0)


def from_chunks(x):
    x = jnp.moveaxis(x, 0, 2)
    b, h, nc, l = x.shape[:4]
    return x.reshape(b, h, nc * l, *x.shape[4:])


def axial_rope_tables(row, col, rot_dim):
    quarter = rot_dim // 4
    inv_freq = ROPE_THETA ** (-jnp.arange(quarter, dtype=jnp.float32) / quarter)
    ang_r = row.astype(jnp.float32)[:, None] * inv_freq
    ang_c = col.astype(jnp.float32)[:, None] * inv_freq
    return (jnp.cos(ang_r)[:, None], jnp.sin(ang_r)[:, None], jnp.cos(ang_c)[:, None], jnp.sin(ang_c)[:, None])


def rope_rotate(x, cos, sin):
    x1, x2 = jnp.split(x, 2, axis=-1)
    return jnp.concatenate([x1 * cos - x2 * sin, x2 * cos + x1 * sin], axis=-1)


def rope_2d(x, tabs):
    cr, sr, cc, sc = tabs
    xr, xc = jnp.split(x, 2, axis=-1)
    return jnp.concatenate([rope_rotate(xr, cr, sr), rope_rotate(xc, cc, sc)], axis=-1).astype(x.dtype)


def block_attention(q, k, v, scale):
    b, nq, hk, g, dk = q.shape
    nb = nq // Q_BLOCK
    qb = q.reshape(b, nb, Q_BLOCK, hk, g, dk).transpose(1, 0, 2, 3, 4, 5)

    def one_block(qi):
        s = jnp.einsum('bqhgd,bmhd->bhgqm', qi, k, preferred_element_type=jnp.float32) * scale
        p = jax.nn.softmax(s, axis=-1)
        return jnp.einsum('bhgqm,bmhd->bqhgd', p.astype(v.dtype), v)

    out = lax.map(one_block, qb)
    return out.transpose(1, 0, 2, 3, 4, 5).reshape(b, nq, hk, g, v.shape[-1])


def short_conv(x, w):
    return lax.conv_general_dilated(x, w[:, None, :].astype(x.dtype), window_strides=(1,), padding='SAME',
                                    dimension_numbers=('NWC', 'WIO', 'NWC'), feature_group_count=x.shape[-1])


def hgrn2_gates(f_pre, lb):
    log_f = jnp.logaddexp(jnp.log(lb), jnp.log1p(-lb) + jax.nn.log_sigmoid(f_pre.astype(jnp.float32)))
    return to_heads(-jnp.expm1(log_f), A_HEADS), to_heads(log_f, A_HEADS)


def hgrn2_scan(q, k, v, log_f, s0):
    causal = jnp.tril(jnp.ones((CHUNK, CHUNK), dtype=bool))

    def step(s, inp):
        qc, kc, vc, lfc = inp
        bcum = jnp.cumsum(lfc, axis=2)
        rel = bcum[:, :, :, None, :] - bcum[:, :, None, :, :]
        decay = jnp.exp(jnp.where(causal[:, :, None], rel, NEG))
        a = jnp.einsum('bhtk,bhsk,bhtsk->bhts', qc, kc, decay)
        o = jnp.einsum('bhtk,bhkv->bhtv', qc * jnp.exp(bcum), s) + jnp.einsum('bhts,bhsv->bhtv', a, vc)
        blast = bcum[:, :, -1:, :]
        s_new = jnp.exp(blast[:, :, 0, :])[..., None] * s + jnp.einsum('bhsk,bhsv->bhkv', kc * jnp.exp(blast - bcum), vc)
        return s_new, o

    s_fin, o = lax.scan(step, s0, (to_chunks(q), to_chunks(k), to_chunks(v), to_chunks(log_f)))
    return from_chunks(o), s_fin


def hgrn2_final_state(k, v, log_f):
    bcum = jnp.cumsum(log_f, axis=2)
    return jnp.einsum('bhnk,bhnv->bhkv', k * jnp.exp(bcum[:, :, -1:, :] - bcum), v)


def hgrn2_mixer(lat, ctx, lb, norm_g, need_ctx_out):
    f32 = jnp.float32
    q_l = to_heads(jax.nn.silu(lat['a_q'].astype(f32)), A_HEADS)
    v_l = to_heads(lat['a_i'].astype(f32), A_HEADS)
    v_c = to_heads(ctx['a_i'].astype(f32), A_HEADS)
    q_c = to_heads(jax.nn.silu(ctx['a_q'].astype(f32)), A_HEADS) if need_ctx_out else None
    o_l, o_c = 0.0, 0.0
    for d, name in enumerate(('a_f_fwd', 'a_f_bwd')):
        rev = d == 1
        k_l, lf_l = hgrn2_gates(lat[name], lb[d])
        k_c, lf_c = hgrn2_gates(ctx[name], lb[d])
        if need_ctx_out:
            zero = jnp.zeros((v_c.shape[0], A_HEADS, A_DK, A_DV), f32)
            oc, s_ctx = hgrn2_scan(flip_time(q_c, rev), flip_time(k_c, rev), flip_time(v_c, rev),
                                   flip_time(lf_c, rev), zero)
            o_c = o_c + flip_time(oc, rev)
        else:
            s_ctx = hgrn2_final_state(flip_time(k_c, rev), flip_time(v_c, rev), flip_time(lf_c, rev))
        ol, _ = hgrn2_scan(flip_time(q_l, rev), flip_time(k_l, rev), flip_time(v_l, rev),
                           flip_time(lf_l, rev), s_ctx)
        o_l = o_l + flip_time(ol, rev)

    def readout(o, g):
        return (from_heads(rms_norm(o, norm_g)) * jax.nn.silu(g.astype(f32))).astype(g.dtype)

    return readout(o_l, lat['a_g']), (readout(o_c, ctx['a_g']) if need_ctx_out else None)


def mlstm_scan(q, k, v, ig, lf, state):
    causal = jnp.tril(jnp.ones((CHUNK, CHUNK), dtype=bool))

    def step(carry, inp):
        cmat, nvec, m = carry
        qc, kc, vc, igc, lfc = inp
        b = jnp.cumsum(lfc, axis=-1)
        dmat = jnp.where(causal, b[..., :, None] - b[..., None, :] + igc[..., None, :], NEG)
        inter = b + m[..., None]
        m_t = jnp.maximum(inter, jnp.max(dmat, axis=-1))
        w_inter = jnp.exp(inter - m_t)
        s = jnp.einsum('bhtk,bhsk->bhts', qc, kc) * jnp.exp(dmat - m_t[..., None])
        num = w_inter[..., None] * jnp.einsum('bhtk,bhkv->bhtv', qc, cmat) + jnp.einsum('bhts,bhsv->bhtv', s, vc)
        den = w_inter * jnp.einsum('bhtk,bhk->bht', qc, nvec) + jnp.sum(s, axis=-1)
        h = num / jnp.maximum(jnp.abs(den), jnp.exp(-m_t))[..., None]
        m_new = m_t[..., -1]
        dec = jnp.exp(b[..., -1] + m - m_new)
        wk = jnp.exp(b[..., -1:] - b + igc - m_new[..., None])
        c_new = dec[..., None, None] * cmat + jnp.einsum('bhs,bhsk,bhsv->bhkv', wk, kc, vc)
        n_new = dec[..., None] * nvec + jnp.einsum('bhs,bhsk->bhk', wk, kc)
        return (c_new, n_new, m_new), h

    fin, h = lax.scan(step, state, (to_chunks(q), to_chunks(k), to_chunks(v), to_chunks(ig), to_chunks(lf)))
    return from_chunks(h), fin


def mlstm_final_state(k, v, ig, lf):
    b = jnp.cumsum(lf, axis=-1)
    logw = b[..., -1:] - b + ig
    m = jnp.max(logw, axis=-1)
    w = jnp.exp(logw - m[..., None])
    return (jnp.einsum('bhn,bhnk,bhnv->bhkv', w, k, v), jnp.einsum('bhn,bhnk->bhk', w, k), m)


def mlstm_mixer(lat, ctx, gate_bias, conv_w, norm_g, need_ctx_out):
    f32 = jnp.float32

    def prep(side, with_q):
        b, n = side['b_v'].shape[:2]
        k = to_heads((jax.nn.silu(short_conv(side['b_k'], conv_w[1])) * B_DQK ** -0.5).astype(f32), B_HEADS)
        v = to_heads(side['b_v'].astype(f32), B_HEADS)
        q = to_heads(jax.nn.silu(short_conv(side['b_q'], conv_w[0])).astype(f32), B_HEADS) if with_q else None
        g = (side['b_gates'] + gate_bias).astype(f32).reshape(b, n, 4, B_HEADS).transpose(2, 0, 3, 1)
        return q, k, v, g

    q_l, k_l, v_l, g_l = prep(lat, True)
    q_c, k_c, v_c, g_c = prep(ctx, need_ctx_out)
    h_l, h_c = 0.0, 0.0
    for d in range(2):
        rev = d == 1
        ig_c, lf_c = g_c[2 * d], jax.nn.log_sigmoid(g_c[2 * d + 1])
        ig_l, lf_l = g_l[2 * d], jax.nn.log_sigmoid(g_l[2 * d + 1])
        if need_ctx_out:
            bsz = v_c.shape[0]
            zero = (jnp.zeros((bsz, B_HEADS, B_DQK, B_DV), f32), jnp.zeros((bsz, B_HEADS, B_DQK), f32),
                    jnp.full((bsz, B_HEADS), NEG, f32))
            hc, st = mlstm_scan(flip_time(q_c, rev), flip_time(k_c, rev), flip_time(v_c, rev),
                                flip_time(ig_c, rev), flip_time(lf_c, rev), zero)
            h_c = h_c + flip_time(hc, rev)
        else:
            st = mlstm_final_state(flip_time(k_c, rev), flip_time(v_c, rev), flip_time(ig_c, rev), flip_time(lf_c, rev))
        hl, _ = mlstm_scan(flip_time(q_l, rev), flip_time(k_l, rev), flip_time(v_l, rev),
                           flip_time(ig_l, rev), flip_time(lf_l, rev), st)
        h_l = h_l + flip_time(hl, rev)

    def readout(h, o):
        return (from_heads(rms_norm(h, norm_g)) * jax.nn.sigmoid(o.astype(f32))).astype(o.dtype)

    return readout(h_l, lat['b_o']), (readout(h_c, ctx['b_o']) if need_ctx_out else None)


def gqa_mixer(lat, ctx, q_g, k_g, tabs, need_ctx_out):
    grp = C_HEADS // C_KV_HEADS
    scale = C_DH ** -0.5

    def keys(side, rope):
        k = rms_norm(split_heads(side['c_k'], C_KV_HEADS), k_g)
        if rope is not None:
            k = rope_2d(k, rope)
        return k, split_heads(side['c_v'], C_KV_HEADS)

    def queries(side, rope):
        q = rms_norm(split_heads(side['c_q'], C_HEADS), q_g)
        if rope is not None:
            q = rope_2d(q, rope)
        return q.reshape(q.shape[0], q.shape[1], C_KV_HEADS, grp, C_DH)

    k_c, v_c = keys(ctx, None)
    k_l, v_l = keys(lat, tabs)
    y_l = block_attention(queries(lat, tabs), jnp.concatenate([k_c, k_l], axis=1),
                          jnp.concatenate([v_c, v_l], axis=1), scale)
    y_l = y_l.reshape(y_l.shape[0], y_l.shape[1], -1)
    if not need_ctx_out:
        return y_l, None
    y_c = block_attention(queries(ctx, None), k_c, v_c, scale)
    return y_l, y_c.reshape(y_c.shape[0], y_c.shape[1], -1)


def mla_mixer(lat, ctx, q_g, kv_g, w_uq, w_uk, w_uv, tabs, need_ctx_out):
    scale = (D_NOPE + D_ROPE) ** -0.5

    def keys(side, rope):
        ckv = rms_norm(side['d_ckv'], kv_g)
        k_nope = split_heads(ckv @ w_uk, D_HEADS)
        v = split_heads(ckv @ w_uv, D_HEADS)
        k_rope = side['d_krope'][:, :, None, :]
        if rope is not None:
            k_rope = rope_2d(k_rope, rope)
        k_rope = jnp.broadcast_to(k_rope, k_nope.shape[:3] + (D_ROPE,))
        return jnp.concatenate([k_nope, k_rope], axis=-1), v

    def queries(side, rope):
        q = split_heads(rms_norm(side['d_cq'], q_g) @ w_uq, D_HEADS)
        q_nope, q_rope = q[..., :D_NOPE], q[..., D_NOPE:]
        if rope is not None:
            q_rope = rope_2d(q_rope, rope)
        return jnp.concatenate([q_nope, q_rope], axis=-1)[:, :, :, None, :]

    k_c, v_c = keys(ctx, None)
    k_l, v_l = keys(lat, tabs)
    y_l = block_attention(queries(lat, tabs), jnp.concatenate([k_c, k_l], axis=1),
                          jnp.concatenate([v_c, v_l], axis=1), scale)
    y_l = y_l.reshape(y_l.shape[0], y_l.shape[1], -1)
    if not need_ctx_out:
        return y_l, None
    y_c = block_attention(queries(ctx, None), k_c, v_c, scale)
    return y_l, y_c.reshape(y_c.shape[0], y_c.shape[1], -1)


def merge_branches(ys, gate_pre, w_branch_l):
    g = gate_pre.reshape(*gate_pre.shape[:-1], N_BRANCH, D_MODEL)
    out = 0.0
    for r, y in enumerate(ys):
        out = out + jax.nn.sigmoid(g[..., r, :]) * (y @ w_branch_l[r])
    return out


def squared_relu_mlp(h, w1, w2):
    return jnp.square(jax.nn.relu(h @ w1)) @ w2


def setup_inputs(seed: int = 0) -> dict:
    key = jax.random.key(seed)
    ks = iter(jax.random.split(key, 32))

    def nrm(shape, std):
        return std * jax.random.normal(next(ks), shape, jnp.float32)

    def gain(shape):
        return 1.0 + nrm(shape, 0.02)

    x = nrm((BATCH, SEQ, D_MODEL), 1.0)
    c = nrm((BATCH, D_MODEL), 1.0)
    ctx = nrm((BATCH, CTX_LEN, D_MODEL), 1.0)
    c_ctx = nrm((D_MODEL,), 1.0)
    w_ada = nrm((DEPTH, D_MODEL, 6 * D_MODEL), 0.5 * D_MODEL ** -0.5)
    b_ada = nrm((DEPTH, 6 * D_MODEL), 0.02)
    g_norm1 = gain((DEPTH, D_MODEL))
    g_norm2 = gain((DEPTH, D_MODEL))
    w_in = nrm((DEPTH, D_MODEL, IN_WIDTH), D_MODEL ** -0.5)
    i_bias = nrm((DEPTH, 2, B_HEADS), 0.1)
    f_bias = jnp.linspace(3.0, 6.0, B_HEADS, dtype=jnp.float32) + nrm((DEPTH, 2, B_HEADS), 0.1)
    b_mlstm_gates = jnp.stack([i_bias, f_bias], axis=2).reshape(DEPTH, 4 * B_HEADS)
    hgrn_lb_logits = nrm((DEPTH, 2, A_HEADS * A_DK), 0.5)
    hgrn_norm_g = gain((DEPTH, A_DV))
    mlstm_conv_w = nrm((DEPTH, 2, B_CONV, B_HEADS * B_DQK), B_CONV ** -0.5)
    mlstm_norm_g = gain((DEPTH, B_DV))
    gqa_q_norm_g = gain((DEPTH, C_DH))
    gqa_k_norm_g = gain((DEPTH, C_DH))
    mla_q_norm_g = gain((DEPTH, D_Q_LORA))
    mla_kv_norm_g = gain((DEPTH, D_KV_LORA))
    w_mla_uq = nrm((DEPTH, D_Q_LORA, D_HEADS * (D_NOPE + D_ROPE)), D_Q_LORA ** -0.5)
    w_mla_uk = nrm((DEPTH, D_KV_LORA, D_HEADS * D_NOPE), D_KV_LORA ** -0.5)
    w_mla_uv = nrm((DEPTH, D_KV_LORA, D_HEADS * D_DV), D_KV_LORA ** -0.5)
    w_branch = nrm((DEPTH, N_BRANCH, BRANCH_W, D_MODEL), BRANCH_W ** -0.5)
    w_out = nrm((DEPTH, D_MODEL, D_MODEL), D_MODEL ** -0.5)
    w_ff1 = nrm((DEPTH, D_MODEL, D_FF), D_MODEL ** -0.5)
    w_ff2 = nrm((DEPTH, D_FF, D_MODEL), D_FF ** -0.5)
    g_final = gain((D_MODEL,))
    return {'x': x, 'c': c, 'ctx': ctx, 'c_ctx': c_ctx, 'w_ada': w_ada, 'b_ada': b_ada,
            'g_norm1': g_norm1, 'g_norm2': g_norm2, 'w_in': w_in, 'b_mlstm_gates': b_mlstm_gates,
            'hgrn_lb_logits': hgrn_lb_logits, 'hgrn_norm_g': hgrn_norm_g, 'mlstm_conv_w': mlstm_conv_w,
            'mlstm_norm_g': mlstm_norm_g, 'gqa_q_norm_g': gqa_q_norm_g, 'gqa_k_norm_g': gqa_k_norm_g,
            'mla_q_norm_g': mla_q_norm_g, 'mla_kv_norm_g': mla_kv_norm_g, 'w_mla_uq': w_mla_uq,
            'w_mla_uk': w_mla_uk, 'w_mla_uv': w_mla_uv, 'w_branch': w_branch, 'w_out': w_out,
            'w_ff1': w_ff1, 'w_ff2': w_ff2, 'g_final': g_final}


def reference(x, c, ctx, c_ctx, w_ada, b_ada, g_norm1, g_norm2, w_in, b_mlstm_gates, hgrn_lb_logits,
              hgrn_norm_g, mlstm_conv_w, mlstm_norm_g, gqa_q_norm_g, gqa_k_norm_g, mla_q_norm_g,
              mla_kv_norm_g, w_mla_uq, w_mla_uk, w_mla_uv, w_branch, w_out, w_ff1, w_ff2, g_final):
    n_lat = x.shape[1]
    rows = n_lat // GRID_W
    row = jnp.repeat(jnp.arange(rows, dtype=jnp.int32), GRID_W)
    col = jnp.broadcast_to(jnp.arange(GRID_W, dtype=jnp.int32), (rows, GRID_W)).reshape(-1)
    rope_c = axial_rope_tables(row, col, C_DH)
    rope_d = axial_rope_tables(row, col, D_ROPE)

    lb_all = jnp.cumsum(jax.nn.softmax(hgrn_lb_logits.astype(jnp.float32), axis=0), axis=0)
    lb_all = lb_all - lb_all[0]

    s_c = jax.nn.silu(c)
    s_cc = jax.nn.silu(c_ctx)
    x_l, x_c = x, ctx
    for l in range(DEPTH):
        need_ctx = l < DEPTH - 1
        mod = jnp.split((s_c @ w_ada[l] + b_ada[l])[:, None, :], 6, axis=-1)
        n_cm = 6 if need_ctx else 2
        mod_c = jnp.split(s_cc @ w_ada[l][:, :n_cm * D_MODEL] + b_ada[l][:n_cm * D_MODEL], n_cm)

        h_l = modulate(rms_norm(x_l, g_norm1[l]), mod[0], mod[1])
        h_c = modulate(rms_norm(x_c, g_norm1[l]), mod_c[0], mod_c[1])
        p_l = h_l @ w_in[l]
        lat = split_cols(p_l[..., :KEY_WIDTH], KEY_COLS)
        lat.update(split_cols(p_l[..., KEY_WIDTH:], QUERY_COLS))
        p_c = h_c @ (w_in[l] if need_ctx else w_in[l][:, :KEY_WIDTH])
        ctxd = split_cols(p_c[..., :KEY_WIDTH], KEY_COLS)
        if need_ctx:
            ctxd.update(split_cols(p_c[..., KEY_WIDTH:], QUERY_COLS))

        ya_l, ya_c = hgrn2_mixer(lat, ctxd, lb_all[l], hgrn_norm_g[l], need_ctx)
        yb_l, yb_c = mlstm_mixer(lat, ctxd, b_mlstm_gates[l], mlstm_conv_w[l], mlstm_norm_g[l], need_ctx)
        yc_l, yc_c = gqa_mixer(lat, ctxd, gqa_q_norm_g[l], gqa_k_norm_g[l], rope_c, need_ctx)
        yd_l, yd_c = mla_mixer(lat, ctxd, mla_q_norm_g[l], mla_kv_norm_g[l], w_mla_uq[l], w_mla_uk[l],
                               w_mla_uv[l], rope_d, need_ctx)

        x_l = x_l + mod[2] * (merge_branches((ya_l, yb_l, yc_l, yd_l), lat['gates'], w_branch[l]) @ w_out[l])
        x_l = x_l + mod[5] * squared_relu_mlp(modulate(rms_norm(x_l, g_norm2[l]), mod[3], mod[4]), w_ff1[l], w_ff2[l])

        if need_ctx:
            x_c = x_c + mod_c[2] * (merge_branches((ya_c, yb_c, yc_c, yd_c), ctxd['gates'], w_branch[l]) @ w_out[l])
            x_c = x_c + mod_c[5] * squared_relu_mlp(modulate(rms_norm(x_c, g_norm2[l]), mod_c[3], mod_c[4]),
                                                    w_ff1[l], w_ff2[l])
    return rms_norm(x_l, g_final)
```

```python
import numpy as np
from contextlib import ExitStack
import concourse.bass as bass
import concourse.mybir as mybir
from concourse.bass_utils import run_bass_kernel_spmd

F32 = mybir.dt.float32
BF16 = mybir.dt.bfloat16
AF = mybir.ActivationFunctionType
ALU = mybir.AluOpType

DM = 1024
L = 4
T = 2304
NCTX = 256
NLAT = 2048
NT = 18
EPS = 1e-6
BLKS = [(0, 256), (256, 512), (768, 512), (1280, 512), (1792, 512)]
NCH_IN = 76
CH_AI, CH_AFF, CH_AFB, CH_AQ, CH_AG = 0, 4, 8, 12, 16
CH_BK, CH_BQ, CH_BV, CH_BO, CH_BIG, CH_BLF = 20, 22, 24, 28, 32, 33
CH_CK, CH_CV, CH_CQ, CH_DCKV, CH_DKR, CH_DCQ, CH_GATES = 34, 35, 36, 40, 41, 42, 44
OFF = dict(a_i=0, a_f_fwd=512, a_f_bwd=1024, b_k=1536, b_v=1792, b_gates=2304, c_k=2320, c_v=2448, d_ckv=2576,
           d_krope=2704, a_q=2768, a_g=3280, b_q=3792, b_o=4048, c_q=4560, d_cq=5072, gates=5328)

ENGINES = ("tensor", "vector", "scalar", "gpsimd", "sync")


class Res:
    __slots__ = ("name", "w", "r")

    def __init__(self, name=""):
        self.name = name
        self.w = None
        self.r = {}


class Sched:
    def __init__(self, nc, stack, n_dma_sems=8):
        self.nc = nc
        self.ops = {e: [] for e in ENGINES}
        self.sems = {}
        self.cnt = {}
        self.known = {e: {} for e in ENGINES}
        for e in ENGINES:
            self.sems[e] = stack.enter_context(nc.semaphore("s_" + e))
            self.cnt[e] = 0
        self.dma_pool = {}
        self.dma_i = {}
        for q in ("sync", "gpsimd"):
            ks = []
            for i in range(n_dma_sems):
                k = "d_%s_%d" % (q, i)
                self.sems[k] = stack.enter_context(nc.semaphore(k))
                self.cnt[k] = 0
                ks.append(k)
            self.dma_pool[q] = ks
            self.dma_i[q] = 0
        self.nops = 0

    def _emit(self, eng, fn, reads, writes, semkey, inc, extra=()):
        waits = {}

        def need(dep, raw):
            if dep is None:
                return
            k, v = dep
            if k == eng and (eng == "tensor" or not raw):
                return
            if v > waits.get(k, 0):
                waits[k] = v

        for r in reads:
            need(r.w, True)
        for w in writes:
            need(w.w, False)
            for k, v in w.r.items():
                need((k, v), False)
        for d in extra:
            need(d, True)
        kn = self.known[eng]
        wl = []
        for k, v in waits.items():
            if kn.get(k, 0) >= v:
                continue
            kn[k] = v
            wl.append((k, v))
        self.cnt[semkey] += inc
        val = self.cnt[semkey]
        for r in reads:
            if r.r.get(semkey, 0) < val:
                r.r[semkey] = val
        for w in writes:
            w.w = (semkey, val)
            w.r = {}
        self.ops[eng].append((wl, fn, semkey, inc))
        self.nops += 1
        return (semkey, val)

    def op(self, eng, fn, reads=(), writes=()):
        return self._emit(eng, fn, reads, writes, eng, 1)

    def dma(self, q, out, in_, reads=(), writes=()):
        pool = self.dma_pool[q]
        k = pool[self.dma_i[q] % len(pool)]
        self.dma_i[q] += 1
        prev = ((k, self.cnt[k]),) if self.cnt[k] > 0 else ()
        return self._emit(q, lambda e: e.dma_start(out=out, in_=in_), reads, writes, k, 16, extra=prev)

    def barrier(self):
        allk = [(k, v) for k, v in self.cnt.items() if v > 0]
        for e in ENGINES:
            kn = self.known[e]
            wl = []
            for k, v in allk:
                if k == e or kn.get(k, 0) >= v:
                    continue
                kn[k] = v
                wl.append((k, v))
            if wl:
                self.ops[e].append((wl, None, None, 0))

    def replay(self, block):
        sems = self.sems

        def mk(engname):
            lst = self.ops[engname]

            def body(e):
                for wl, fn, semkey, inc in lst:
                    for k, v in wl:
                        e.wait_ge(sems[k], v)
                    if fn is not None:
                        fn(e).then_inc(sems[semkey], inc)
            return body

        block.tensor(mk("tensor"))
        block.vector(mk("vector"))
        block.scalar(mk("scalar"))
        block.gpsimd(mk("gpsimd"))
        block.sync(mk("sync"))


class Buf:
    __slots__ = ("ap", "res")

    def __init__(self, ap, name=""):
        self.ap = ap
        self.res = Res(name)


PV_SPEC = [("g1", L * 8), ("g2", L * 8), ("gfin", 8), ("bada", L * 48), ("lbl", L * 8), ("hng", L),
           ("convw", L * 12), ("mgb", L * 2), ("mng", L * 128), ("gqg", L), ("gkg", L), ("mqg", L * 2),
           ("mkvg", L), ("sel", 2), ("cT", 64)]


def pv_offsets():
    o = {}
    off = 0
    for n, w in PV_SPEC:
        o[n] = (off, w)
        off += w
    return o, off


CST_SPEC = [("ident", 128), ("ones", 128), ("blk64", 128), ("prot", 128), ("masks", 128), ("onehot", 1024)]


def cst_offsets():
    o = {}
    off = 0
    for n, w in CST_SPEC:
        o[n] = (off, w)
        off += w
    return o, off


def in_chunks():
    ch = []

    def grp(name, n):
        for i in range(n):
            ch.append([(OFF[name] + i * 128, 128, 0)])
    grp("a_i", 4); grp("a_f_fwd", 4); grp("a_f_bwd", 4); grp("a_q", 4); grp("a_g", 4)
    grp("b_k", 2); grp("b_q", 2); grp("b_v", 4); grp("b_o", 4)
    g = OFF["b_gates"]
    ch.append([(g + 0, 4, 0), (g + 8, 4, 4)])
    ch.append([(g + 4, 4, 0), (g + 12, 4, 4)])
    grp("c_k", 1); grp("c_v", 1)
    for c in range(4):
        ch.append([(OFF["c_q"] + c * 64, 64, 0), (OFF["c_q"] + (c + 4) * 64, 64, 64)])
    grp("d_ckv", 1)
    ch.append([(OFF["d_krope"], 64, 0)])
    grp("d_cq", 2)
    for i in range(8):
        for r in range(4):
            ch.append([(OFF["gates"] + r * 1024 + i * 128, 128, 0)])
    assert len(ch) == NCH_IN
    return ch


def layout_w(W, chunks, nkc):
    out = np.zeros((len(chunks), 128, nkc, 128), np.float32)
    Wr = W.reshape(nkc, 128, W.shape[1])
    for j, segs in enumerate(chunks):
        for (c0, w, pos) in segs:
            out[j, :, :, pos:pos + w] = Wr[:, :, c0:c0 + w].transpose(1, 0, 2)
    return out.reshape(len(chunks), 128, nkc * 128)


def simple_chunks(n):
    return [[(i * 128, 128, 0)] for i in range(n)]


def host_consts():
    co, nc_ = cst_offsets()
    c = np.zeros((128, nc_), np.float32)
    p = np.arange(128)
    c[:, co["ident"][0]:co["ident"][0] + 128] = np.eye(128, dtype=np.float32)
    c[:, co["ones"][0]:co["ones"][0] + 128] = 1.0
    blk = (p[:, None] // 64 == p[None, :] // 64).astype(np.float32)
    c[:, co["blk64"][0]:co["blk64"][0] + 128] = blk
    prot = np.zeros((128, 128), np.float32)
    for f in range(128):
        if f % 32 < 16:
            prot[f + 16, f] = -1.0
        else:
            prot[f - 16, f] = 1.0
    c[:, co["prot"][0]:co["prot"][0] + 128] = prot
    s = (p % 64)[:, None]
    t = np.arange(64)[None, :]
    m0 = co["masks"][0]
    c[:, m0:m0 + 64] = (s <= t).astype(np.float32)
    c[:, m0 + 64:m0 + 128] = (s >= t).astype(np.float32)
    o0 = co["onehot"][0]
    for r in range(8):
        c[r, o0 + r * 128:o0 + (r + 1) * 128] = 1.0
    inv = (10000.0 ** (-np.arange(16, dtype=np.float32) / 16)).astype(np.float32)
    tt = np.arange(NLAT)
    row = (tt // 64).astype(np.float32)
    col = (tt % 64).astype(np.float32)
    rope = np.zeros((128, 2 * NLAT), np.float32)
    for f in range(128):
        j = f % 64
        pos = row if j < 32 else col
        ang = (pos * inv[j % 16]).astype(np.float32)
        rope[f, :NLAT] = np.cos(ang)
        rope[f, NLAT:] = np.sin(ang)
    return c, rope


def host_pv(inp, core, nseq):
    po, npv = pv_offsets()
    pv = np.zeros((128, npv), np.float32)

    def put(name, arr):
        o, w = po[name]
        assert arr.shape == (128, w), (name, arr.shape, w)
        pv[:, o:o + w] = arr
    put("g1", inp["g_norm1"].reshape(L, 8, 128).transpose(2, 0, 1).reshape(128, L * 8))
    put("g2", inp["g_norm2"].reshape(L, 8, 128).transpose(2, 0, 1).reshape(128, L * 8))
    put("gfin", inp["g_final"].reshape(8, 128).T)
    put("bada", inp["b_ada"].reshape(L, 48, 128).transpose(2, 0, 1).reshape(128, L * 48))
    put("lbl", inp["hgrn_lb_logits"].reshape(L, 2, 4, 128).transpose(3, 0, 1, 2).reshape(128, L * 8))
    put("hng", inp["hgrn_norm_g"].T)
    put("convw", inp["mlstm_conv_w"].reshape(L, 2, 3, 2, 128).transpose(4, 0, 1, 3, 2).reshape(128, L * 12))
    mgb = np.zeros((128, L, 2), np.float32)
    b = inp["b_mlstm_gates"]
    mgb[0:4, :, 0] = b[:, 0:4].T
    mgb[4:8, :, 0] = b[:, 8:12].T
    mgb[0:4, :, 1] = b[:, 4:8].T
    mgb[4:8, :, 1] = b[:, 12:16].T
    put("mgb", mgb.reshape(128, L * 2))
    put("mng", np.broadcast_to(inp["mlstm_norm_g"].reshape(1, L * 128), (128, L * 128)))
    put("gqg", np.tile(inp["gqa_q_norm_g"].T, (2, 1)))
    put("gkg", np.tile(inp["gqa_k_norm_g"].T, (2, 1)))
    put("mqg", inp["mla_q_norm_g"].reshape(L, 2, 128).transpose(2, 0, 1).reshape(128, L * 2))
    put("mkvg", inp["mla_kv_norm_g"].T)
    sel = np.zeros((128, 2), np.float32)
    sel[0:4, 0] = 1.0
    sel[:, 1] = sel[:, 0] - 1.0
    put("sel", sel)
    cT = np.zeros((128, 8, 8), np.float32)
    cc = inp["c"][core * nseq:(core + 1) * nseq]
    cT[:, :, 0:nseq] = cc.reshape(nseq, 8, 128).transpose(2, 1, 0)
    cT[:, :, 4] = inp["c_ctx"].reshape(8, 128).T
    put("cT", cT.reshape(128, 64))
    return pv


def build(NSEQ=4, NL=4, dbg=False, stages="ABCD"):
    nc = bass.Bass("TRN2", target_bir_lowering=False)
    po, NPV = pv_offsets()
    co, NCST = cst_offsets()

    def din(name, shape, dt=F32):
        return nc.dram_tensor(name, list(shape), dt, kind="ExternalInput").ap()

    xT_in = din("xT", [NSEQ, 128, 8 * T])
    pv_in = din("pv", [128, NPV])
    cst_in = din("cst", [128, NCST])
    rope_in = din("rope", [128, 2 * NLAT])
    wada_in = din("w_ada", [L, DM, 6 * DM])
    win_in = din("win", [L * NCH_IN, 128, 1024])
    wff1_in = din("wff1", [L * 32, 128, 1024])
    wout_in = din("wout", [L * 8, 128, 1024])
    wbr_in = din("wbr", [L * 8, 128, 2048])
    wff2_in = din("wff2", [L * 8, 128, 4096])
    wuq_in = din("wuq", [L, 256, 768])
    wuk_in = din("wuk", [L, 128, 512])
    wuv_in = din("wuv", [L, 128, 512])
    outT = nc.dram_tensor("outT", [NSEQ, 128, 8 * NLAT], F32, kind="ExternalOutput").ap()

    def dscr(name, shape, dt):
        if dbg:
            return nc.dram_tensor(name, list(shape), dt, kind="ExternalOutput").ap()
        return nc.dram_tensor(name, list(shape), dt).ap()

    win_b = nc.dram_tensor("win_b", [L * NCH_IN, 128, 1024], BF16).ap()
    wff1_b = nc.dram_tensor("wff1_b", [L * 32, 128, 1024], BF16).ap()
    wout_b = nc.dram_tensor("wout_b", [L * 8, 128, 1024], BF16).ap()
    wbr_b = nc.dram_tensor("wbr_b", [L * 8, 128, 2048], BF16).ap()
    wff2_b = nc.dram_tensor("wff2_b", [L * 8, 128, 4096], BF16).ap()
    xT_d = nc.dram_tensor("xT_d", [128, 8 * T], F32).ap()
    yT_d = dscr("yT_d", [16, 128, T], BF16)
    hT_dbg = dscr("hT_dbg", [128, 8 * T], BF16) if dbg else None

    with ExitStack() as st:
        S = Sched(nc, st)

        def sb(name, shape, dt=F32):
            return Buf(st.enter_context(nc.sbuf_tensor(name, list(shape), dt)), name)

        hT = sb("hT", [128, 8 * T], BF16)
        hT3 = hT.ap[:].rearrange("p (k t) -> p k t", k=8)
        hres = [Res("h%d" % i) for i in range(len(BLKS))]
        cst = sb("cst_sb", [128, NCST])
        cbf = sb("cbf", [128, 512], BF16)
        ropeb = sb("ropeb", [128, 2 * NLAT], BF16)
        pv = sb("pv_sb", [128, NPV])
        modT = sb("modT", [128, L * 48 * 8])
        mod4 = modT.ap[:].rearrange("p (l j s) -> p l j s", l=L, j=48)
        lbt = sb("lbt", [128, 3 * L * 8])
        smallc = sb("smallc", [128, 8])
        wslots = [sb("wslot%d" % i, [128, 1024], BF16) for i in range(8)]
        tmpF = [sb("tmpF%d" % i, [128, 512]) for i in range(6)]
        tmpB = [sb("tmpB%d" % i, [128, 512], BF16) for i in range(6)]
        tmpR = [sb("tmpR%d" % i, [128, 512]) for i in range(3)]
        tmpE = [sb("tmpE%d" % i, [128, 512], BF16) for i in range(3)]
        small = [sb("small%d" % i, [128, 64]) for i in range(8)]
        ARENA_B = int(nc.sbuf_bytes_remaining) - 2048
        ARENA_B -= ARENA_B % 64
        arena = sb("arena", [128, ARENA_B // 2], BF16)
        banks = [Buf(st.enter_context(nc.psum_tensor("bank%d" % i, [128, 512], F32)), "bank%d" % i) for i in range(8)]

        rr = {"w": 0, "F": 0, "B": 0, "E": 0, "s": 0, "A": 0, "R": 0}

        def nxt(lst, key):
            b = lst[rr[key] % len(lst)]
            rr[key] += 1
            return b

        def bankA():
            return nxt(banks[0:4], "A")
        bankO = banks[4:8]

        class Arena:
            def __init__(self):
                self.off = 0

            def mark(self):
                return self.off

            def release(self, m):
                S.barrier()
                self.off = m

            def alloc(self, nelem, dt, name="a"):
                nb = nelem * (4 if dt == F32 else 2)
                nb = (nb + 63) // 64 * 64
                assert self.off + nb <= ARENA_B, ("arena overflow", name, self.off, nb, ARENA_B)
                v = arena.ap[:, self.off // 2:(self.off + nb) // 2]
                self.off += nb
                if dt == F32:
                    v = v.bitcast(F32)
                return Buf(v[:, 0:nelem], name)
        AR = Arena()

        def pvc(name, i=0, n=1):
            o, w = po[name]
            return pv.ap[:, o + i:o + i + n]

        def cstv(name):
            o, w = co[name]
            return cst.ap[:, o:o + w]

        def R_(x):
            return [b.res if isinstance(b, Buf) else b for b in x]

        def mm(out, lhsT, rhs, start, stop, R, W):
            S.op("tensor", lambda e: e.matmul(out, lhsT=lhsT, rhs=rhs, start=start, stop=stop), R_(R), R_(W))

        def tp(out, in_, ident, R, W):
            S.op("tensor", lambda e: e.transpose(out=out, in_=in_, identity=ident), R_(R), R_(W))

        def act(out, in_, func, R, W, scale=1.0, bias=None):
            if bias is None:
                S.op("scalar", lambda e: e.activation(out=out, in_=in_, func=func, scale=scale), R_(R), R_(W))
            else:
                S.op("scalar", lambda e: e.activation(out=out, in_=in_, func=func, scale=scale, bias=bias), R_(R), R_(W))

        def tt(eng, out, in0, in1, op, R, W):
            S.op(eng, lambda e: e.tensor_tensor(out=out, in0=in0, in1=in1, op=op), R_(R), R_(W))

        def ts(eng, out, in0, s1, s2, op0, op1, R, W):
            if s2 is None:
                S.op(eng, lambda e: e.tensor_scalar(out=out, in0=in0, scalar1=s1, scalar2=None, op0=op0), R_(R), R_(W))
            else:
                S.op(eng, lambda e: e.tensor_scalar(out=out, in0=in0, scalar1=s1, scalar2=s2, op0=op0, op1=op1), R_(R), R_(W))

        def stt(out, in0, scalar, in1, op0, op1, R, W):
            S.op("vector", lambda e: e.scalar_tensor_tensor(out=out, in0=in0, scalar=scalar, in1=in1, op0=op0, op1=op1), R_(R), R_(W))

        def cp(eng, out, in_, R, W):
            if eng == "scalar":
                S.op("scalar", lambda e: e.copy(out=out, in_=in_), R_(R), R_(W))
            else:
                S.op(eng, lambda e: e.tensor_copy(out=out, in_=in_), R_(R), R_(W))

        def scan(out, d0, d1, init, op0, op1, R, W):
            S.op("vector", lambda e: e.tensor_tensor_scan(out=out, data0=d0, data1=d1, initial=init, op0=op0, op1=op1), R_(R), R_(W))

        def recip(out, in_, R, W):
            S.op("vector", lambda e: e.reciprocal(out=out, in_=in_), R_(R), R_(W))

        def memset(eng, ap, val, W):
            S.op(eng, lambda e: e.memset(ap, val), [], R_(W))

        def dma(q, out, in_, R, W):
            return S.dma(q, out, in_, R_(R), R_(W))

        MUL, ADD, SUB, MAX = ALU.mult, ALU.add, ALU.subtract, ALU.max

        dma("sync", cst.ap[:], cst_in[:, :], [], [cst])
        dma("sync", pv.ap[:], pv_in[:, :], [], [pv])
        memset("vector", smallc.ap[:, 0:1], 0.0, [smallc])
        memset("vector", smallc.ap[:, 1:2], float(np.log(0.125)), [smallc])
        memset("vector", smallc.ap[:, 2:3], 1.0, [smallc])
        memset("vector", smallc.ap[:, 3:4], EPS, [smallc])
        cp("vector", cbf.ap[:, 0:512], cst.ap[:, 0:512], [cst], [cbf])
        ident_f = cstv("ident")
        ident_b = cbf.ap[:, 0:128]
        ones_b = cbf.ap[:, 128:256]
        blk64_b = cbf.ap[:, 256:384]
        prot_b = cbf.ap[:, 384:512]
        masks = cstv("masks")
        onehot = cstv("onehot")
        zcol = smallc.ap[:, 0:1]
        m0 = AR.mark()
        rstage = AR.alloc(2 * NLAT, F32, "rstage")
        dma("sync", rstage.ap[:], rope_in[:, :], [], [rstage])
        cp("vector", ropeb.ap[:, 0:NLAT], rstage.ap[:, 0:NLAT], [rstage], [ropeb])
        cp("gpsimd", ropeb.ap[:, NLAT:], rstage.ap[:, NLAT:], [rstage], [ropeb])
        AR.release(m0)
        cosb = ropeb.ap[:, 0:NLAT]
        sinb = ropeb.ap[:, NLAT:]

        wres = Res("wconv")
        m0 = AR.mark()
        stf = [AR.alloc(4096, F32, "stf%d" % i) for i in range(3)]
        stb = [AR.alloc(4096, BF16, "stb%d" % i) for i in range(3)]
        ci = 0
        for (src, dst, per_l, W) in ((win_in, win_b, NCH_IN, 1024), (wff1_in, wff1_b, 32, 1024), (wout_in, wout_b, 8, 1024),
                                     (wbr_in, wbr_b, 8, 2048), (wff2_in, wff2_b, 8, 4096)):
            g = 4096 // W
            for j0 in range(0, per_l * NL, g):
                f = stf[ci % 3]
                b = stb[ci % 3]
                dma("sync", f.ap[:].rearrange("p (g w) -> p g w", g=g), src[j0:j0 + g].rearrange("g p w -> p g w"), [], [f])
                eng = ("vector", "gpsimd", "scalar")[ci % 3]
                cp(eng, b.ap[:], f.ap[:], [f], [b])
                dma("gpsimd", dst[j0:j0 + g].rearrange("g p w -> p g w"), b.ap[:].rearrange("p (g w) -> p g w", g=g), [b], [wres])
                ci += 1
        AR.release(m0)

        m0 = AR.mark()
        sT = AR.alloc(64, F32, "sT")
        act(sT.ap[:], pvc("cT", 0, 64), AF.Silu, [pv], [sT])
        sT3 = sT.ap[:].rearrange("p (k s) -> p k s", k=8)
        wst = [AR.alloc(8 * 512, F32, "wada%d" % i) for i in range(2)]
        for l in range(NL):
            for ct in range(12):
                w = wst[(l * 12 + ct) % 2]
                dma("sync", w.ap[:].rearrange("p (k n) -> p k n", k=8),
                    wada_in[l].rearrange("(k p) n -> p k n", p=128)[:, :, ct * 512:(ct + 1) * 512], [], [w])
                w3 = w.ap[:].rearrange("p (k n) -> p k n", k=8)
                ps = bankA()
                for fc in range(4):
                    for kc in range(8):
                        mm(ps.ap[:, fc * 8:(fc + 1) * 8], w3[:, kc, fc * 128:(fc + 1) * 128], sT3[:, kc, :], kc == 0, kc == 7, [w, sT], [ps])
                o, _ = po["bada"]
                bb = pv.ap[:, o + l * 48 + ct * 4:o + l * 48 + ct * 4 + 4].unsqueeze(2).to_broadcast([128, 4, 8])
                tt("vector", mod4[:, l, ct * 4:(ct + 1) * 4, :], ps.ap[:, 0:32].rearrange("p (a s) -> p a s", a=4), bb, ADD, [ps, pv], [modT])
            for (jm, gname) in ((8, "g1"), (32, "g2")):
                o, _ = po[gname]
                gb = pv.ap[:, o + l * 8:o + l * 8 + 8].unsqueeze(2).to_broadcast([128, 8, 8])
                stt(mod4[:, l, jm:jm + 8, :], mod4[:, l, jm:jm + 8, :], 1.0, gb, ADD, MUL, [modT, pv], [modT])
        AR.release(m0)

        def modc(l, m, kc, s):
            return mod4[:, l, m * 8 + kc, s:s + 1]

        lb3 = lbt.ap[:].rearrange("p (a l j) -> p a l j", a=3, l=L)
        ex = small[0]
        act(ex.ap[:, 0:L * 8], pvc("lbl", 0, L * 8), AF.Exp, [pv], [ex])
        ex3 = ex.ap[:, 0:L * 8].rearrange("p (l j) -> p l j", l=L)
        ssum = small[1]
        tt("vector", ssum.ap[:, 0:8], ex3[:, 0, :], ex3[:, 1, :], ADD, [ex], [ssum])
        tt("vector", ssum.ap[:, 0:8], ssum.ap[:, 0:8], ex3[:, 2, :], ADD, [ex, ssum], [ssum])
        tt("vector", ssum.ap[:, 0:8], ssum.ap[:, 0:8], ex3[:, 3, :], ADD, [ex, ssum], [ssum])
        recip(ssum.ap[:, 0:8], ssum.ap[:, 0:8], [ssum], [ssum])
        memset("vector", lb3[:, 0, 0, :], 0.0, [lbt])
        for l in range(1, L):
            tt("vector", lb3[:, 0, l, :], lb3[:, 0, l - 1, :], ex3[:, l, :], ADD, [ex, lbt], [lbt])
        for l in range(1, L):
            tt("vector", lb3[:, 0, l, :], lb3[:, 0, l, :], ssum.ap[:, 0:8], MUL, [ssum, lbt], [lbt])
        for l in range(L):
            ts("vector", lb3[:, 2, l, :], lb3[:, 0, l, :], -1.0, None, ADD, None, [lbt], [lbt])
            ts("vector", lb3[:, 1, l, :], lb3[:, 2, l, :], -1.0, None, MUL, None, [lbt], [lbt])
        S.barrier()

        def load_w(wb, idx, nelem=1024, slot=None):
            s_ = slot if slot is not None else nxt(wslots, "w")
            dma("sync", s_.ap[:, 0:nelem], wb[idx], [wres], [s_])
            return s_

        def proj_fm(wslot, t0, n, bi, width=128):
            ps = bankA()
            w3 = wslot.ap[:].rearrange("p (k c) -> p k c", k=8)
            for kc in range(8):
                mm(ps.ap[0:width, 0:n], w3[:, kc, 0:width], hT3[:, kc, t0:t0 + n], kc == 0, kc == 7, [wslot, hres[bi]], [ps])
            return ps

        def proj_tm(wslot, tile, ps, c0, width=128):
            w3 = wslot.ap[:].rearrange("p (k c) -> p k c", k=8)
            bi = 0 if tile < 2 else 1 + (tile - 2) // 4
            for kc in range(8):
                mm(ps.ap[:, c0:c0 + width], hT3[:, kc, tile * 128:(tile + 1) * 128], w3[:, kc, 0:width], kc == 0, kc == 7, [wslot, hres[bi]], [ps])

        def norm_mod(xb3, n, l, mA, mB, s, out3, t0, R, W):
            ps = bankA()
            for kc in range(8):
                sq = nxt(tmpB, "B")
                act(sq.ap[:, 0:n], xb3[:, kc, 0:n], AF.Square, R, [sq])
                mm(ps.ap[:, 0:n], ones_b, sq.ap[:, 0:n], kc == 0, kc == 7, [sq, cbf], [ps])
            rs = nxt(tmpR, "R")
            act(rs.ap[:, 0:n], ps.ap[:, 0:n], AF.Sqrt, [ps, smallc], [rs], scale=1.0 / DM, bias=smallc.ap[:, 3:4])
            recip(rs.ap[:, 0:n], rs.ap[:, 0:n], [rs], [rs])
            for kc in range(8):
                t_ = nxt(tmpF, "F")
                tt("vector", t_.ap[:, 0:n], xb3[:, kc, 0:n], rs.ap[:, 0:n], MUL, R + [rs], [t_])
                act(out3[:, kc, t0:t0 + n], t_.ap[:, 0:n], AF.Identity, [t_, modT], W, scale=modc(l, mA, kc, s), bias=modc(l, mB, kc, s))

        def rms_fm(srcs, rows, n, blkmat, denom, gaps, outs, R, W, keep=None):
            kfs = []
            ps2 = bankA()
            for i, (src, sres) in enumerate(srcs):
                kf = nxt(tmpF, "F")
                cp("scalar", kf.ap[0:rows, 0:n], src, [sres], [kf])
                sq = nxt(tmpB, "B")
                tt("vector", sq.ap[0:rows, 0:n], kf.ap[0:rows, 0:n], kf.ap[0:rows, 0:n], MUL, [kf], [sq])
                mm(ps2.ap[0:rows, 0:n], blkmat[0:rows, 0:rows], sq.ap[0:rows, 0:n], i == 0, i == len(srcs) - 1, [sq, cbf], [ps2])
                kfs.append(kf)
            rs = nxt(tmpR, "R")
            act(rs.ap[0:rows, 0:n], ps2.ap[0:rows, 0:n], AF.Sqrt, [ps2, smallc], [rs], scale=1.0 / denom, bias=smallc.ap[0:rows, 3:4])
            recip(rs.ap[0:rows, 0:n], rs.ap[0:rows, 0:n], [rs], [rs])
            for kf, g, o in zip(kfs, gaps, outs):
                stt(o, kf.ap[0:rows, 0:n], g, rs.ap[0:rows, 0:n], MUL, MUL, [kf, rs, pv] + R, W)

        def rope(src_b, sres, rows, n, t0, out, W):
            lt = t0 - NCTX
            ps3 = bankA()
            mm(ps3.ap[0:rows, 0:n], prot_b[0:rows, 0:rows], src_b, True, True, [sres, cbf], [ps3])
            t1 = nxt(tmpF, "F")
            tt("gpsimd", t1.ap[0:rows, 0:n], src_b, cosb[0:rows, lt:lt + n], MUL, [sres, ropeb], [t1])
            t2 = nxt(tmpF, "F")
            tt("vector", t2.ap[0:rows, 0:n], ps3.ap[0:rows, 0:n], sinb[0:rows, lt:lt + n], MUL, [ps3, ropeb], [t2])
            tt("vector", out, t1.ap[0:rows, 0:n], t2.ap[0:rows, 0:n], ADD, [t1, t2], W)

        def attention(pairs_fn, v_fn, nkt, n, dv, scale, y_fn, R, yres):
            nqs = n // 128
            for kt in range(nkt):
                psS = bankA()
                prs = pairs_fn(kt)
                for i, (k_ap, q_ap) in enumerate(prs):
                    mm(psS.ap[:, 0:n], k_ap, q_ap, i == 0, i == len(prs) - 1, R, [psS])
                E = nxt(tmpE, "E")
                act(E.ap[:, 0:n], psS.ap[:, 0:n], AF.Exp, [psS], [E], scale=scale)
                for qs in range(nqs):
                    mm(bankO[qs].ap[:, 0:dv + 1], E.ap[:, qs * 128:(qs + 1) * 128], v_fn(kt), kt == 0, kt == nkt - 1, [E] + R, [bankO[qs]])
            for qs in range(nqs):
                rd = nxt(small, "s")
                recip(rd.ap[:, 0:1], bankO[qs].ap[:, dv:dv + 1], [bankO[qs]], [rd])
                ts("vector", y_fn(qs), bankO[qs].ap[:, 0:dv], rd.ap[:, 0:1], None, MUL, None, [bankO[qs], rd], [yres])

        def store_yT(ytok, yres, n, t0, branch):
            nqs = n // 128
            y3 = ytok.ap[:].rearrange("p (q f) -> p q f", f=512)
            for c in range(4):
                ps = bankA()
                pb = ps.ap[:].bitcast(BF16)
                for qs in range(nqs):
                    tp(pb[:, qs * 128:(qs + 1) * 128], y3[:, qs, c * 128:(c + 1) * 128], ident_b, [yres, cbf], [ps])
                yt = nxt(tmpB, "B")
                cp("scalar", yt.ap[:, 0:n], pb[:, 0:n], [ps], [yt])
                dma("gpsimd", yT_d[branch * 4 + c][:, t0:t0 + n], yt.ap[:, 0:n], [yt], [yres_d])

        yres_d = Res("yT_d")

        def mixer_C(l, s):
            m_ = AR.mark()
            kT = AR.alloc(T, BF16, "kT_c")
            vC = AR.alloc(NT * 2 * 65, BF16, "vC")
            vC4 = vC.ap[:].rearrange("p (t h d) -> p t h d", t=NT, h=2)
            ytok = AR.alloc(4 * 512, BF16, "ytokC")
            y3 = ytok.ap[:].rearrange("p (q f) -> p q f", f=512)
            qTs = [AR.alloc(4 * 512, BF16, "qT%d" % i) for i in range(2)]
            memset("gpsimd", vC.ap[:], 1.0, [vC])
            wk = load_w(win_b, l * NCH_IN + CH_CK)
            for bi, (t0, n) in enumerate(BLKS):
                ps = proj_fm(wk, t0, n, bi)
                if t0 < NCTX:
                    rms_fm([(ps.ap[:, 0:n], ps)], 128, n, blk64_b, 64.0, [pvc("gkg", l)], [kT.ap[:, t0:t0 + n]], [], [kT])
                else:
                    kn = nxt(tmpB, "B")
                    rms_fm([(ps.ap[:, 0:n], ps)], 128, n, blk64_b, 64.0, [pvc("gkg", l)], [kn.ap[:, 0:n]], [], [kn])
                    rope(kn.ap[:, 0:n], kn, 128, n, t0, kT.ap[:, t0:t0 + n], [kT])
            wv = load_w(win_b, l * NCH_IN + CH_CV)
            for tl in range(NT):
                ps = bankA()
                proj_tm(wv, tl, ps, 0)
                cp("vector", vC4[:, tl, :, 0:64], ps.ap[:, 0:128].rearrange("p (h d) -> p h d", h=2), [ps], [vC])
            wq = [load_w(win_b, l * NCH_IN + CH_CQ + c) for c in range(4)]
            for bi, (t0, n) in enumerate(BLKS):
                qT = qTs[bi % 2]
                q3 = qT.ap[:].rearrange("p (c t) -> p c t", c=4)
                for c in range(4):
                    ps = proj_fm(wq[c], t0, n, bi)
                    if t0 < NCTX:
                        rms_fm([(ps.ap[:, 0:n], ps)], 128, n, blk64_b, 64.0, [pvc("gqg", l)], [q3[:, c, 0:n]], [], [qT])
                    else:
                        qn = nxt(tmpB, "B")
                        rms_fm([(ps.ap[:, 0:n], ps)], 128, n, blk64_b, 64.0, [pvc("gqg", l)], [qn.ap[:, 0:n]], [], [qn])
                        rope(qn.ap[:, 0:n], qn, 128, n, t0, q3[:, c, 0:n], [qT])
                nkt = 2 if bi == 0 else NT
                for hq in range(8):
                    kvh = hq // 4
                    c = hq % 4
                    b0 = kvh * 64

                    def pairs(kt, b0=b0, c=c, n=n, q3=q3):
                        return [(kT.ap[b0:b0 + 64, kt * 128:(kt + 1) * 128], q3[b0:b0 + 64, c, 0:n])]

                    def vfn(kt, kvh=kvh):
                        return vC4[:, kt, kvh, :]

                    def yfn(qs, hq=hq):
                        return y3[:, qs, hq * 64:(hq + 1) * 64]
                    attention(pairs, vfn, nkt, n, 64, 0.125, yfn, [kT, qT, vC], ytok)
                store_yT(ytok, ytok, n, t0, 2)
            AR.release(m_)

        def mixer_D(l, s):
            m_ = AR.mark()
            wuq_f = AR.alloc(2 * 768, F32, "wuq_f")
            wuq = AR.alloc(2 * 768, BF16, "wuq")
            wuk_f = AR.alloc(512, F32, "wuk_f")
            wuv_f = AR.alloc(512, F32, "wuv_f")
            wukv = AR.alloc(1024, BF16, "wukv")
            dma("sync", wuq_f.ap[:].rearrange("p (k n) -> p k n", k=2), wuq_in[l].rearrange("(k p) n -> p k n", p=128), [], [wuq_f])
            dma("sync", wuk_f.ap[:], wuk_in[l], [], [wuk_f])
            dma("sync", wuv_f.ap[:], wuv_in[l], [], [wuv_f])
            cp("vector", wuq.ap[:], wuq_f.ap[:], [wuq_f], [wuq])
            cp("vector", wukv.ap[:, 0:512], wuk_f.ap[:], [wuk_f], [wukv])
            cp("vector", wukv.ap[:, 512:1024], wuv_f.ap[:], [wuv_f], [wukv])
            wuq3 = wuq.ap[:].rearrange("p (k n) -> p k n", k=2)
            ckvn = AR.alloc(T, BF16, "ckvn")
            knT = AR.alloc(4 * T, BF16, "knT")
            kn3 = knT.ap[:].rearrange("p (h t) -> p h t", h=4)
            krT = AR.alloc(T, BF16, "krT")
            vD = AR.alloc(NT * 4 * 129, BF16, "vD")
            vD4 = vD.ap[:].rearrange("p (t h d) -> p t h d", t=NT, h=4)
            ytok = AR.alloc(4 * 512, BF16, "ytokD")
            y3 = ytok.ap[:].rearrange("p (q f) -> p q f", f=512)
            qn = [AR.alloc(4 * 512, BF16, "qnD%d" % i) for i in range(2)]
            qr = [AR.alloc(4 * 512, BF16, "qrD%d" % i) for i in range(2)]
            cqn = AR.alloc(2 * 512, BF16, "cqn")
            cq3 = cqn.ap[:].rearrange("p (k t) -> p k t", k=2)
            memset("gpsimd", vD.ap[:], 1.0, [vD])
            wc = load_w(win_b, l * NCH_IN + CH_DCKV)
            wr = load_w(win_b, l * NCH_IN + CH_DKR)
            for bi, (t0, n) in enumerate(BLKS):
                ps = proj_fm(wc, t0, n, bi)
                rms_fm([(ps.ap[:, 0:n], ps)], 128, n, ones_b, 128.0, [pvc("mkvg", l)], [ckvn.ap[:, t0:t0 + n]], [], [ckvn])
                for h in range(4):
                    ps = bankA()
                    mm(ps.ap[:, 0:n], wukv.ap[:, h * 128:(h + 1) * 128], ckvn.ap[:, t0:t0 + n], True, True, [wukv, ckvn], [ps])
                    cp("scalar" if h % 2 else "vector", kn3[:, h, t0:t0 + n], ps.ap[:, 0:n], [ps], [knT])
                ps = proj_fm(wr, t0, n, bi, width=64)
                if t0 < NCTX:
                    cp("scalar", krT.ap[0:64, t0:t0 + n], ps.ap[0:64, 0:n], [ps], [krT])
                else:
                    kb = nxt(tmpB, "B")
                    cp("scalar", kb.ap[0:64, 0:n], ps.ap[0:64, 0:n], [ps], [kb])
                    rope(kb.ap[0:64, 0:n], kb, 64, n, t0, krT.ap[0:64, t0:t0 + n], [krT])
            for tl in range(NT):
                ps = bankA()
                mm(ps.ap[:, 0:512], ckvn.ap[:, tl * 128:(tl + 1) * 128], wukv.ap[:, 512:1024], True, True, [wukv, ckvn], [ps])
                cp("vector" if tl % 2 else "scalar", vD4[:, tl, :, 0:128], ps.ap[:, 0:512].rearrange("p (h d) -> p h d", h=4), [ps], [vD])
            wq = [load_w(win_b, l * NCH_IN + CH_DCQ + c) for c in range(2)]
            o_mq, _ = po["mqg"]
            for bi, (t0, n) in enumerate(BLKS):
                pss = [proj_fm(wq[c], t0, n, bi) for c in range(2)]
                rms_fm([(p_.ap[:, 0:n], p_) for p_ in pss], 128, n, ones_b, 256.0,
                       [pv.ap[:, o_mq + l * 2 + c:o_mq + l * 2 + c + 1] for c in range(2)], [cq3[:, c, 0:n] for c in range(2)], [], [cqn])
                qn3 = qn[bi % 2].ap[:].rearrange("p (h t) -> p h t", h=4)
                qr3 = qr[bi % 2].ap[:].rearrange("p (h t) -> p h t", h=4)
                for h in range(4):
                    ps = bankA()
                    for kc in range(2):
                        mm(ps.ap[:, 0:n], wuq3[:, kc, h * 192:h * 192 + 128], cq3[:, kc, 0:n], kc == 0, kc == 1, [wuq, cqn], [ps])
                    cp("scalar", qn3[:, h, 0:n], ps.ap[:, 0:n], [ps], [qn[bi % 2]])
                    ps = bankA()
                    for kc in range(2):
                        mm(ps.ap[0:64, 0:n], wuq3[:, kc, h * 192 + 128:h * 192 + 192], cq3[:, kc, 0:n], kc == 0, kc == 1, [wuq, cqn], [ps])
                    if t0 < NCTX:
                        cp("vector", qr3[0:64, h, 0:n], ps.ap[0:64, 0:n], [ps], [qr[bi % 2]])
                    else:
                        qb = nxt(tmpB, "B")
                        cp("scalar", qb.ap[0:64, 0:n], ps.ap[0:64, 0:n], [ps], [qb])
                        rope(qb.ap[0:64, 0:n], qb, 64, n, t0, qr3[0:64, h, 0:n], [qr[bi % 2]])
                nkt = 2 if bi == 0 else NT
                for h in range(4):
                    def pairs(kt, h=h, n=n, qn3=qn3, qr3=qr3):
                        return [(kn3[:, h, kt * 128:(kt + 1) * 128], qn3[:, h, 0:n]),
                                (krT.ap[0:64, kt * 128:(kt + 1) * 128], qr3[0:64, h, 0:n])]

                    def vfn(kt, h=h):
                        return vD4[:, kt, h, :]

                    def yfn(qs, h=h):
                        return y3[:, qs, h * 128:(h + 1) * 128]
                    attention(pairs, vfn, nkt, n, 128, float(192 ** -0.5), yfn, [knT, krT, qn[bi % 2], qr[bi % 2], vD], ytok)
                store_yT(ytok, ytok, n, t0, 3)
            AR.release(m_)

        def vorder(d):
            return list(range(36)) if d == 0 else [3, 2, 1, 0] + list(range(35, 3, -1))

        def mixer_A(l, s):
            for hd in range(4):
                m_ = AR.mark()
                qs_ = AR.alloc(T, F32, "qs")
                sg = AR.alloc(T, BF16, "sg")
                lf = AR.alloc(T, F32, "lf")
                kk = AR.alloc(T, F32, "kk")
                vtok = AR.alloc(NT * 128, BF16, "vtok")
                v3 = vtok.ap[:].rearrange("p (t v) -> p t v", t=NT)
                oacc = AR.alloc(T, F32, "oacc")
                Fc = AR.alloc(T, F32, "Fc")
                tmp = AR.alloc(T, F32, "tmpA")
                eq = AR.alloc(T, F32, "eq")
                qt = AR.alloc(T, BF16, "qt")
                ktT = AR.alloc(T, BF16, "ktT")
                ktok = AR.alloc(NT * 128, BF16, "ktok")
                k3 = ktok.ap[:].rearrange("p (t v) -> p t v", t=NT)
                Sst = AR.alloc(128, F32, "Sst")
                Sb = [AR.alloc(128, BF16, "Sb%d" % i) for i in range(2)]
                tS = AR.alloc(128, F32, "tS")
                ATs = [[AR.alloc(64, BF16, "AT%d_%d" % (dd, i)) for i in range(3)] for dd in range(2)]
                for dd in range(2):
                    for i in range(3):
                        memset("vector", ATs[dd][i].ap[:], 0.0, [ATs[dd][i]])
                qtX = AR.alloc(T, BF16, "qtX")
                ktX = AR.alloc(T, BF16, "ktX")
                sm = AR.alloc(6 * 36, F32, "smA")
                sm3 = sm.ap[:].rearrange("p (a c) -> p a c", a=6)
                wv = load_w(win_b, l * NCH_IN + CH_AI + hd)
                wq = load_w(win_b, l * NCH_IN + CH_AQ + hd)
                wg = load_w(win_b, l * NCH_IN + CH_AG + hd)
                for bi, (t0, n) in enumerate(BLKS):
                    ps = proj_fm(wq, t0, n, bi)
                    act(qs_.ap[:, t0:t0 + n], ps.ap[:, 0:n], AF.Silu, [ps], [qs_])
                    ps = proj_fm(wg, t0, n, bi)
                    act(sg.ap[:, t0:t0 + n], ps.ap[:, 0:n], AF.Silu, [ps], [sg])
                for tl in range(NT):
                    ps = bankA()
                    proj_tm(wv, tl, ps, 0)
                    cp("vector", v3[:, tl, :], ps.ap[:, 0:128], [ps], [vtok])
                for d in range(2):
                    wf = load_w(win_b, l * NCH_IN + (CH_AFF if d == 0 else CH_AFB) + hd)
                    j = d * 4 + hd
                    lbv = lb3[:, 0, l, j:j + 1]
                    omlb = lb3[:, 1, l, j:j + 1]
                    nomlb = lb3[:, 2, l, j:j + 1]
                    for bi, (t0, n) in enumerate(BLKS):
                        ps = proj_fm(wf, t0, n, bi)
                        sig = nxt(tmpF, "F")
                        act(sig.ap[:, 0:n], ps.ap[:, 0:n], AF.Sigmoid, [ps], [sig])
                        act(lf.ap[:, t0:t0 + n], sig.ap[:, 0:n], AF.Ln, [sig, lbt], [lf], scale=omlb, bias=lbv)
                        act(kk.ap[:, t0:t0 + n], sig.ap[:, 0:n], AF.Identity, [sig, lbt], [kk], scale=nomlb, bias=omlb)
                    scan(Fc.ap[:], lf.ap[:], zcol.to_broadcast([128, T]), 0.0, ADD, ADD, [lf, smallc], [Fc])
                    if d == 0:
                        vc = Fc
                    else:
                        tt("vector", Fc.ap[:], lf.ap[:], Fc.ap[:], SUB, [lf, Fc], [Fc])
                        vc = Fc
                    vc3 = vc.ap[:].rearrange("p (c t) -> p c t", t=64)
                    lf3 = lf.ap[:].rearrange("p (c t) -> p c t", t=64)
                    fv = 0 if d == 0 else 63
                    lv = 63 if d == 0 else 0
                    tt("vector", sm3[:, 0, :], vc3[:, :, fv], lf3[:, :, fv], SUB, [vc, lf], [sm])
                    tt("vector", sm3[:, 1, :], vc3[:, :, 32], sm3[:, 0, :], SUB, [vc, sm], [sm])
                    tt("vector", sm3[:, 2, :], vc3[:, :, lv], vc3[:, :, 32], SUB, [vc], [sm])
                    tt("vector", sm3[:, 3, :], vc3[:, :, lv], sm3[:, 0, :], SUB, [vc, sm], [sm])
                    act(sm3[:, 1:4, :], sm3[:, 1:4, :], AF.Exp, [sm], [sm])
                    tmp3 = tmp.ap[:].rearrange("p (c t) -> p c t", t=64)
                    tt("vector", tmp3, vc3, vc3[:, :, 32:33].to_broadcast([128, 36, 64]), SUB, [vc], [tmp])
                    act(eq.ap[:], tmp.ap[:], AF.Exp, [tmp], [eq])
                    act(tmp.ap[:], tmp.ap[:], AF.Exp, [tmp], [tmp], scale=-1.0)
                    tt("vector", qt.ap[:], qs_.ap[:], eq.ap[:], MUL, [qs_, eq], [qt])
                    tt("gpsimd", ktT.ap[:], kk.ap[:], tmp.ap[:], MUL, [kk, tmp], [ktT])
                    tmp32 = tmp.ap[:].rearrange("p (c t) -> p c t", t=32)
                    vc32 = vc.ap[:].rearrange("p (c t) -> p c t", t=32)
                    tt("vector", tmp32, vc32, vc32[:, :, 16:17].to_broadcast([128, 72, 32]), SUB, [vc], [tmp])
                    act(eq.ap[:], tmp.ap[:], AF.Exp, [tmp], [eq])
                    act(tmp.ap[:], tmp.ap[:], AF.Exp, [tmp], [tmp], scale=-1.0)
                    tt("vector", qtX.ap[:], qs_.ap[:], eq.ap[:], MUL, [qs_, eq], [qtX])
                    tt("gpsimd", ktX.ap[:], kk.ap[:], tmp.ap[:], MUL, [kk, tmp], [ktX])
                    for tl in range(NT):
                        ps = bankA()
                        pb = ps.ap[:].bitcast(BF16)
                        tp(pb[:, 0:128], ktT.ap[:, tl * 128:(tl + 1) * 128], ident_b, [ktT, cbf], [ps])
                        cp("scalar" if tl % 2 else "vector", k3[:, tl, :], pb[:, 0:128], [ps], [ktok])
                    memset("vector", Sst.ap[:], 0.0, [Sst])
                    memset("vector", Sb[0].ap[:], 0.0, [Sb[0]])
                    order = vorder(d)
                    for ci_, c in enumerate(order):
                        tl = c // 2
                        b = (c % 2) * 64
                        tk0 = c * 64
                        sbc = Sb[ci_ % 2]
                        psA = bankA()
                        AT = ATs[d][ci_ % 3]
                        mk_ = masks[:, d * 64:(d + 1) * 64]
                        if d == 0:
                            mm(psA.ap[b:b + 64, 0:32], ktX.ap[:, tk0:tk0 + 64], qtX.ap[:, tk0:tk0 + 32], True, True, [ktX, qtX], [psA])
                            mm(psA.ap[b:b + 64, 32:64], ktX.ap[:, tk0:tk0 + 64], qtX.ap[:, tk0 + 32:tk0 + 64], True, True, [ktX, qtX], [psA])
                            mm(psA.ap[b:b + 32, 32:64], ktT.ap[:, tk0:tk0 + 32], qt.ap[:, tk0 + 32:tk0 + 64], True, True, [ktT, qt], [psA])
                        else:
                            mm(psA.ap[b:b + 64, 0:32], ktT.ap[:, tk0:tk0 + 64], qt.ap[:, tk0:tk0 + 32], True, True, [ktT, qt], [psA])
                            mm(psA.ap[b:b + 32, 0:32], ktX.ap[:, tk0:tk0 + 32], qtX.ap[:, tk0:tk0 + 32], True, True, [ktX, qtX], [psA])
                            mm(psA.ap[b:b + 64, 32:64], ktX.ap[:, tk0:tk0 + 64], qtX.ap[:, tk0 + 32:tk0 + 64], True, True, [ktX, qtX], [psA])
                        tt("vector", AT.ap[b:b + 64, :], psA.ap[b:b + 64, 0:64], mk_[b:b + 64, :], MUL, [psA, cst], [AT])
                        psO = bankA()
                        mm(psO.ap[:, 0:64], sbc.ap[:], qt.ap[:, tk0:tk0 + 64], True, False, [sbc, qt], [psO])
                        mm(psO.ap[:, 0:64], v3[b:b + 64, tl, :], AT.ap[b:b + 64, :], False, True, [vtok, AT], [psO])
                        if d == 0:
                            cp("scalar", oacc.ap[:, tk0:tk0 + 64], psO.ap[:, 0:64], [psO], [oacc])
                        else:
                            tt("vector", oacc.ap[:, tk0:tk0 + 64], oacc.ap[:, tk0:tk0 + 64], psO.ap[:, 0:64], ADD, [psO, oacc], [oacc])
                        if ci_ < len(order) - 1:
                            psD = bankA()
                            mm(psD.ap[:, 0:128], k3[b:b + 64, tl, :], v3[b:b + 64, tl, :], True, True, [ktok, vtok], [psD])
                            ts("vector", tS.ap[:], psD.ap[:, 0:128], sm3[:, 2, c:c + 1], None, MUL, None, [psD, sm], [tS])
                            stt(Sst.ap[:], Sst.ap[:], sm3[:, 3, c:c + 1], tS.ap[:], MUL, ADD, [Sst, tS, sm], [Sst])
                            cn = order[ci_ + 1]
                            ts("vector", Sb[(ci_ + 1) % 2].ap[:], Sst.ap[:], sm3[:, 1, cn:cn + 1], None, MUL, None, [Sst, sm], [Sb[(ci_ + 1) % 2]])
                for bi, (t0, n) in enumerate(BLKS):
                    sq = nxt(tmpB, "B")
                    act(sq.ap[:, 0:n], oacc.ap[:, t0:t0 + n], AF.Square, [oacc], [sq])
                    ps = bankA()
                    mm(ps.ap[:, 0:n], ones_b, sq.ap[:, 0:n], True, True, [sq, cbf], [ps])
                    rs = nxt(tmpR, "R")
                    act(rs.ap[:, 0:n], ps.ap[:, 0:n], AF.Sqrt, [ps, smallc], [rs], scale=1.0 / 128, bias=smallc.ap[:, 3:4])
                    recip(rs.ap[:, 0:n], rs.ap[:, 0:n], [rs], [rs])
                    t1 = nxt(tmpF, "F")
                    stt(t1.ap[:, 0:n], oacc.ap[:, t0:t0 + n], pvc("hng", l), rs.ap[:, 0:n], MUL, MUL, [oacc, rs, pv], [t1])
                    yb = nxt(tmpB, "B")
                    tt("vector", yb.ap[:, 0:n], t1.ap[:, 0:n], sg.ap[:, t0:t0 + n], MUL, [t1, sg], [yb])
                    dma("gpsimd", yT_d[0 * 4 + hd][:, t0:t0 + n], yb.ap[:, 0:n], [yb], [yres_d])
                AR.release(m_)

        def mixer_B(l, s):
            m_ = AR.mark()
            tokscal = AR.alloc(NT * 24, F32, "tokscal")
            tsc3 = tokscal.ap[:].rearrange("p (t r) -> p t r", t=NT)
            dB = AR.alloc(8 * 36, F32, "dB")
            dB3 = dB.ap[:].rearrange("p (r c) -> p r c", r=8)
            m1 = AR.mark()
            A0 = AR.alloc(T, F32, "A0")
            A1 = AR.alloc(T, F32, "A1")
            A2 = AR.alloc(T, F32, "A2")
            A3 = AR.alloc(T, F32, "A3")
            Rr = AR.alloc(36, F32, "Rr")
            Rb = AR.alloc(8 * 36, F32, "Rb")
            Rb3 = Rb.ap[:].rearrange("p (r c) -> p r c", r=8)
            sel = pvc("sel", 0)
            selm1 = pvc("sel", 1)
            o_mgb, _ = po["mgb"]
            wig = load_w(win_b, l * NCH_IN + CH_BIG)
            wlf = load_w(win_b, l * NCH_IN + CH_BLF)
            P8 = slice(0, 8)
            for bi, (t0, n) in enumerate(BLKS):
                ps = proj_fm(wig, t0, n, bi, width=8)
                act(A0.ap[P8, t0:t0 + n], ps.ap[P8, 0:n], AF.Identity, [ps, pv], [A0], bias=pv.ap[P8, o_mgb + l * 2:o_mgb + l * 2 + 1])
                ps = proj_fm(wlf, t0, n, bi, width=8)
                t_ = nxt(tmpF, "F")
                ts("vector", t_.ap[P8, 0:n], ps.ap[P8, 0:n], pv.ap[P8, o_mgb + l * 2 + 1:o_mgb + l * 2 + 2], -1.0, ADD, MUL, [ps, pv], [t_])
                act(t_.ap[P8, 0:n], t_.ap[P8, 0:n], AF.Exp, [t_], [t_])
                act(A1.ap[P8, t0:t0 + n], t_.ap[P8, 0:n], AF.Ln, [t_, smallc], [A1], bias=smallc.ap[P8, 2:3])
            ts("vector", A1.ap[P8, :], A1.ap[P8, :], -1.0, None, MUL, None, [A1], [A1])
            scan(A2.ap[P8, :], A1.ap[P8, :], zcol[P8, :].to_broadcast([8, T]), 0.0, ADD, ADD, [A1, smallc], [A2])
            Kt = nxt(small, "s")
            ts("vector", Kt.ap[P8, 0:1], A2.ap[P8, T - 1:T], selm1[P8, :], -1.0, MUL, MUL, [A2, pv], [Kt])
            tt("vector", A1.ap[P8, :], A2.ap[P8, :], A1.ap[P8, :], SUB, [A1, A2], [A1])
            ts("vector", A3.ap[P8, :], A1.ap[P8, :], selm1[P8, :], None, MUL, None, [A1, pv], [A3])
            stt(A2.ap[P8, :], A2.ap[P8, :], sel[P8, :], A3.ap[P8, :], MUL, ADD, [A2, A3, pv], [A2])
            ts("vector", A2.ap[P8, NCTX:T], A2.ap[P8, NCTX:T], Kt.ap[P8, 0:1], None, ADD, None, [A2, Kt], [A2])
            tt("vector", A0.ap[P8, :], A0.ap[P8, :], A2.ap[P8, :], SUB, [A0, A2], [A0])
            scan(A1.ap[P8, :], A0.ap[P8, :], A0.ap[P8, :], -1e30, MAX, MAX, [A0], [A1])
            scan(A3.ap[P8, 0:NCTX][:, ::-1], A0.ap[P8, 0:NCTX][:, ::-1], A0.ap[P8, 0:NCTX][:, ::-1], -1e30, MAX, MAX, [A0], [A3])
            scan(A3.ap[P8, NCTX:T][:, ::-1], A0.ap[P8, NCTX:T][:, ::-1], A0.ap[P8, NCTX:T][:, ::-1], A3.ap[P8, 0:1], MAX, MAX, [A0, A3], [A3])
            ts("vector", A3.ap[P8, :], A3.ap[P8, :], selm1[P8, :], -1.0, MUL, MUL, [A3, pv], [A3])
            stt(A1.ap[P8, :], A1.ap[P8, :], sel[P8, :], A3.ap[P8, :], MUL, ADD, [A1, A3, pv], [A1])
            G3 = A1.ap[P8, :].rearrange("p (c t) -> p c t", t=64)
            ts("vector", Rr.ap[P8, :], G3[:, :, 63], selm1[P8, :], -1.0, MUL, MUL, [A1, pv], [Rr])
            stt(Rr.ap[P8, :], G3[:, :, 0], sel[P8, :], Rr.ap[P8, :], MUL, ADD, [A1, Rr, pv], [Rr])
            Rbc = Rr.ap[P8, :].unsqueeze(2).to_broadcast([8, 36, 64])
            a3 = A0.ap[P8, :].rearrange("p (c t) -> p c t", t=64)
            A33 = A3.ap[P8, :].rearrange("p (c t) -> p c t", t=64)
            tt("vector", A33, a3, Rbc, SUB, [A0, Rr], [A3])
            act(A3.ap[P8, :], A3.ap[P8, :], AF.Exp, [A3, smallc], [A3], bias=smallc.ap[P8, 1:2])
            tt("vector", a3, G3, Rbc, SUB, [A1, Rr], [A0])
            act(A0.ap[P8, :], A0.ap[P8, :], AF.Exp, [A0], [A0], scale=-1.0)
            tt("vector", A2.ap[P8, :], A2.ap[P8, :], A1.ap[P8, :], ADD, [A1, A2], [A2])
            act(A2.ap[P8, :], A2.ap[P8, :], AF.Exp, [A2], [A2], scale=-1.0)
            for tl in range(NT):
                ps = bankA()
                for qi, Aq in enumerate((A3, A0, A2)):
                    tp(ps.ap[:, qi * 8:(qi + 1) * 8], Aq.ap[P8, tl * 128:(tl + 1) * 128], ident_f[0:8, 0:8], [Aq, cst], [ps])
                cp("vector", tsc3[:, tl, :], ps.ap[:, 0:24], [ps], [tokscal])
            Df = AR.alloc(36, F32, "Df")
            Db = AR.alloc(36, F32, "Db")
            memset("vector", Df.ap[P8, :], 0.0, [Df])
            tt("vector", Df.ap[P8, 0:35], Rr.ap[P8, 0:35], Rr.ap[P8, 1:36], SUB, [Rr], [Df])
            tt("vector", Db.ap[P8, 1:36], Rr.ap[P8, 1:36], Rr.ap[P8, 0:35], SUB, [Rr], [Db])
            tt("vector", Db.ap[P8, 0:1], Rr.ap[P8, 0:1], Rr.ap[P8, 35:36], SUB, [Rr], [Db])
            memset("vector", Db.ap[P8, 4:5], 0.0, [Db])
            ts("vector", Db.ap[P8, :], Db.ap[P8, :], selm1[P8, :], -1.0, MUL, MUL, [Db, pv], [Db])
            stt(Df.ap[P8, :], Df.ap[P8, :], sel[P8, :], Db.ap[P8, :], MUL, ADD, [Df, Db, pv], [Df])
            act(Df.ap[P8, :], Df.ap[P8, :], AF.Exp, [Df], [Df])
            ps = bankA()
            for r in range(8):
                mm(ps.ap[:, r * 36:(r + 1) * 36], onehot[0:8, r * 128:(r + 1) * 128], Df.ap[P8, :], True, True, [Df, cst], [ps])
            cp("vector", dB.ap[:, :], ps.ap[:, 0:288], [ps], [dB])
            if dbg:
                dma("gpsimd", tsc_dbg[:, :], tokscal.ap[:], [tokscal], [Res()])
                dma("gpsimd", dB_dbg[:, :], dB.ap[:], [dB], [Res()])
            AR.release(m1)
            qTb = AR.alloc(2 * T, BF16, "qTb")
            kTb = AR.alloc(2 * T, BF16, "kTb")
            q3 = qTb.ap[:].rearrange("p (c t) -> p c t", c=2)
            k3 = kTb.ap[:].rearrange("p (c t) -> p c t", c=2)
            vtok = AR.alloc(NT * 4 * 129, BF16, "vtokB")
            v4 = vtok.ap[:].rearrange("p (t h d) -> p t h d", t=NT, h=4)
            kttok = AR.alloc(NT * 2 * 256, BF16, "kttok")
            kt5 = kttok.ap[:].rearrange("p (t d h k) -> p t d h k", t=NT, d=2, h=4)
            memset("gpsimd", vtok.ap[:], 1.0, [vtok])
            m2 = AR.mark()
            pre = AR.alloc(2 * T, F32, "pre")
            pre3 = pre.ap[:].rearrange("p (c t) -> p c t", c=2)
            cv = AR.alloc(2 * T, F32, "cv")
            cv3 = cv.ap[:].rearrange("p (c t) -> p c t", c=2)
            o_cw, _ = po["convw"]
            for qk, (chb, dst, dst3) in enumerate(((CH_BQ, qTb, q3), (CH_BK, kTb, k3))):
                ws = [load_w(win_b, l * NCH_IN + chb + c) for c in range(2)]
                for c in range(2):
                    for bi, (t0, n) in enumerate(BLKS):
                        ps = proj_fm(ws[c], t0, n, bi)
                        cp("scalar" if bi % 2 else "vector", pre3[:, c, t0:t0 + n], ps.ap[:, 0:n], [ps], [pre])
                    def cw(j, c=c, qk=qk):
                        i_ = o_cw + ((l * 2 + qk) * 2 + c) * 3 + j
                        return pv.ap[:, i_:i_ + 1]
                    ts("vector", cv3[:, c, :], pre3[:, c, :], cw(1), None, MUL, None, [pre, pv], [cv])
                    for (a, b_) in ((0, NCTX), (NCTX, T)):
                        stt(cv3[:, c, a + 1:b_], pre3[:, c, a:b_ - 1], cw(0), cv3[:, c, a + 1:b_], MUL, ADD, [pre, cv, pv], [cv])
                        stt(cv3[:, c, a:b_ - 1], pre3[:, c, a + 1:b_], cw(2), cv3[:, c, a:b_ - 1], MUL, ADD, [pre, cv, pv], [cv])
                    act(dst3[:, c, :], cv3[:, c, :], AF.Silu, [cv], [dst])
            AR.release(m2)
            wvs = [load_w(win_b, l * NCH_IN + CH_BV + h) for h in range(4)]
            for tl in range(NT):
                ps = bankA()
                for h in range(4):
                    proj_tm(wvs[h], tl, ps, h * 128)
                cp("vector" if tl % 2 else "scalar", v4[:, tl, :, 0:128], ps.ap[:, 0:512].rearrange("p (h d) -> p h d", h=4), [ps], [vtok])
                psk = bankA()
                pkb = psk.ap[:].bitcast(BF16)
                for c in range(2):
                    tp(pkb[:, c * 128:(c + 1) * 128], k3[:, c, tl * 128:(tl + 1) * 128], ident_b, [kTb, cbf], [psk])
                for d in range(2):
                    tt("vector", kt5[:, tl, d, :, :], pkb[:, 0:256].rearrange("p (h k) -> p h k", h=4),
                       tsc3[:, tl, d * 4:(d + 1) * 4].unsqueeze(2).to_broadcast([128, 4, 64]), MUL, [psk, tokscal], [kttok])
            nd = AR.alloc(NT * 129, F32, "nd")
            nd3 = nd.ap[:].rearrange("p (t d) -> p t d", t=NT)
            hh = AR.alloc(NT * 128, F32, "hh")
            hh3 = hh.ap[:].rearrange("p (t d) -> p t d", t=NT)
            sgo = AR.alloc(NT * 128, BF16, "sgo")
            sgo3 = sgo.ap[:].rearrange("p (t d) -> p t d", t=NT)
            ybt = AR.alloc(NT * 128, BF16, "ybt")
            ybt3 = ybt.ap[:].rearrange("p (t d) -> p t d", t=NT)
            Cst = AR.alloc(129, F32, "Cst")
            Cb = [AR.alloc(129, BF16, "Cb%d" % i) for i in range(2)]
            STs = [AR.alloc(64, BF16, "ST%d" % i) for i in range(3)]
            sn = AR.alloc(4 * NT, F32, "sn")
            sn3 = sn.ap[:].rearrange("p (a t) -> p a t", a=4)
            o_mng, _ = po["mng"]
            for h in range(4):
                hb = (h % 2) * 64
                ch = h // 2
                wo = load_w(win_b, l * NCH_IN + CH_BO + h)
                for tl in range(NT):
                    ps = bankA()
                    proj_tm(wo, tl, ps, 0)
                    act(sgo3[:, tl, :], ps.ap[:, 0:128], AF.Sigmoid, [ps], [sgo])
                for d in range(2):
                    r = d * 4 + h
                    memset("vector", Cst.ap[hb:hb + 64, :], 0.0, [Cst])
                    memset("vector", Cb[0].ap[hb:hb + 64, :], 0.0, [Cb[0]])
                    order = vorder(d)
                    for ci_, c in enumerate(order):
                        tl = c // 2
                        b = (c % 2) * 64
                        tk0 = c * 64
                        cbc = Cb[ci_ % 2]
                        psS = bankA()
                        mm(psS.ap[b:b + 64, 0:64], k3[hb:hb + 64, ch, tk0:tk0 + 64], q3[hb:hb + 64, ch, tk0:tk0 + 64], True, True, [kTb, qTb], [psS])
                        ST = STs[ci_ % 3]
                        stt(ST.ap[b:b + 64, :], psS.ap[b:b + 64, 0:64], tsc3[b:b + 64, tl, r:r + 1], masks[b:b + 64, d * 64:(d + 1) * 64], MUL, MUL,
                            [psS, tokscal, cst], [ST])
                        psN = bankA()
                        mm(psN.ap[b:b + 64, 0:129], q3[hb:hb + 64, ch, tk0:tk0 + 64], cbc.ap[hb:hb + 64, :], True, False, [qTb, cbc], [psN])
                        mm(psN.ap[b:b + 64, 0:129], ST.ap[b:b + 64, :], v4[b:b + 64, tl, h, :], False, True, [ST, vtok], [psN])
                        cp("scalar", nd3[b:b + 64, tl, :], psN.ap[b:b + 64, 0:129], [psN], [nd])
                        if ci_ < len(order) - 1:
                            psD = bankA()
                            mm(psD.ap[hb:hb + 64, 0:129], kt5[b:b + 64, tl, d, h, :], v4[b:b + 64, tl, h, :], True, True, [kttok, vtok], [psD])
                            dcol = dB3[hb:hb + 64, r, c:c + 1]
                            ts("vector", Cst.ap[hb:hb + 64, :], Cst.ap[hb:hb + 64, :], dcol, None, MUL, None, [Cst, dB], [Cst])
                            stt(Cst.ap[hb:hb + 64, :], psD.ap[hb:hb + 64, 0:129], dcol, Cst.ap[hb:hb + 64, :], MUL, ADD, [psD, Cst, dB], [Cst])
                            cp("vector", Cb[(ci_ + 1) % 2].ap[hb:hb + 64, :], Cst.ap[hb:hb + 64, :], [Cst], [Cb[(ci_ + 1) % 2]])
                    wv_ = tsc3[:, :, 8 + r]
                    ev_ = tsc3[:, :, 16 + r]
                    tt("vector", sn3[:, 0, :], nd3[:, :, 128], wv_, MUL, [nd, tokscal], [sn])
                    act(sn3[:, 0, :], sn3[:, 0, :], AF.Abs, [sn], [sn])
                    tt("vector", sn3[:, 0, :], sn3[:, 0, :], ev_, MAX, [sn, tokscal], [sn])
                    recip(sn3[:, 1, :], sn3[:, 0, :], [sn], [sn])
                    tt("vector", sn3[:, 1, :], sn3[:, 1, :], wv_, MUL, [sn, tokscal], [sn])
                    rwb = sn3[:, 1, :].unsqueeze(2).to_broadcast([128, NT, 128])
                    if d == 0:
                        tt("vector", hh3, nd3[:, :, 0:128], rwb, MUL, [nd, sn], [hh])
                    else:
                        tmpn = nd3[:, :, 0:128]
                        tt("vector", tmpn, tmpn, rwb, MUL, [nd, sn], [nd])
                        tt("gpsimd", hh3, hh3, tmpn, ADD, [nd, hh], [hh])
                sqn = nd3[:, :, 0:128]
                tt("vector", sqn, hh3, hh3, MUL, [hh], [nd])
                S.op("vector", (lambda o_, i_: (lambda e: e.tensor_reduce(out=o_, in_=i_, axis=mybir.AxisListType.X, op=ADD)))(sn3[:, 2, :], sqn), R_([nd]), R_([sn]))
                act(sn3[:, 2, :], sn3[:, 2, :], AF.Sqrt, [sn, smallc], [sn], scale=1.0 / 128, bias=smallc.ap[:, 3:4])
                recip(sn3[:, 2, :], sn3[:, 2, :], [sn], [sn])
                tt("vector", hh3, hh3, sn3[:, 2, :].unsqueeze(2).to_broadcast([128, NT, 128]), MUL, [hh, sn], [hh])
                tt("vector", hh3, hh3, pv.ap[:, o_mng + l * 128:o_mng + (l + 1) * 128].unsqueeze(1).to_broadcast([128, NT, 128]), MUL, [hh, pv], [hh])
                tt("vector", ybt3, hh3, sgo3, MUL, [hh, sgo], [ybt])
                for t0 in range(0, NT, 4):
                    nt_ = min(4, NT - t0)
                    ps = bankA()
                    pb = ps.ap[:].bitcast(BF16)
                    for i_ in range(nt_):
                        tp(pb[:, i_ * 128:(i_ + 1) * 128], ybt3[:, t0 + i_, :], ident_b, [ybt, cbf], [ps])
                    yt = nxt(tmpB, "B")
                    cp("scalar", yt.ap[:, 0:nt_ * 128], pb[:, 0:nt_ * 128], [ps], [yt])
                    dma("gpsimd", yT_d[1 * 4 + h][:, t0 * 128:(t0 + nt_) * 128], yt.ap[:, 0:nt_ * 128], [yt], [yres_d])
            AR.release(m_)

        def p34(l, s, sq, last):
            m_ = AR.mark()
            xb = AR.alloc(8 * 512, F32, "xb")
            xb3 = xb.ap[:].rearrange("p (k t) -> p k t", k=8)

            mT = AR.alloc(8 * 512, BF16, "mT")
            mT3 = mT.ap[:].rearrange("p (k t) -> p k t", k=8)
            acc = AR.alloc(512, F32, "acc")
            gsig = AR.alloc(512, F32, "gsig")
            h2 = Buf(mT.ap, "h2")
            h2.res = mT.res
            h23 = h2.ap[:].rearrange("p (k t) -> p k t", k=8)
            uT = AR.alloc(32 * 512, BF16, "uT")
            uT3 = uT.ap[:].rearrange("p (j t) -> p j t", j=32)
            yb = Buf(uT.ap[:, 0:16 * 512], "ybl")
            yb.res = uT.res
            yb3 = yb.ap[:].rearrange("p (j t) -> p j t", j=16)
            ot = Buf(uT.ap[:, 0:16 * 512].bitcast(F32), "ot")
            ot.res = uT.res
            ot3 = ot.ap[:].rearrange("p (k t) -> p k t", k=8)
            wg4 = [AR.alloc(4 * 1024, BF16, "wg4_%d" % i) for i in range(1)]
            wb4 = [AR.alloc(2048, BF16, "wb4_%d" % i) for i in range(2)]
            w2s = [AR.alloc(4096, BF16, "w2s_%d" % i) for i in range(2)]
            xsrc = xT_in[sq] if l == 0 else xT_d
            xsrc3 = xsrc.rearrange("p (k t) -> p k t", k=8)
            xdst3 = xT_d.rearrange("p (k t) -> p k t", k=8)
            cnt = 0
            for bi, (t0, n) in enumerate(BLKS):
                sc = 4 if t0 < NCTX else s
                dma("sync", xb3[:, :, 0:n], xsrc3[:, :, t0:t0 + n], [xres[bi]], [xb])
                dma("sync", yb3[:, :, 0:n], yT_d[:, :, t0:t0 + n].rearrange("j p t -> p j t"), [yres_d], [yb])
                for i in range(8):
                    wg = wg4[0]
                    wb = wb4[cnt % 2]
                    cnt += 1
                    j0 = l * NCH_IN + CH_GATES + i * 4
                    dma("sync", wg.ap[:].rearrange("p (g w) -> p g w", g=4), win_b[j0:j0 + 4].rearrange("g p w -> p g w"), [wres], [wg])
                    dma("sync", wb.ap[:], wbr_b[l * 8 + i], [wres], [wb])
                    wg3 = wg.ap[:].rearrange("p (g k c) -> p g k c", g=4, k=8)
                    wb3 = wb.ap[:].rearrange("p (r k c) -> p r k c", r=4, k=4)
                    for r in range(4):
                        psg = bankA()
                        for kc in range(8):
                            mm(psg.ap[:, 0:n], wg3[:, r, kc, :], hT3[:, kc, t0:t0 + n], kc == 0, kc == 7, [wg, hres[bi]], [psg])
                        act(gsig.ap[:, 0:n], psg.ap[:, 0:n], AF.Sigmoid, [psg], [gsig])
                        psz = bankA()
                        for kc in range(4):
                            mm(psz.ap[:, 0:n], wb3[:, r, kc, :], yb3[:, r * 4 + kc, 0:n], kc == 0, kc == 3, [wb, yb], [psz])
                        if r == 0:
                            tt("vector", acc.ap[:, 0:n], psz.ap[:, 0:n], gsig.ap[:, 0:n], MUL, [psz, gsig], [acc])
                        else:
                            t_ = nxt(tmpF, "F")
                            tt("vector", t_.ap[:, 0:n], psz.ap[:, 0:n], gsig.ap[:, 0:n], MUL, [psz, gsig], [t_])
                            if r < 3:
                                tt("gpsimd", acc.ap[:, 0:n], acc.ap[:, 0:n], t_.ap[:, 0:n], ADD, [acc, t_], [acc])
                            else:
                                tt("gpsimd", mT3[:, i, 0:n], acc.ap[:, 0:n], t_.ap[:, 0:n], ADD, [acc, t_], [mT])
                for i in range(8):
                    wo = load_w(wout_b, l * 8 + i)
                    wo3 = wo.ap[:].rearrange("p (k c) -> p k c", k=8)
                    ps = bankA()
                    for kc in range(8):
                        mm(ps.ap[:, 0:n], wo3[:, kc, :], mT3[:, kc, 0:n], kc == 0, kc == 7, [wo, mT], [ps])
                    stt(xb3[:, i, 0:n], ps.ap[:, 0:n], modc(l, 2, i, sc), xb3[:, i, 0:n], MUL, ADD, [ps, xb, modT], [xb])
                if dbg and l == NL - 1:
                    dma("gpsimd", xmid_dbg.rearrange("p (k t) -> p k t", k=8)[:, :, t0:t0 + n], xb3[:, :, 0:n], [xb], [Res()])
                norm_mod(xb3, n, l, 4, 3, sc, h23, 0, [xb], [h2])
                for j in range(32):
                    w1 = load_w(wff1_b, l * 32 + j)
                    w13 = w1.ap[:].rearrange("p (k c) -> p k c", k=8)
                    ps = bankA()
                    for kc in range(8):
                        mm(ps.ap[:, 0:n], w13[:, kc, :], h23[:, kc, 0:n], kc == 0, kc == 7, [w1, h2], [ps])
                    r_ = nxt(tmpF, "F")
                    act(r_.ap[:, 0:n], ps.ap[:, 0:n], AF.Relu, [ps], [r_])
                    tt("vector" if j % 2 else "gpsimd", uT3[:, j, 0:n], r_.ap[:, 0:n], r_.ap[:, 0:n], MUL, [r_], [uT])
                for i in range(8):
                    w2 = w2s[cnt % 2]
                    cnt += 1
                    dma("sync", w2.ap[:], wff2_b[l * 8 + i], [wres], [w2])
                    w23 = w2.ap[:].rearrange("p (k c) -> p k c", k=32)
                    ps = bankA()
                    for kc in range(32):
                        mm(ps.ap[:, 0:n], w23[:, kc, :], uT3[:, kc, 0:n], kc == 0, kc == 31, [w2, uT], [ps])
                    stt(xb3[:, i, 0:n], ps.ap[:, 0:n], modc(l, 5, i, sc), xb3[:, i, 0:n], MUL, ADD, [ps, xb, modT], [xb])
                if not last:
                    dma("gpsimd", xdst3[:, :, t0:t0 + n], xb3[:, :, 0:n], [xb], [xres[bi]])
                elif t0 >= NCTX:
                    ps = bankA()
                    for kc in range(8):
                        sq_ = nxt(tmpB, "B")
                        act(sq_.ap[:, 0:n], xb3[:, kc, 0:n], AF.Square, [xb], [sq_])
                        mm(ps.ap[:, 0:n], ones_b, sq_.ap[:, 0:n], kc == 0, kc == 7, [sq_, cbf], [ps])
                    rs = nxt(tmpR, "R")
                    act(rs.ap[:, 0:n], ps.ap[:, 0:n], AF.Sqrt, [ps, smallc], [rs], scale=1.0 / DM, bias=smallc.ap[:, 3:4])
                    recip(rs.ap[:, 0:n], rs.ap[:, 0:n], [rs], [rs])
                    for kc in range(8):
                        stt(ot3[:, kc, 0:n], xb3[:, kc, 0:n], pvc("gfin", kc), rs.ap[:, 0:n], MUL, MUL, [xb, rs, pv], [ot])
                    od = dma("gpsimd", outT[sq].rearrange("p (k t) -> p k t", k=8)[:, :, t0 - NCTX:t0 - NCTX + n], ot3[:, :, 0:n], [ot], [Res()])
                    out_deps.append(od)
            AR.release(m_)

        xres = [Res("x%d" % i) for i in range(len(BLKS))]
        out_deps = []
        xmid_dbg = dscr("xmid_dbg", [128, 8 * T], F32) if dbg else None
        tsc_dbg = dscr("tsc_dbg", [128, NT * 24], F32) if dbg else None
        dB_dbg = dscr("dB_dbg", [128, 288], F32) if dbg else None
        zt = None

        for sq in range(NSEQ):
            for l in range(NL):
                m_ = AR.mark()
                xbs = [AR.alloc(8 * 512, F32, "xb1_%d" % i) for i in range(2)]
                xsrc = xT_in[sq] if l == 0 else xT_d
                xsrc3 = xsrc.rearrange("p (k t) -> p k t", k=8)
                for bi, (t0, n) in enumerate(BLKS):
                    xb = xbs[bi % 2]
                    xb3 = xb.ap[:].rearrange("p (k t) -> p k t", k=8)
                    dma("sync", xb3[:, :, 0:n], xsrc3[:, :, t0:t0 + n], [xres[bi]], [xb])
                    sc = 4 if t0 < NCTX else sq
                    norm_mod(xb3, n, l, 1, 0, sc, hT3, t0, [xb], [hres[bi]])
                if dbg and l == NL - 1:
                    dma("gpsimd", hT_dbg[:, :], hT.ap[:], hres, [Res()])
                AR.release(m_)
                for nm, fn, br in (("A", mixer_A, 0), ("B", mixer_B, 1), ("C", mixer_C, 2), ("D", mixer_D, 3)):
                    if nm in stages:
                        fn(l, sq)
                    else:
                        z = nxt(tmpB, "B")
                        memset("vector", z.ap[:], 0.0, [z])
                        for c in range(4):
                            for (t0, n) in BLKS:
                                dma("gpsimd", yT_d[br * 4 + c][:, t0:t0 + n], z.ap[:, 0:n], [z], [yres_d])
                S.barrier()
                p34(l, sq, sq, l == NL - 1)
        S.barrier()
        S.ops["sync"].append(([(k, S.cnt[k]) for k in S.dma_pool["gpsimd"] + S.dma_pool["sync"] if S.cnt[k] > 0], None, None, 0))
        print("ops recorded:", S.nops, flush=True)
        with nc.Block() as block:
            S.replay(block)
    return nc


def make_inputs(inp, core, nseq, NL=4):
    x = inp["x"][core * nseq:(core + 1) * nseq]
    ctx = inp["ctx"][core * nseq:(core + 1) * nseq]
    xc = np.concatenate([ctx, x], axis=1)
    xT = np.ascontiguousarray(xc.reshape(nseq, T, 8, 128).transpose(0, 3, 2, 1)).reshape(nseq, 128, 8 * T)
    return {"xT": xT, "pv": host_pv(inp, core, nseq)}


_SHARED = {}


def shared_inputs(inp):
    cst, rope = host_consts()
    ich = in_chunks()
    win = np.concatenate([layout_w(inp["w_in"][l], ich, 8) for l in range(L)], axis=0)
    wff1 = np.concatenate([layout_w(inp["w_ff1"][l], simple_chunks(32), 8) for l in range(L)], axis=0)
    wout = np.concatenate([layout_w(inp["w_out"][l], simple_chunks(8), 8) for l in range(L)], axis=0)
    wbr = np.concatenate([layout_w(inp["w_branch"][l].reshape(2048, 1024), simple_chunks(8), 16) for l in range(L)], axis=0)
    wff2 = np.concatenate([layout_w(inp["w_ff2"][l], simple_chunks(8), 32) for l in range(L)], axis=0)
    return {"cst": cst, "rope": rope, "w_ada": np.ascontiguousarray(inp["w_ada"]), "win": win, "wff1": wff1, "wout": wout,
            "wbr": wbr, "wff2": wff2, "wuq": np.ascontiguousarray(inp["w_mla_uq"]), "wuk": np.ascontiguousarray(inp["w_mla_uk"]),
            "wuv": np.ascontiguousarray(inp["w_mla_uv"])}


def kernel(**inputs):
    inp = {k: np.asarray(v, dtype=np.float32) for k, v in inputs.items()}
    ncores = 8
    nseq = inp["x"].shape[0] // ncores
    sh = shared_inputs(inp)
    in_maps = []
    for c in range(ncores):
        m = dict(sh)
        m.update(make_inputs(inp, c, nseq))
        in_maps.append(m)
    nc = build(NSEQ=nseq, NL=L)
    res = run_bass_kernel_spmd(nc, in_maps, core_ids=list(range(ncores)))
    outs = []
    for c in range(ncores):
        o = np.asarray(res.results[c]["outT"]).reshape(nseq, 128, 8, NLAT)
        outs.append(o.transpose(0, 3, 2, 1).reshape(nseq, NLAT, DM))
    return np.ascontiguousarray(np.concatenate(outs, axis=0)).astype(np.float32)
```

```python
import numpy as np
from contextlib import ExitStack
import concourse.bass as bass
import concourse.mybir as mybir
from concourse.bass_utils import run_bass_kernel_spmd

F32 = mybir.dt.float32
BF16 = mybir.dt.bfloat16
AF = mybir.ActivationFunctionType
ALU = mybir.AluOpType

DM = 1024
L = 4
T = 2304
NCTX = 256
NLAT = 2048
NT = 18
EPS = 1e-6
BLKS = [(0, 256), (256, 512), (768, 512), (1280, 512), (1792, 512)]
NCH_IN = 76
CH_AI, CH_AFF, CH_AFB, CH_AQ, CH_AG = 0, 4, 8, 12, 16
CH_BK, CH_BQ, CH_BV, CH_BO, CH_BIG, CH_BLF = 20, 22, 24, 28, 32, 33
CH_CK, CH_CV, CH_CQ, CH_DCKV, CH_DKR, CH_DCQ, CH_GATES = 34, 35, 36, 40, 41, 42, 44
OFF = dict(a_i=0, a_f_fwd=512, a_f_bwd=1024, b_k=1536, b_v=1792, b_gates=2304, c_k=2320, c_v=2448, d_ckv=2576,
           d_krope=2704, a_q=2768, a_g=3280, b_q=3792, b_o=4048, c_q=4560, d_cq=5072, gates=5328)

ENGINES = ("tensor", "vector", "scalar", "gpsimd", "sync")


class Res:
    __slots__ = ("name", "w", "r")

    def __init__(self, name=""):
        self.name = name
        self.w = None
        self.r = {}


class Sched:
    def __init__(self, nc, stack, n_dma_sems=8):
        self.nc = nc
        self.ops = {e: [] for e in ENGINES}
        self.sems = {}
        self.cnt = {}
        self.known = {e: {} for e in ENGINES}
        for e in ENGINES:
            self.sems[e] = stack.enter_context(nc.semaphore("s_" + e))
            self.cnt[e] = 0
        self.dma_pool = {}
        self.dma_i = {}
        for q in ("sync", "gpsimd"):
            ks = []
            for i in range(n_dma_sems):
                k = "d_%s_%d" % (q, i)
                self.sems[k] = stack.enter_context(nc.semaphore(k))
                self.cnt[k] = 0
                ks.append(k)
            self.dma_pool[q] = ks
            self.dma_i[q] = 0
        self.nops = 0

    def _emit(self, eng, fn, reads, writes, semkey, inc, extra=()):
        waits = {}

        def need(dep, raw):
            if dep is None:
                return
            k, v = dep
            if k == eng and (eng == "tensor" or not raw):
                return
            if v > waits.get(k, 0):
                waits[k] = v

        for r in reads:
            need(r.w, True)
        for w in writes:
            need(w.w, False)
            for k, v in w.r.items():
                need((k, v), False)
        for d in extra:
            need(d, True)
        kn = self.known[eng]
        wl = []
        for k, v in waits.items():
            if kn.get(k, 0) >= v:
                continue
            kn[k] = v
            wl.append((k, v))
        self.cnt[semkey] += inc
        val = self.cnt[semkey]
        for r in reads:
            if r.r.get(semkey, 0) < val:
                r.r[semkey] = val
        for w in writes:
            w.w = (semkey, val)
            w.r = {}
        self.ops[eng].append((wl, fn, semkey, inc))
        self.nops += 1
        return (semkey, val)

    def op(self, eng, fn, reads=(), writes=()):
        return self._emit(eng, fn, reads, writes, eng, 1)

    def dma(self, q, out, in_, reads=(), writes=()):
        pool = self.dma_pool[q]
        k = pool[self.dma_i[q] % len(pool)]
        self.dma_i[q] += 1
        prev = ((k, self.cnt[k]),) if self.cnt[k] > 0 else ()
        return self._emit(q, lambda e: e.dma_start(out=out, in_=in_), reads, writes, k, 16, extra=prev)

    def barrier(self):
        allk = [(k, v) for k, v in self.cnt.items() if v > 0]
        for e in ENGINES:
            kn = self.known[e]
            wl = []
            for k, v in allk:
                if k == e or kn.get(k, 0) >= v:
                    continue
                kn[k] = v
                wl.append((k, v))
            if wl:
                self.ops[e].append((wl, None, None, 0))

    def replay(self, block):
        sems = self.sems

        def mk(engname):
            lst = self.ops[engname]

            def body(e):
                for wl, fn, semkey, inc in lst:
                    for k, v in wl:
                        e.wait_ge(sems[k], v)
                    if fn is not None:
                        fn(e).then_inc(sems[semkey], inc)
            return body

        block.tensor(mk("tensor"))
        block.vector(mk("vector"))
        block.scalar(mk("scalar"))
        block.gpsimd(mk("gpsimd"))
        block.sync(mk("sync"))


class Buf:
    __slots__ = ("ap", "res")

    def __init__(self, ap, name=""):
        self.ap = ap
        self.res = Res(name)


PV_SPEC = [("g1", L * 8), ("g2", L * 8), ("gfin", 8), ("bada", L * 48), ("lbl", L * 8), ("hng", L),
           ("convw", L * 12), ("mgb", L * 2), ("mng", L * 128), ("gqg", L), ("gkg", L), ("mqg", L * 2),
           ("mkvg", L), ("sel", 2), ("cT", 64)]


def pv_offsets():
    o = {}
    off = 0
    for n, w in PV_SPEC:
        o[n] = (off, w)
        off += w
    return o, off


CST_SPEC = [("ident", 128), ("ones", 128), ("blk64", 128), ("prot", 128), ("masks", 128), ("onehot", 1024)]


def cst_offsets():
    o = {}
    off = 0
    for n, w in CST_SPEC:
        o[n] = (off, w)
        off += w
    return o, off


def in_chunks():
    ch = []

    def grp(name, n):
        for i in range(n):
            ch.append([(OFF[name] + i * 128, 128, 0)])
    grp("a_i", 4); grp("a_f_fwd", 4); grp("a_f_bwd", 4); grp("a_q", 4); grp("a_g", 4)
    grp("b_k", 2); grp("b_q", 2); grp("b_v", 4); grp("b_o", 4)
    g = OFF["b_gates"]
    ch.append([(g + 0, 4, 0), (g + 8, 4, 4)])
    ch.append([(g + 4, 4, 0), (g + 12, 4, 4)])
    grp("c_k", 1); grp("c_v", 1)
    for c in range(4):
        ch.append([(OFF["c_q"] + c * 64, 64, 0), (OFF["c_q"] + (c + 4) * 64, 64, 64)])
    grp("d_ckv", 1)
    ch.append([(OFF["d_krope"], 64, 0)])
    grp("d_cq", 2)
    for i in range(8):
        for r in range(4):
            ch.append([(OFF["gates"] + r * 1024 + i * 128, 128, 0)])
    assert len(ch) == NCH_IN
    return ch


def layout_w(W, chunks, nkc):
    out = np.zeros((len(chunks), 128, nkc, 128), np.float32)
    Wr = W.reshape(nkc, 128, W.shape[1])
    for j, segs in enumerate(chunks):
        for (c0, w, pos) in segs:
            out[j, :, :, pos:pos + w] = Wr[:, :, c0:c0 + w].transpose(1, 0, 2)
    return out.reshape(len(chunks), 128, nkc * 128)


def simple_chunks(n):
    return [[(i * 128, 128, 0)] for i in range(n)]


def host_consts():
    co, nc_ = cst_offsets()
    c = np.zeros((128, nc_), np.float32)
    p = np.arange(128)
    c[:, co["ident"][0]:co["ident"][0] + 128] = np.eye(128, dtype=np.float32)
    c[:, co["ones"][0]:co["ones"][0] + 128] = 1.0
    blk = (p[:, None] // 64 == p[None, :] // 64).astype(np.float32)
    c[:, co["blk64"][0]:co["blk64"][0] + 128] = blk
    prot = np.zeros((128, 128), np.float32)
    for f in range(128):
        if f % 32 < 16:
            prot[f + 16, f] = -1.0
        else:
            prot[f - 16, f] = 1.0
    c[:, co["prot"][0]:co["prot"][0] + 128] = prot
    s = (p % 64)[:, None]
    t = np.arange(64)[None, :]
    m0 = co["masks"][0]
    c[:, m0:m0 + 64] = (s <= t).astype(np.float32)
    c[:, m0 + 64:m0 + 128] = (s >= t).astype(np.float32)
    o0 = co["onehot"][0]
    for r in range(8):
        c[r, o0 + r * 128:o0 + (r + 1) * 128] = 1.0
    inv = (10000.0 ** (-np.arange(16, dtype=np.float32) / 16)).astype(np.float32)
    tt = np.arange(NLAT)
    row = (tt // 64).astype(np.float32)
    col = (tt % 64).astype(np.float32)
    rope = np.zeros((128, 2 * NLAT), np.float32)
    for f in range(128):
        j = f % 64
        pos = row if j < 32 else col
        ang = (pos * inv[j % 16]).astype(np.float32)
        rope[f, :NLAT] = np.cos(ang)
        rope[f, NLAT:] = np.sin(ang)
    return c, rope


def host_pv(inp, core, nseq):
    po, npv = pv_offsets()
    pv = np.zeros((128, npv), np.float32)

    def put(name, arr):
        o, w = po[name]
        assert arr.shape == (128, w), (name, arr.shape, w)
        pv[:, o:o + w] = arr
    put("g1", inp["g_norm1"].reshape(L, 8, 128).transpose(2, 0, 1).reshape(128, L * 8))
    put("g2", inp["g_norm2"].reshape(L, 8, 128).transpose(2, 0, 1).reshape(128, L * 8))
    put("gfin", inp["g_final"].reshape(8, 128).T)
    put("bada", inp["b_ada"].reshape(L, 48, 128).transpose(2, 0, 1).reshape(128, L * 48))
    put("lbl", inp["hgrn_lb_logits"].reshape(L, 2, 4, 128).transpose(3, 0, 1, 2).reshape(128, L * 8))
    put("hng", inp["hgrn_norm_g"].T)
    put("convw", inp["mlstm_conv_w"].reshape(L, 2, 3, 2, 128).transpose(4, 0, 1, 3, 2).reshape(128, L * 12))
    mgb = np.zeros((128, L, 2), np.float32)
    b = inp["b_mlstm_gates"]
    mgb[0:4, :, 0] = b[:, 0:4].T
    mgb[4:8, :, 0] = b[:, 8:12].T
    mgb[0:4, :, 1] = b[:, 4:8].T
    mgb[4:8, :, 1] = b[:, 12:16].T
    put("mgb", mgb.reshape(128, L * 2))
    put("mng", np.broadcast_to(inp["mlstm_norm_g"].reshape(1, L * 128), (128, L * 128)))
    put("gqg", np.tile(inp["gqa_q_norm_g"].T, (2, 1)))
    put("gkg", np.tile(inp["gqa_k_norm_g"].T, (2, 1)))
    put("mqg", inp["mla_q_norm_g"].reshape(L, 2, 128).transpose(2, 0, 1).reshape(128, L * 2))
    put("mkvg", inp["mla_kv_norm_g"].T)
    sel = np.zeros((128, 2), np.float32)
    sel[0:4, 0] = 1.0
    sel[:, 1] = sel[:, 0] - 1.0
    put("sel", sel)
    cT = np.zeros((128, 8, 8), np.float32)
    cc = inp["c"][core * nseq:(core + 1) * nseq]
    cT[:, :, 0:nseq] = cc.reshape(nseq, 8, 128).transpose(2, 1, 0)
    cT[:, :, 4] = inp["c_ctx"].reshape(8, 128).T
    put("cT", cT.reshape(128, 64))
    return pv


def build(NSEQ=4, NL=4, dbg=False, stages="ABCD"):
    nc = bass.Bass("TRN2", target_bir_lowering=False)
    po, NPV = pv_offsets()
    co, NCST = cst_offsets()

    def din(name, shape, dt=F32):
        return nc.dram_tensor(name, list(shape), dt, kind="ExternalInput").ap()

    xT_in = din("xT", [NSEQ, 128, 8 * T])
    pv_in = din("pv", [128, NPV])
    cst_in = din("cst", [128, NCST])
    rope_in = din("rope", [128, 2 * NLAT])
    wada_in = din("w_ada", [L, DM, 6 * DM])
    win_in = din("win", [L * NCH_IN, 128, 1024])
    wff1_in = din("wff1", [L * 32, 128, 1024])
    wout_in = din("wout", [L * 8, 128, 1024])
    wbr_in = din("wbr", [L * 8, 128, 2048])
    wff2_in = din("wff2", [L * 8, 128, 4096])
    wuq_in = din("wuq", [L, 256, 768])
    wuk_in = din("wuk", [L, 128, 512])
    wuv_in = din("wuv", [L, 128, 512])
    outT = nc.dram_tensor("outT", [NSEQ, 128, 8 * NLAT], F32, kind="ExternalOutput").ap()

    def dscr(name, shape, dt):
        if dbg:
            return nc.dram_tensor(name, list(shape), dt, kind="ExternalOutput").ap()
        return nc.dram_tensor(name, list(shape), dt).ap()

    win_b = nc.dram_tensor("win_b", [L * NCH_IN, 128, 1024], BF16).ap()
    wff1_b = nc.dram_tensor("wff1_b", [L * 32, 128, 1024], BF16).ap()
    wout_b = nc.dram_tensor("wout_b", [L * 8, 128, 1024], BF16).ap()
    wbr_b = nc.dram_tensor("wbr_b", [L * 8, 128, 2048], BF16).ap()
    wff2_b = nc.dram_tensor("wff2_b", [L * 8, 128, 4096], BF16).ap()
    xT_d = nc.dram_tensor("xT_d", [128, 8 * T], F32).ap()
    yT_d = dscr("yT_d", [16, 128, T], BF16)
    hT_dbg = dscr("hT_dbg", [128, 8 * T], BF16) if dbg else None

    with ExitStack() as st:
        S = Sched(nc, st)

        def sb(name, shape, dt=F32):
            return Buf(st.enter_context(nc.sbuf_tensor(name, list(shape), dt)), name)

        hT = sb("hT", [128, 8 * T], BF16)
        hT3 = hT.ap[:].rearrange("p (k t) -> p k t", k=8)
        hres = [Res("h%d" % i) for i in range(len(BLKS))]
        cst = sb("cst_sb", [128, NCST])
        cbf = sb("cbf", [128, 512], BF16)
        ropeb = sb("ropeb", [128, 2 * NLAT], BF16)
        pv = sb("pv_sb", [128, NPV])
        modT = sb("modT", [128, L * 48 * 8])
        mod4 = modT.ap[:].rearrange("p (l j s) -> p l j s", l=L, j=48)
        lbt = sb("lbt", [128, 3 * L * 8])
        smallc = sb("smallc", [128, 8])
        wslots = [sb("wslot%d" % i, [128, 1024], BF16) for i in range(8)]
        tmpF = [sb("tmpF%d" % i, [128, 512]) for i in range(6)]
        tmpB = [sb("tmpB%d" % i, [128, 512], BF16) for i in range(6)]
        tmpR = [sb("tmpR%d" % i, [128, 512]) for i in range(3)]
        tmpE = [sb("tmpE%d" % i, [128, 512], BF16) for i in range(3)]
        small = [sb("small%d" % i, [128, 64]) for i in range(8)]
        ARENA_B = int(nc.sbuf_bytes_remaining) - 2048
        ARENA_B -= ARENA_B % 64
        arena = sb("arena", [128, ARENA_B // 2], BF16)
        banks = [Buf(st.enter_context(nc.psum_tensor("bank%d" % i, [128, 512], F32)), "bank%d" % i) for i in range(8)]

        rr = {"w": 0, "F": 0, "B": 0, "E": 0, "s": 0, "A": 0, "R": 0}

        def nxt(lst, key):
            b = lst[rr[key] % len(lst)]
            rr[key] += 1
            return b

        def bankA():
            return nxt(banks[0:4], "A")
        bankO = banks[4:8]
        rr["S"] = 0

        def bankS():
            return nxt(banks, "S")

        class Arena:
            def __init__(self):
                self.off = 0

            def mark(self):
                return self.off

            def release(self, m):
                S.barrier()
                self.off = m

            def alloc(self, nelem, dt, name="a"):
                nb = nelem * (4 if dt == F32 else 2)
                nb = (nb + 63) // 64 * 64
                assert self.off + nb <= ARENA_B, ("arena overflow", name, self.off, nb, ARENA_B)
                v = arena.ap[:, self.off // 2:(self.off + nb) // 2]
                self.off += nb
                if dt == F32:
                    v = v.bitcast(F32)
                return Buf(v[:, 0:nelem], name)
        AR = Arena()

        def pvc(name, i=0, n=1):
            o, w = po[name]
            return pv.ap[:, o + i:o + i + n]

        def cstv(name):
            o, w = co[name]
            return cst.ap[:, o:o + w]

        def R_(x):
            return [b.res if isinstance(b, Buf) else b for b in x]

        def mm(out, lhsT, rhs, start, stop, R, W):
            S.op("tensor", lambda e: e.matmul(out, lhsT=lhsT, rhs=rhs, start=start, stop=stop), R_(R), R_(W))

        def tp(out, in_, ident, R, W):
            S.op("tensor", lambda e: e.transpose(out=out, in_=in_, identity=ident), R_(R), R_(W))

        def act(out, in_, func, R, W, scale=1.0, bias=None):
            if bias is None:
                S.op("scalar", lambda e: e.activation(out=out, in_=in_, func=func, scale=scale), R_(R), R_(W))
            else:
                S.op("scalar", lambda e: e.activation(out=out, in_=in_, func=func, scale=scale, bias=bias), R_(R), R_(W))

        def tt(eng, out, in0, in1, op, R, W):
            S.op(eng, lambda e: e.tensor_tensor(out=out, in0=in0, in1=in1, op=op), R_(R), R_(W))

        def ts(eng, out, in0, s1, s2, op0, op1, R, W):
            if s2 is None:
                S.op(eng, lambda e: e.tensor_scalar(out=out, in0=in0, scalar1=s1, scalar2=None, op0=op0), R_(R), R_(W))
            else:
                S.op(eng, lambda e: e.tensor_scalar(out=out, in0=in0, scalar1=s1, scalar2=s2, op0=op0, op1=op1), R_(R), R_(W))

        def stt(out, in0, scalar, in1, op0, op1, R, W):
            S.op("vector", lambda e: e.scalar_tensor_tensor(out=out, in0=in0, scalar=scalar, in1=in1, op0=op0, op1=op1), R_(R), R_(W))

        def cp(eng, out, in_, R, W):
            if eng == "scalar":
                S.op("scalar", lambda e: e.copy(out=out, in_=in_), R_(R), R_(W))
            else:
                S.op(eng, lambda e: e.tensor_copy(out=out, in_=in_), R_(R), R_(W))

        def scan(out, d0, d1, init, op0, op1, R, W):
            S.op("vector", lambda e: e.tensor_tensor_scan(out=out, data0=d0, data1=d1, initial=init, op0=op0, op1=op1), R_(R), R_(W))

        def recip(out, in_, R, W):
            S.op("vector", lambda e: e.reciprocal(out=out, in_=in_), R_(R), R_(W))

        def memset(eng, ap, val, W):
            S.op(eng, lambda e: e.memset(ap, val), [], R_(W))

        def dma(q, out, in_, R, W):
            return S.dma(q, out, in_, R_(R), R_(W))

        MUL, ADD, SUB, MAX = ALU.mult, ALU.add, ALU.subtract, ALU.max

        dma("sync", cst.ap[:], cst_in[:, :], [], [cst])
        dma("sync", pv.ap[:], pv_in[:, :], [], [pv])
        memset("vector", smallc.ap[:, 0:1], 0.0, [smallc])
        memset("vector", smallc.ap[:, 1:2], float(np.log(0.125)), [smallc])
        memset("vector", smallc.ap[:, 2:3], 1.0, [smallc])
        memset("vector", smallc.ap[:, 3:4], EPS, [smallc])
        cp("vector", cbf.ap[:, 0:512], cst.ap[:, 0:512], [cst], [cbf])
        ident_f = cstv("ident")
        ident_b = cbf.ap[:, 0:128]
        ones_b = cbf.ap[:, 128:256]
        blk64_b = cbf.ap[:, 256:384]
        prot_b = cbf.ap[:, 384:512]
        masks = cstv("masks")
        onehot = cstv("onehot")
        zcol = smallc.ap[:, 0:1]
        m0 = AR.mark()
        rstage = AR.alloc(2 * NLAT, F32, "rstage")
        dma("sync", rstage.ap[:], rope_in[:, :], [], [rstage])
        cp("vector", ropeb.ap[:, 0:NLAT], rstage.ap[:, 0:NLAT], [rstage], [ropeb])
        cp("gpsimd", ropeb.ap[:, NLAT:], rstage.ap[:, NLAT:], [rstage], [ropeb])
        AR.release(m0)
        cosb = ropeb.ap[:, 0:NLAT]
        sinb = ropeb.ap[:, NLAT:]

        wres = Res("wconv")
        m0 = AR.mark()
        stf = [AR.alloc(4096, F32, "stf%d" % i) for i in range(3)]
        stb = [AR.alloc(4096, BF16, "stb%d" % i) for i in range(3)]
        ci = 0
        for (src, dst, per_l, W) in ((win_in, win_b, NCH_IN, 1024), (wff1_in, wff1_b, 32, 1024), (wout_in, wout_b, 8, 1024),
                                     (wbr_in, wbr_b, 8, 2048), (wff2_in, wff2_b, 8, 4096)):
            g = 4096 // W
            for j0 in range(0, per_l * NL, g):
                f = stf[ci % 3]
                b = stb[ci % 3]
                dma("sync", f.ap[:].rearrange("p (g w) -> p g w", g=g), src[j0:j0 + g].rearrange("g p w -> p g w"), [], [f])
                eng = ("vector", "gpsimd", "scalar")[ci % 3]
                cp(eng, b.ap[:], f.ap[:], [f], [b])
                dma("gpsimd", dst[j0:j0 + g].rearrange("g p w -> p g w"), b.ap[:].rearrange("p (g w) -> p g w", g=g), [b], [wres])
                ci += 1
        AR.release(m0)

        m0 = AR.mark()
        sT = AR.alloc(64, F32, "sT")
        act(sT.ap[:], pvc("cT", 0, 64), AF.Silu, [pv], [sT])
        sT3 = sT.ap[:].rearrange("p (k s) -> p k s", k=8)
        wst = [AR.alloc(8 * 512, F32, "wada%d" % i) for i in range(2)]
        for l in range(NL):
            for ct in range(12):
                w = wst[(l * 12 + ct) % 2]
                dma("sync", w.ap[:].rearrange("p (k n) -> p k n", k=8),
                    wada_in[l].rearrange("(k p) n -> p k n", p=128)[:, :, ct * 512:(ct + 1) * 512], [], [w])
                w3 = w.ap[:].rearrange("p (k n) -> p k n", k=8)
                ps = bankA()
                for fc in range(4):
                    for kc in range(8):
                        mm(ps.ap[:, fc * 8:(fc + 1) * 8], w3[:, kc, fc * 128:(fc + 1) * 128], sT3[:, kc, :], kc == 0, kc == 7, [w, sT], [ps])
                o, _ = po["bada"]
                bb = pv.ap[:, o + l * 48 + ct * 4:o + l * 48 + ct * 4 + 4].unsqueeze(2).to_broadcast([128, 4, 8])
                tt("vector", mod4[:, l, ct * 4:(ct + 1) * 4, :], ps.ap[:, 0:32].rearrange("p (a s) -> p a s", a=4), bb, ADD, [ps, pv], [modT])
            for (jm, gname) in ((8, "g1"), (32, "g2")):
                o, _ = po[gname]
                gb = pv.ap[:, o + l * 8:o + l * 8 + 8].unsqueeze(2).to_broadcast([128, 8, 8])
                stt(mod4[:, l, jm:jm + 8, :], mod4[:, l, jm:jm + 8, :], 1.0, gb, ADD, MUL, [modT, pv], [modT])
        AR.release(m0)

        def modc(l, m, kc, s):
            return mod4[:, l, m * 8 + kc, s:s + 1]

        lb3 = lbt.ap[:].rearrange("p (a l j) -> p a l j", a=3, l=L)
        ex = small[0]
        act(ex.ap[:, 0:L * 8], pvc("lbl", 0, L * 8), AF.Exp, [pv], [ex])
        ex3 = ex.ap[:, 0:L * 8].rearrange("p (l j) -> p l j", l=L)
        ssum = small[1]
        tt("vector", ssum.ap[:, 0:8], ex3[:, 0, :], ex3[:, 1, :], ADD, [ex], [ssum])
        tt("vector", ssum.ap[:, 0:8], ssum.ap[:, 0:8], ex3[:, 2, :], ADD, [ex, ssum], [ssum])
        tt("vector", ssum.ap[:, 0:8], ssum.ap[:, 0:8], ex3[:, 3, :], ADD, [ex, ssum], [ssum])
        recip(ssum.ap[:, 0:8], ssum.ap[:, 0:8], [ssum], [ssum])
        memset("vector", lb3[:, 0, 0, :], 0.0, [lbt])
        for l in range(1, L):
            tt("vector", lb3[:, 0, l, :], lb3[:, 0, l - 1, :], ex3[:, l, :], ADD, [ex, lbt], [lbt])
        for l in range(1, L):
            tt("vector", lb3[:, 0, l, :], lb3[:, 0, l, :], ssum.ap[:, 0:8], MUL, [ssum, lbt], [lbt])
        for l in range(L):
            ts("vector", lb3[:, 2, l, :], lb3[:, 0, l, :], -1.0, None, ADD, None, [lbt], [lbt])
            ts("vector", lb3[:, 1, l, :], lb3[:, 2, l, :], -1.0, None, MUL, None, [lbt], [lbt])
        S.barrier()

        def load_w(wb, idx, nelem=1024, slot=None):
            s_ = slot if slot is not None else nxt(wslots, "w")
            dma("sync", s_.ap[:, 0:nelem], wb[idx], [wres], [s_])
            return s_

        def proj_fm(wslot, t0, n, bi, width=128):
            ps = bankA()
            w3 = wslot.ap[:].rearrange("p (k c) -> p k c", k=8)
            for kc in range(8):
                mm(ps.ap[0:width, 0:n], w3[:, kc, 0:width], hT3[:, kc, t0:t0 + n], kc == 0, kc == 7, [wslot, hres[bi]], [ps])
            return ps

        def proj_tm(wslot, tile, ps, c0, width=128):
            w3 = wslot.ap[:].rearrange("p (k c) -> p k c", k=8)
            bi = 0 if tile < 2 else 1 + (tile - 2) // 4
            for kc in range(8):
                mm(ps.ap[:, c0:c0 + width], hT3[:, kc, tile * 128:(tile + 1) * 128], w3[:, kc, 0:width], kc == 0, kc == 7, [wslot, hres[bi]], [ps])

        def norm_mod(xb3, n, l, mA, mB, s, out3, t0, R, W):
            ps = bankA()
            for kc in range(8):
                sq = nxt(tmpB, "B")
                act(sq.ap[:, 0:n], xb3[:, kc, 0:n], AF.Square, R, [sq])
                mm(ps.ap[:, 0:n], ones_b, sq.ap[:, 0:n], kc == 0, kc == 7, [sq, cbf], [ps])
            rs = nxt(tmpR, "R")
            act(rs.ap[:, 0:n], ps.ap[:, 0:n], AF.Sqrt, [ps, smallc], [rs], scale=1.0 / DM, bias=smallc.ap[:, 3:4])
            recip(rs.ap[:, 0:n], rs.ap[:, 0:n], [rs], [rs])
            for kc in range(8):
                t_ = nxt(tmpF, "F")
                tt("vector", t_.ap[:, 0:n], xb3[:, kc, 0:n], rs.ap[:, 0:n], MUL, R + [rs], [t_])
                act(out3[:, kc, t0:t0 + n], t_.ap[:, 0:n], AF.Identity, [t_, modT], W, scale=modc(l, mA, kc, s), bias=modc(l, mB, kc, s))

        def rms_fm(srcs, rows, n, blkmat, denom, gaps, outs, R, W, keep=None):
            kfs = []
            ps2 = bankA()
            for i, (src, sres) in enumerate(srcs):
                kf = nxt(tmpF, "F")
                cp("scalar", kf.ap[0:rows, 0:n], src, [sres], [kf])
                sq = nxt(tmpB, "B")
                tt("vector", sq.ap[0:rows, 0:n], kf.ap[0:rows, 0:n], kf.ap[0:rows, 0:n], MUL, [kf], [sq])
                mm(ps2.ap[0:rows, 0:n], blkmat[0:rows, 0:rows], sq.ap[0:rows, 0:n], i == 0, i == len(srcs) - 1, [sq, cbf], [ps2])
                kfs.append(kf)
            rs = nxt(tmpR, "R")
            act(rs.ap[0:rows, 0:n], ps2.ap[0:rows, 0:n], AF.Sqrt, [ps2, smallc], [rs], scale=1.0 / denom, bias=smallc.ap[0:rows, 3:4])
            recip(rs.ap[0:rows, 0:n], rs.ap[0:rows, 0:n], [rs], [rs])
            for kf, g, o in zip(kfs, gaps, outs):
                stt(o, kf.ap[0:rows, 0:n], g, rs.ap[0:rows, 0:n], MUL, MUL, [kf, rs, pv] + R, W)

        def rope(src_b, sres, rows, n, t0, out, W):
            lt = t0 - NCTX
            ps3 = bankA()
            mm(ps3.ap[0:rows, 0:n], prot_b[0:rows, 0:rows], src_b, True, True, [sres, cbf], [ps3])
            t1 = nxt(tmpF, "F")
            tt("gpsimd", t1.ap[0:rows, 0:n], src_b, cosb[0:rows, lt:lt + n], MUL, [sres, ropeb], [t1])
            t2 = nxt(tmpF, "F")
            tt("vector", t2.ap[0:rows, 0:n], ps3.ap[0:rows, 0:n], sinb[0:rows, lt:lt + n], MUL, [ps3, ropeb], [t2])
            tt("vector", out, t1.ap[0:rows, 0:n], t2.ap[0:rows, 0:n], ADD, [t1, t2], W)

        def attention(pairs_fn, v_fn, nkt, n, dv, scale, y_fn, R, yres):
            nqs = n // 128

            def s_stage(kt):
                psS_ = bankA()
                prs = pairs_fn(kt)
                for i, (k_ap, q_ap) in enumerate(prs):
                    mm(psS_.ap[:, 0:n], k_ap, q_ap, i == 0, i == len(prs) - 1, R, [psS_])
                return psS_
            ps_next = s_stage(0)
            for kt in range(nkt):
                psS = ps_next
                if kt + 1 < nkt:
                    ps_next = s_stage(kt + 1)
                E = nxt(tmpE, "E")
                act(E.ap[:, 0:n], psS.ap[:, 0:n], AF.Exp, [psS], [E], scale=scale)
                for qs in range(nqs):
                    mm(bankO[qs].ap[:, 0:dv + 1], E.ap[:, qs * 128:(qs + 1) * 128], v_fn(kt), kt == 0, kt == nkt - 1, [E] + R, [bankO[qs]])
            for qs in range(nqs):
                rd = nxt(small, "s")
                recip(rd.ap[:, 0:1], bankO[qs].ap[:, dv:dv + 1], [bankO[qs]], [rd])
                ts("vector", y_fn(qs), bankO[qs].ap[:, 0:dv], rd.ap[:, 0:1], None, MUL, None, [bankO[qs], rd], [yres])

        def store_yT(ytok, yres, n, t0, branch):
            nqs = n // 128
            y3 = ytok.ap[:].rearrange("p (q f) -> p q f", f=512)
            for c in range(4):
                ps = bankA()
                pb = ps.ap[:].bitcast(BF16)
                for qs in range(nqs):
                    tp(pb[:, qs * 128:(qs + 1) * 128], y3[:, qs, c * 128:(c + 1) * 128], ident_b, [yres, cbf], [ps])
                yt = nxt(tmpB, "B")
                cp("scalar", yt.ap[:, 0:n], pb[:, 0:n], [ps], [yt])
                dma("gpsimd", yT_d[branch * 4 + c][:, t0:t0 + n], yt.ap[:, 0:n], [yt], [yres_d])

        yres_d = Res("yT_d")

        def mixer_C(l, s):
            m_ = AR.mark()
            kT = AR.alloc(T, BF16, "kT_c")
            vC = AR.alloc(NT * 2 * 65, BF16, "vC")
            vC4 = vC.ap[:].rearrange("p (t h d) -> p t h d", t=NT, h=2)
            ytok = AR.alloc(4 * 512, BF16, "ytokC")
            y3 = ytok.ap[:].rearrange("p (q f) -> p q f", f=512)
            qTs = [AR.alloc(4 * 512, BF16, "qT%d" % i) for i in range(2)]
            memset("gpsimd", vC.ap[:], 1.0, [vC])
            wk = load_w(win_b, l * NCH_IN + CH_CK)
            for bi, (t0, n) in enumerate(BLKS):
                ps = proj_fm(wk, t0, n, bi)
                if t0 < NCTX:
                    rms_fm([(ps.ap[:, 0:n], ps)], 128, n, blk64_b, 64.0, [pvc("gkg", l)], [kT.ap[:, t0:t0 + n]], [], [kT])
                else:
                    kn = nxt(tmpB, "B")
                    rms_fm([(ps.ap[:, 0:n], ps)], 128, n, blk64_b, 64.0, [pvc("gkg", l)], [kn.ap[:, 0:n]], [], [kn])
                    rope(kn.ap[:, 0:n], kn, 128, n, t0, kT.ap[:, t0:t0 + n], [kT])
            wv = load_w(win_b, l * NCH_IN + CH_CV)
            for tl in range(NT):
                ps = bankA()
                proj_tm(wv, tl, ps, 0)
                cp("vector", vC4[:, tl, :, 0:64], ps.ap[:, 0:128].rearrange("p (h d) -> p h d", h=2), [ps], [vC])
            wq = [load_w(win_b, l * NCH_IN + CH_CQ + c) for c in range(4)]
            for bi, (t0, n) in enumerate(BLKS):
                qT = qTs[bi % 2]
                q3 = qT.ap[:].rearrange("p (c t) -> p c t", c=4)
                for c in range(4):
                    ps = proj_fm(wq[c], t0, n, bi)
                    if t0 < NCTX:
                        rms_fm([(ps.ap[:, 0:n], ps)], 128, n, blk64_b, 64.0, [pvc("gqg", l)], [q3[:, c, 0:n]], [], [qT])
                    else:
                        qn = nxt(tmpB, "B")
                        rms_fm([(ps.ap[:, 0:n], ps)], 128, n, blk64_b, 64.0, [pvc("gqg", l)], [qn.ap[:, 0:n]], [], [qn])
                        rope(qn.ap[:, 0:n], qn, 128, n, t0, q3[:, c, 0:n], [qT])
                nkt = 2 if bi == 0 else NT
                for hq in range(8):
                    kvh = hq // 4
                    c = hq % 4
                    b0 = kvh * 64

                    def pairs(kt, b0=b0, c=c, n=n, q3=q3):
                        return [(kT.ap[b0:b0 + 64, kt * 128:(kt + 1) * 128], q3[b0:b0 + 64, c, 0:n])]

                    def vfn(kt, kvh=kvh):
                        return vC4[:, kt, kvh, :]

                    def yfn(qs, hq=hq):
                        return y3[:, qs, hq * 64:(hq + 1) * 64]
                    attention(pairs, vfn, nkt, n, 64, 0.125, yfn, [kT, qT, vC], ytok)
                store_yT(ytok, ytok, n, t0, 2)
            AR.release(m_)

        def mixer_D(l, s):
            m_ = AR.mark()
            wuq_f = AR.alloc(2 * 768, F32, "wuq_f")
            wuq = AR.alloc(2 * 768, BF16, "wuq")
            wuk_f = AR.alloc(512, F32, "wuk_f")
            wuv_f = AR.alloc(512, F32, "wuv_f")
            wukv = AR.alloc(1024, BF16, "wukv")
            dma("sync", wuq_f.ap[:].rearrange("p (k n) -> p k n", k=2), wuq_in[l].rearrange("(k p) n -> p k n", p=128), [], [wuq_f])
            dma("sync", wuk_f.ap[:], wuk_in[l], [], [wuk_f])
            dma("sync", wuv_f.ap[:], wuv_in[l], [], [wuv_f])
            cp("vector", wuq.ap[:], wuq_f.ap[:], [wuq_f], [wuq])
            cp("vector", wukv.ap[:, 0:512], wuk_f.ap[:], [wuk_f], [wukv])
            cp("vector", wukv.ap[:, 512:1024], wuv_f.ap[:], [wuv_f], [wukv])
            wuq3 = wuq.ap[:].rearrange("p (k n) -> p k n", k=2)
            ckvn = AR.alloc(T, BF16, "ckvn")
            knT = AR.alloc(4 * T, BF16, "knT")
            kn3 = knT.ap[:].rearrange("p (h t) -> p h t", h=4)
            krT = AR.alloc(T, BF16, "krT")
            vD = AR.alloc(NT * 4 * 129, BF16, "vD")
            vD4 = vD.ap[:].rearrange("p (t h d) -> p t h d", t=NT, h=4)
            ytok = AR.alloc(4 * 512, BF16, "ytokD")
            y3 = ytok.ap[:].rearrange("p (q f) -> p q f", f=512)
            qn = [AR.alloc(4 * 512, BF16, "qnD%d" % i) for i in range(2)]
            qr = [AR.alloc(4 * 512, BF16, "qrD%d" % i) for i in range(2)]
            cqn = AR.alloc(2 * 512, BF16, "cqn")
            cq3 = cqn.ap[:].rearrange("p (k t) -> p k t", k=2)
            memset("gpsimd", vD.ap[:], 1.0, [vD])
            wc = load_w(win_b, l * NCH_IN + CH_DCKV)
            wr = load_w(win_b, l * NCH_IN + CH_DKR)
            for bi, (t0, n) in enumerate(BLKS):
                ps = proj_fm(wc, t0, n, bi)
                rms_fm([(ps.ap[:, 0:n], ps)], 128, n, ones_b, 128.0, [pvc("mkvg", l)], [ckvn.ap[:, t0:t0 + n]], [], [ckvn])
                for h in range(4):
                    ps = bankA()
                    mm(ps.ap[:, 0:n], wukv.ap[:, h * 128:(h + 1) * 128], ckvn.ap[:, t0:t0 + n], True, True, [wukv, ckvn], [ps])
                    cp("scalar" if h % 2 else "vector", kn3[:, h, t0:t0 + n], ps.ap[:, 0:n], [ps], [knT])
                ps = proj_fm(wr, t0, n, bi, width=64)
                if t0 < NCTX:
                    cp("scalar", krT.ap[0:64, t0:t0 + n], ps.ap[0:64, 0:n], [ps], [krT])
                else:
                    kb = nxt(tmpB, "B")
                    cp("scalar", kb.ap[0:64, 0:n], ps.ap[0:64, 0:n], [ps], [kb])
                    rope(kb.ap[0:64, 0:n], kb, 64, n, t0, krT.ap[0:64, t0:t0 + n], [krT])
            for tl in range(NT):
                ps = bankA()
                mm(ps.ap[:, 0:512], ckvn.ap[:, tl * 128:(tl + 1) * 128], wukv.ap[:, 512:1024], True, True, [wukv, ckvn], [ps])
                cp("vector" if tl % 2 else "scalar", vD4[:, tl, :, 0:128], ps.ap[:, 0:512].rearrange("p (h d) -> p h d", h=4), [ps], [vD])
            wq = [load_w(win_b, l * NCH_IN + CH_DCQ + c) for c in range(2)]
            o_mq, _ = po["mqg"]
            for bi, (t0, n) in enumerate(BLKS):
                pss = [proj_fm(wq[c], t0, n, bi) for c in range(2)]
                rms_fm([(p_.ap[:, 0:n], p_) for p_ in pss], 128, n, ones_b, 256.0,
                       [pv.ap[:, o_mq + l * 2 + c:o_mq + l * 2 + c + 1] for c in range(2)], [cq3[:, c, 0:n] for c in range(2)], [], [cqn])
                qn3 = qn[bi % 2].ap[:].rearrange("p (h t) -> p h t", h=4)
                qr3 = qr[bi % 2].ap[:].rearrange("p (h t) -> p h t", h=4)
                for h in range(4):
                    ps = bankA()
                    for kc in range(2):
                        mm(ps.ap[:, 0:n], wuq3[:, kc, h * 192:h * 192 + 128], cq3[:, kc, 0:n], kc == 0, kc == 1, [wuq, cqn], [ps])
                    cp("scalar", qn3[:, h, 0:n], ps.ap[:, 0:n], [ps], [qn[bi % 2]])
                    ps = bankA()
                    for kc in range(2):
                        mm(ps.ap[0:64, 0:n], wuq3[:, kc, h * 192 + 128:h * 192 + 192], cq3[:, kc, 0:n], kc == 0, kc == 1, [wuq, cqn], [ps])
                    if t0 < NCTX:
                        cp("vector", qr3[0:64, h, 0:n], ps.ap[0:64, 0:n], [ps], [qr[bi % 2]])
                    else:
                        qb = nxt(tmpB, "B")
                        cp("scalar", qb.ap[0:64, 0:n], ps.ap[0:64, 0:n], [ps], [qb])
                        rope(qb.ap[0:64, 0:n], qb, 64, n, t0, qr3[0:64, h, 0:n], [qr[bi % 2]])
                nkt = 2 if bi == 0 else NT
                for h in range(4):
                    def pairs(kt, h=h, n=n, qn3=qn3, qr3=qr3):
                        return [(kn3[:, h, kt * 128:(kt + 1) * 128], qn3[:, h, 0:n]),
                                (krT.ap[0:64, kt * 128:(kt + 1) * 128], qr3[0:64, h, 0:n])]

                    def vfn(kt, h=h):
                        return vD4[:, kt, h, :]

                    def yfn(qs, h=h):
                        return y3[:, qs, h * 128:(h + 1) * 128]
                    attention(pairs, vfn, nkt, n, 128, float(192 ** -0.5), yfn, [knT, krT, qn[bi % 2], qr[bi % 2], vD], ytok)
                store_yT(ytok, ytok, n, t0, 3)
            AR.release(m_)

        def vorder(d):
            return list(range(36)) if d == 0 else [3, 2, 1, 0] + list(range(35, 3, -1))

        def mixer_A(l, s):
            for hd in range(4):
                m_ = AR.mark()
                qs_ = AR.alloc(T, F32, "qs")
                sg = AR.alloc(T, BF16, "sg")
                lf = AR.alloc(T, F32, "lf")
                kk = AR.alloc(T, F32, "kk")
                vtok = AR.alloc(NT * 128, BF16, "vtok")
                v3 = vtok.ap[:].rearrange("p (t v) -> p t v", t=NT)
                oacc = AR.alloc(T, F32, "oacc")
                Fc = AR.alloc(T, F32, "Fc")
                tmp = AR.alloc(T, F32, "tmpA")
                eq = AR.alloc(T, F32, "eq")
                qt = AR.alloc(T, BF16, "qt")
                ktT = AR.alloc(T, BF16, "ktT")
                ktok = AR.alloc(NT * 128, BF16, "ktok")
                k3 = ktok.ap[:].rearrange("p (t v) -> p t v", t=NT)
                Sst = AR.alloc(128, F32, "Sst")
                Sb = [AR.alloc(128, BF16, "Sb%d" % i) for i in range(2)]
                tS = AR.alloc(128, F32, "tS")
                ATs = [[AR.alloc(64, BF16, "AT%d_%d" % (dd, i)) for i in range(3)] for dd in range(2)]
                for dd in range(2):
                    for i in range(3):
                        memset("vector", ATs[dd][i].ap[:], 0.0, [ATs[dd][i]])
                qtX = AR.alloc(T, BF16, "qtX")
                ktX = AR.alloc(T, BF16, "ktX")
                sm = AR.alloc(6 * 36, F32, "smA")
                sm3 = sm.ap[:].rearrange("p (a c) -> p a c", a=6)
                wv = load_w(win_b, l * NCH_IN + CH_AI + hd)
                wq = load_w(win_b, l * NCH_IN + CH_AQ + hd)
                wg = load_w(win_b, l * NCH_IN + CH_AG + hd)
                for bi, (t0, n) in enumerate(BLKS):
                    ps = proj_fm(wq, t0, n, bi)
                    act(qs_.ap[:, t0:t0 + n], ps.ap[:, 0:n], AF.Silu, [ps], [qs_])
                    ps = proj_fm(wg, t0, n, bi)
                    act(sg.ap[:, t0:t0 + n], ps.ap[:, 0:n], AF.Silu, [ps], [sg])
                for tl in range(NT):
                    ps = bankA()
                    proj_tm(wv, tl, ps, 0)
                    cp("vector", v3[:, tl, :], ps.ap[:, 0:128], [ps], [vtok])
                for d in range(2):
                    wf = load_w(win_b, l * NCH_IN + (CH_AFF if d == 0 else CH_AFB) + hd)
                    j = d * 4 + hd
                    lbv = lb3[:, 0, l, j:j + 1]
                    omlb = lb3[:, 1, l, j:j + 1]
                    nomlb = lb3[:, 2, l, j:j + 1]
                    for bi, (t0, n) in enumerate(BLKS):
                        ps = proj_fm(wf, t0, n, bi)
                        sig = nxt(tmpF, "F")
                        act(sig.ap[:, 0:n], ps.ap[:, 0:n], AF.Sigmoid, [ps], [sig])
                        act(lf.ap[:, t0:t0 + n], sig.ap[:, 0:n], AF.Ln, [sig, lbt], [lf], scale=omlb, bias=lbv)
                        act(kk.ap[:, t0:t0 + n], sig.ap[:, 0:n], AF.Identity, [sig, lbt], [kk], scale=nomlb, bias=omlb)
                    scan(Fc.ap[:], lf.ap[:], zcol.to_broadcast([128, T]), 0.0, ADD, ADD, [lf, smallc], [Fc])
                    if d == 0:
                        vc = Fc
                    else:
                        tt("vector", Fc.ap[:], lf.ap[:], Fc.ap[:], SUB, [lf, Fc], [Fc])
                        vc = Fc
                    vc3 = vc.ap[:].rearrange("p (c t) -> p c t", t=64)
                    lf3 = lf.ap[:].rearrange("p (c t) -> p c t", t=64)
                    fv = 0 if d == 0 else 63
                    lv = 63 if d == 0 else 0
                    tt("vector", sm3[:, 0, :], vc3[:, :, fv], lf3[:, :, fv], SUB, [vc, lf], [sm])
                    tt("vector", sm3[:, 1, :], vc3[:, :, 32], sm3[:, 0, :], SUB, [vc, sm], [sm])
                    tt("vector", sm3[:, 2, :], vc3[:, :, lv], vc3[:, :, 32], SUB, [vc], [sm])
                    tt("vector", sm3[:, 3, :], vc3[:, :, lv], sm3[:, 0, :], SUB, [vc, sm], [sm])
                    act(sm3[:, 1:4, :], sm3[:, 1:4, :], AF.Exp, [sm], [sm])
                    tmp3 = tmp.ap[:].rearrange("p (c t) -> p c t", t=64)
                    tt("vector", tmp3, vc3, vc3[:, :, 32:33].to_broadcast([128, 36, 64]), SUB, [vc], [tmp])
                    act(eq.ap[:], tmp.ap[:], AF.Exp, [tmp], [eq])
                    act(tmp.ap[:], tmp.ap[:], AF.Exp, [tmp], [tmp], scale=-1.0)
                    tt("vector", qt.ap[:], qs_.ap[:], eq.ap[:], MUL, [qs_, eq], [qt])
                    tt("gpsimd", ktT.ap[:], kk.ap[:], tmp.ap[:], MUL, [kk, tmp], [ktT])
                    tmp32 = tmp.ap[:].rearrange("p (c t) -> p c t", t=32)
                    vc32 = vc.ap[:].rearrange("p (c t) -> p c t", t=32)
                    tt("vector", tmp32, vc32, vc32[:, :, 16:17].to_broadcast([128, 72, 32]), SUB, [vc], [tmp])
                    act(eq.ap[:], tmp.ap[:], AF.Exp, [tmp], [eq])
                    act(tmp.ap[:], tmp.ap[:], AF.Exp, [tmp], [tmp], scale=-1.0)
                    tt("vector", qtX.ap[:], qs_.ap[:], eq.ap[:], MUL, [qs_, eq], [qtX])
                    tt("gpsimd", ktX.ap[:], kk.ap[:], tmp.ap[:], MUL, [kk, tmp], [ktX])
                    for tl in range(NT):
                        ps = bankA()
                        pb = ps.ap[:].bitcast(BF16)
                        tp(pb[:, 0:128], ktT.ap[:, tl * 128:(tl + 1) * 128], ident_b, [ktT, cbf], [ps])
                        cp("scalar" if tl % 2 else "vector", k3[:, tl, :], pb[:, 0:128], [ps], [ktok])
                    memset("vector", Sst.ap[:], 0.0, [Sst])
                    memset("vector", Sb[0].ap[:], 0.0, [Sb[0]])
                    order = vorder(d)
                    mk_ = masks[:, d * 64:(d + 1) * 64]

                    def a_stage(ci_, c, d=d, mk_=mk_):
                        b = (c % 2) * 64
                        tk0 = c * 64
                        psA = bankS()
                        AT = ATs[d][ci_ % 3]
                        if d == 0:
                            mm(psA.ap[b:b + 64, 0:32], ktX.ap[:, tk0:tk0 + 64], qtX.ap[:, tk0:tk0 + 32], True, True, [ktX, qtX], [psA])
                            mm(psA.ap[b:b + 64, 32:64], ktX.ap[:, tk0:tk0 + 64], qtX.ap[:, tk0 + 32:tk0 + 64], True, True, [ktX, qtX], [psA])
                            mm(psA.ap[b:b + 32, 32:64], ktT.ap[:, tk0:tk0 + 32], qt.ap[:, tk0 + 32:tk0 + 64], True, True, [ktT, qt], [psA])
                        else:
                            mm(psA.ap[b:b + 64, 0:32], ktT.ap[:, tk0:tk0 + 64], qt.ap[:, tk0:tk0 + 32], True, True, [ktT, qt], [psA])
                            mm(psA.ap[b:b + 32, 0:32], ktX.ap[:, tk0:tk0 + 32], qtX.ap[:, tk0:tk0 + 32], True, True, [ktX, qtX], [psA])
                            mm(psA.ap[b:b + 64, 32:64], ktX.ap[:, tk0:tk0 + 64], qtX.ap[:, tk0 + 32:tk0 + 64], True, True, [ktX, qtX], [psA])
                        tt("vector", AT.ap[b:b + 64, :], psA.ap[b:b + 64, 0:64], mk_[b:b + 64, :], MUL, [psA, cst], [AT])
                        return AT
                    AT_next = a_stage(0, order[0])
                    for ci_, c in enumerate(order):
                        tl = c // 2
                        b = (c % 2) * 64
                        tk0 = c * 64
                        sbc = Sb[ci_ % 2]
                        AT = AT_next
                        psD = None
                        if ci_ < len(order) - 1:
                            psD = bankS()
                            mm(psD.ap[:, 0:128], k3[b:b + 64, tl, :], v3[b:b + 64, tl, :], True, True, [ktok, vtok], [psD])
                            AT_next = a_stage(ci_ + 1, order[ci_ + 1])
                        psO = bankS()
                        mm(psO.ap[:, 0:64], sbc.ap[:], qt.ap[:, tk0:tk0 + 64], True, False, [sbc, qt], [psO])
                        mm(psO.ap[:, 0:64], v3[b:b + 64, tl, :], AT.ap[b:b + 64, :], False, True, [vtok, AT], [psO])
                        if d == 0:
                            cp("scalar", oacc.ap[:, tk0:tk0 + 64], psO.ap[:, 0:64], [psO], [oacc])
                        else:
                            tt("vector", oacc.ap[:, tk0:tk0 + 64], oacc.ap[:, tk0:tk0 + 64], psO.ap[:, 0:64], ADD, [psO, oacc], [oacc])
                        if psD is not None:
                            act(tS.ap[:], psD.ap[:, 0:128], AF.Copy, [psD, sm], [tS], scale=sm3[:, 2, c:c + 1])
                            stt(Sst.ap[:], Sst.ap[:], sm3[:, 3, c:c + 1], tS.ap[:], MUL, ADD, [Sst, tS, sm], [Sst])
                            cn = order[ci_ + 1]
                            act(Sb[(ci_ + 1) % 2].ap[:], Sst.ap[:], AF.Copy, [Sst, sm], [Sb[(ci_ + 1) % 2]], scale=sm3[:, 1, cn:cn + 1])
                for bi, (t0, n) in enumerate(BLKS):
                    sq = nxt(tmpB, "B")
                    act(sq.ap[:, 0:n], oacc.ap[:, t0:t0 + n], AF.Square, [oacc], [sq])
                    ps = bankA()
                    mm(ps.ap[:, 0:n], ones_b, sq.ap[:, 0:n], True, True, [sq, cbf], [ps])
                    rs = nxt(tmpR, "R")
                    act(rs.ap[:, 0:n], ps.ap[:, 0:n], AF.Sqrt, [ps, smallc], [rs], scale=1.0 / 128, bias=smallc.ap[:, 3:4])
                    recip(rs.ap[:, 0:n], rs.ap[:, 0:n], [rs], [rs])
                    t1 = nxt(tmpF, "F")
                    stt(t1.ap[:, 0:n], oacc.ap[:, t0:t0 + n], pvc("hng", l), rs.ap[:, 0:n], MUL, MUL, [oacc, rs, pv], [t1])
                    yb = nxt(tmpB, "B")
                    tt("vector", yb.ap[:, 0:n], t1.ap[:, 0:n], sg.ap[:, t0:t0 + n], MUL, [t1, sg], [yb])
                    dma("gpsimd", yT_d[0 * 4 + hd][:, t0:t0 + n], yb.ap[:, 0:n], [yb], [yres_d])
                AR.release(m_)

        def mixer_B(l, s):
            m_ = AR.mark()
            tokscal = AR.alloc(NT * 24, F32, "tokscal")
            tsc3 = tokscal.ap[:].rearrange("p (t r) -> p t r", t=NT)
            dB = AR.alloc(8 * 36, F32, "dB")
            dB3 = dB.ap[:].rearrange("p (r c) -> p r c", r=8)
            m1 = AR.mark()
            A0 = AR.alloc(T, F32, "A0")
            A1 = AR.alloc(T, F32, "A1")
            A2 = AR.alloc(T, F32, "A2")
            A3 = AR.alloc(T, F32, "A3")
            Rr = AR.alloc(36, F32, "Rr")
            Rb = AR.alloc(8 * 36, F32, "Rb")
            Rb3 = Rb.ap[:].rearrange("p (r c) -> p r c", r=8)
            sel = pvc("sel", 0)
            selm1 = pvc("sel", 1)
            o_mgb, _ = po["mgb"]
            wig = load_w(win_b, l * NCH_IN + CH_BIG)
            wlf = load_w(win_b, l * NCH_IN + CH_BLF)
            P8 = slice(0, 8)
            for bi, (t0, n) in enumerate(BLKS):
                ps = proj_fm(wig, t0, n, bi, width=8)
                act(A0.ap[P8, t0:t0 + n], ps.ap[P8, 0:n], AF.Identity, [ps, pv], [A0], bias=pv.ap[P8, o_mgb + l * 2:o_mgb + l * 2 + 1])
                ps = proj_fm(wlf, t0, n, bi, width=8)
                t_ = nxt(tmpF, "F")
                ts("vector", t_.ap[P8, 0:n], ps.ap[P8, 0:n], pv.ap[P8, o_mgb + l * 2 + 1:o_mgb + l * 2 + 2], -1.0, ADD, MUL, [ps, pv], [t_])
                act(t_.ap[P8, 0:n], t_.ap[P8, 0:n], AF.Exp, [t_], [t_])
                act(A1.ap[P8, t0:t0 + n], t_.ap[P8, 0:n], AF.Ln, [t_, smallc], [A1], bias=smallc.ap[P8, 2:3])
            ts("vector", A1.ap[P8, :], A1.ap[P8, :], -1.0, None, MUL, None, [A1], [A1])
            scan(A2.ap[P8, :], A1.ap[P8, :], zcol[P8, :].to_broadcast([8, T]), 0.0, ADD, ADD, [A1, smallc], [A2])
            Kt = nxt(small, "s")
            ts("vector", Kt.ap[P8, 0:1], A2.ap[P8, T - 1:T], selm1[P8, :], -1.0, MUL, MUL, [A2, pv], [Kt])
            tt("vector", A1.ap[P8, :], A2.ap[P8, :], A1.ap[P8, :], SUB, [A1, A2], [A1])
            ts("vector", A3.ap[P8, :], A1.ap[P8, :], selm1[P8, :], None, MUL, None, [A1, pv], [A3])
            stt(A2.ap[P8, :], A2.ap[P8, :], sel[P8, :], A3.ap[P8, :], MUL, ADD, [A2, A3, pv], [A2])
            ts("vector", A2.ap[P8, NCTX:T], A2.ap[P8, NCTX:T], Kt.ap[P8, 0:1], None, ADD, None, [A2, Kt], [A2])
            tt("vector", A0.ap[P8, :], A0.ap[P8, :], A2.ap[P8, :], SUB, [A0, A2], [A0])
            scan(A1.ap[P8, :], A0.ap[P8, :], A0.ap[P8, :], -1e30, MAX, MAX, [A0], [A1])
            scan(A3.ap[P8, 0:NCTX][:, ::-1], A0.ap[P8, 0:NCTX][:, ::-1], A0.ap[P8, 0:NCTX][:, ::-1], -1e30, MAX, MAX, [A0], [A3])
            scan(A3.ap[P8, NCTX:T][:, ::-1], A0.ap[P8, NCTX:T][:, ::-1], A0.ap[P8, NCTX:T][:, ::-1], A3.ap[P8, 0:1], MAX, MAX, [A0, A3], [A3])
            ts("vector", A3.ap[P8, :], A3.ap[P8, :], selm1[P8, :], -1.0, MUL, MUL, [A3, pv], [A3])
            stt(A1.ap[P8, :], A1.ap[P8, :], sel[P8, :], A3.ap[P8, :], MUL, ADD, [A1, A3, pv], [A1])
            G3 = A1.ap[P8, :].rearrange("p (c t) -> p c t", t=64)
            ts("vector", Rr.ap[P8, :], G3[:, :, 63], selm1[P8, :], -1.0, MUL, MUL, [A1, pv], [Rr])
            stt(Rr.ap[P8, :], G3[:, :, 0], sel[P8, :], Rr.ap[P8, :], MUL, ADD, [A1, Rr, pv], [Rr])
            Rbc = Rr.ap[P8, :].unsqueeze(2).to_broadcast([8, 36, 64])
            a3 = A0.ap[P8, :].rearrange("p (c t) -> p c t", t=64)
            A33 = A3.ap[P8, :].rearrange("p (c t) -> p c t", t=64)
            tt("vector", A33, a3, Rbc, SUB, [A0, Rr], [A3])
            act(A3.ap[P8, :], A3.ap[P8, :], AF.Exp, [A3, smallc], [A3], bias=smallc.ap[P8, 1:2])
            tt("vector", a3, G3, Rbc, SUB, [A1, Rr], [A0])
            act(A0.ap[P8, :], A0.ap[P8, :], AF.Exp, [A0], [A0], scale=-1.0)
            tt("vector", A2.ap[P8, :], A2.ap[P8, :], A1.ap[P8, :], ADD, [A1, A2], [A2])
            act(A2.ap[P8, :], A2.ap[P8, :], AF.Exp, [A2], [A2], scale=-1.0)
            for tl in range(NT):
                ps = bankA()
                for qi, Aq in enumerate((A3, A0, A2)):
                    tp(ps.ap[:, qi * 8:(qi + 1) * 8], Aq.ap[P8, tl * 128:(tl + 1) * 128], ident_f[0:8, 0:8], [Aq, cst], [ps])
                cp("vector", tsc3[:, tl, :], ps.ap[:, 0:24], [ps], [tokscal])
            Df = AR.alloc(36, F32, "Df")
            Db = AR.alloc(36, F32, "Db")
            memset("vector", Df.ap[P8, :], 0.0, [Df])
            tt("vector", Df.ap[P8, 0:35], Rr.ap[P8, 0:35], Rr.ap[P8, 1:36], SUB, [Rr], [Df])
            tt("vector", Db.ap[P8, 1:36], Rr.ap[P8, 1:36], Rr.ap[P8, 0:35], SUB, [Rr], [Db])
            tt("vector", Db.ap[P8, 0:1], Rr.ap[P8, 0:1], Rr.ap[P8, 35:36], SUB, [Rr], [Db])
            memset("vector", Db.ap[P8, 4:5], 0.0, [Db])
            ts("vector", Db.ap[P8, :], Db.ap[P8, :], selm1[P8, :], -1.0, MUL, MUL, [Db, pv], [Db])
            stt(Df.ap[P8, :], Df.ap[P8, :], sel[P8, :], Db.ap[P8, :], MUL, ADD, [Df, Db, pv], [Df])
            act(Df.ap[P8, :], Df.ap[P8, :], AF.Exp, [Df], [Df])
            ps = bankA()
            for r in range(8):
                mm(ps.ap[:, r * 36:(r + 1) * 36], onehot[0:8, r * 128:(r + 1) * 128], Df.ap[P8, :], True, True, [Df, cst], [ps])
            cp("vector", dB.ap[:, :], ps.ap[:, 0:288], [ps], [dB])
            if dbg:
                dma("gpsimd", tsc_dbg[:, :], tokscal.ap[:], [tokscal], [Res()])
                dma("gpsimd", dB_dbg[:, :], dB.ap[:], [dB], [Res()])
            AR.release(m1)
            qTb = AR.alloc(2 * T, BF16, "qTb")
            kTb = AR.alloc(2 * T, BF16, "kTb")
            q3 = qTb.ap[:].rearrange("p (c t) -> p c t", c=2)
            k3 = kTb.ap[:].rearrange("p (c t) -> p c t", c=2)
            vtok = AR.alloc(NT * 4 * 129, BF16, "vtokB")
            v4 = vtok.ap[:].rearrange("p (t h d) -> p t h d", t=NT, h=4)
            kttok = AR.alloc(NT * 2 * 256, BF16, "kttok")
            kt5 = kttok.ap[:].rearrange("p (t d h k) -> p t d h k", t=NT, d=2, h=4)
            memset("gpsimd", vtok.ap[:], 1.0, [vtok])
            m2 = AR.mark()
            pre = AR.alloc(2 * T, F32, "pre")
            pre3 = pre.ap[:].rearrange("p (c t) -> p c t", c=2)
            cv = AR.alloc(2 * T, F32, "cv")
            cv3 = cv.ap[:].rearrange("p (c t) -> p c t", c=2)
            o_cw, _ = po["convw"]
            for qk, (chb, dst, dst3) in enumerate(((CH_BQ, qTb, q3), (CH_BK, kTb, k3))):
                ws = [load_w(win_b, l * NCH_IN + chb + c) for c in range(2)]
                for c in range(2):
                    for bi, (t0, n) in enumerate(BLKS):
                        ps = proj_fm(ws[c], t0, n, bi)
                        cp("scalar" if bi % 2 else "vector", pre3[:, c, t0:t0 + n], ps.ap[:, 0:n], [ps], [pre])
                    def cw(j, c=c, qk=qk):
                        i_ = o_cw + ((l * 2 + qk) * 2 + c) * 3 + j
                        return pv.ap[:, i_:i_ + 1]
                    ts("vector", cv3[:, c, :], pre3[:, c, :], cw(1), None, MUL, None, [pre, pv], [cv])
                    for (a, b_) in ((0, NCTX), (NCTX, T)):
                        stt(cv3[:, c, a + 1:b_], pre3[:, c, a:b_ - 1], cw(0), cv3[:, c, a + 1:b_], MUL, ADD, [pre, cv, pv], [cv])
                        stt(cv3[:, c, a:b_ - 1], pre3[:, c, a + 1:b_], cw(2), cv3[:, c, a:b_ - 1], MUL, ADD, [pre, cv, pv], [cv])
                    act(dst3[:, c, :], cv3[:, c, :], AF.Silu, [cv], [dst])
            AR.release(m2)
            wvs = [load_w(win_b, l * NCH_IN + CH_BV + h) for h in range(4)]
            for tl in range(NT):
                ps = bankA()
                for h in range(4):
                    proj_tm(wvs[h], tl, ps, h * 128)
                cp("vector" if tl % 2 else "scalar", v4[:, tl, :, 0:128], ps.ap[:, 0:512].rearrange("p (h d) -> p h d", h=4), [ps], [vtok])
                psk = bankA()
                pkb = psk.ap[:].bitcast(BF16)
                for c in range(2):
                    tp(pkb[:, c * 128:(c + 1) * 128], k3[:, c, tl * 128:(tl + 1) * 128], ident_b, [kTb, cbf], [psk])
                for d in range(2):
                    tt("vector", kt5[:, tl, d, :, :], pkb[:, 0:256].rearrange("p (h k) -> p h k", h=4),
                       tsc3[:, tl, d * 4:(d + 1) * 4].unsqueeze(2).to_broadcast([128, 4, 64]), MUL, [psk, tokscal], [kttok])
            nd = AR.alloc(NT * 129, F32, "nd")
            nd3 = nd.ap[:].rearrange("p (t d) -> p t d", t=NT)
            hh = AR.alloc(NT * 128, F32, "hh")
            hh3 = hh.ap[:].rearrange("p (t d) -> p t d", t=NT)
            sgo = AR.alloc(NT * 128, BF16, "sgo")
            sgo3 = sgo.ap[:].rearrange("p (t d) -> p t d", t=NT)
            ybt = AR.alloc(NT * 128, BF16, "ybt")
            ybt3 = ybt.ap[:].rearrange("p (t d) -> p t d", t=NT)
            Cst = AR.alloc(129, F32, "Cst")
            Cb = [AR.alloc(129, BF16, "Cb%d" % i) for i in range(2)]
            STs = [AR.alloc(64, BF16, "ST%d" % i) for i in range(3)]
            sn = AR.alloc(4 * NT, F32, "sn")
            sn3 = sn.ap[:].rearrange("p (a t) -> p a t", a=4)
            o_mng, _ = po["mng"]
            for h in range(4):
                hb = (h % 2) * 64
                ch = h // 2
                wo = load_w(win_b, l * NCH_IN + CH_BO + h)
                for tl in range(NT):
                    ps = bankA()
                    proj_tm(wo, tl, ps, 0)
                    act(sgo3[:, tl, :], ps.ap[:, 0:128], AF.Sigmoid, [ps], [sgo])
                for d in range(2):
                    r = d * 4 + h
                    memset("vector", Cst.ap[hb:hb + 64, :], 0.0, [Cst])
                    memset("vector", Cb[0].ap[hb:hb + 64, :], 0.0, [Cb[0]])
                    order = vorder(d)

                    def s_stage(ci_, c, d=d, r=r, hb=hb, ch=ch):
                        tl = c // 2
                        b = (c % 2) * 64
                        tk0 = c * 64
                        psS = bankS()
                        mm(psS.ap[b:b + 64, 0:64], k3[hb:hb + 64, ch, tk0:tk0 + 64], q3[hb:hb + 64, ch, tk0:tk0 + 64], True, True, [kTb, qTb], [psS])
                        ST = STs[ci_ % 3]
                        stt(ST.ap[b:b + 64, :], psS.ap[b:b + 64, 0:64], tsc3[b:b + 64, tl, r:r + 1], masks[b:b + 64, d * 64:(d + 1) * 64], MUL, MUL,
                            [psS, tokscal, cst], [ST])
                        return ST
                    ST_next = s_stage(0, order[0])
                    for ci_, c in enumerate(order):
                        tl = c // 2
                        b = (c % 2) * 64
                        tk0 = c * 64
                        cbc = Cb[ci_ % 2]
                        ST = ST_next
                        psD = None
                        if ci_ < len(order) - 1:
                            psD = bankS()
                            mm(psD.ap[hb:hb + 64, 0:129], kt5[b:b + 64, tl, d, h, :], v4[b:b + 64, tl, h, :], True, True, [kttok, vtok], [psD])
                            ST_next = s_stage(ci_ + 1, order[ci_ + 1])
                        psN = bankS()
                        mm(psN.ap[b:b + 64, 0:129], q3[hb:hb + 64, ch, tk0:tk0 + 64], cbc.ap[hb:hb + 64, :], True, False, [qTb, cbc], [psN])
                        mm(psN.ap[b:b + 64, 0:129], ST.ap[b:b + 64, :], v4[b:b + 64, tl, h, :], False, True, [ST, vtok], [psN])
                        cp("scalar", nd3[b:b + 64, tl, :], psN.ap[b:b + 64, 0:129], [psN], [nd])
                        if psD is not None:
                            dcol = dB3[hb:hb + 64, r, c:c + 1]
                            ts("vector", Cst.ap[hb:hb + 64, :], Cst.ap[hb:hb + 64, :], dcol, None, MUL, None, [Cst, dB], [Cst])
                            stt(Cst.ap[hb:hb + 64, :], psD.ap[hb:hb + 64, 0:129], dcol, Cst.ap[hb:hb + 64, :], MUL, ADD, [psD, Cst, dB], [Cst])
                            cp("scalar", Cb[(ci_ + 1) % 2].ap[hb:hb + 64, :], Cst.ap[hb:hb + 64, :], [Cst], [Cb[(ci_ + 1) % 2]])
                    wv_ = tsc3[:, :, 8 + r]
                    ev_ = tsc3[:, :, 16 + r]
                    tt("vector", sn3[:, 0, :], nd3[:, :, 128], wv_, MUL, [nd, tokscal], [sn])
                    act(sn3[:, 0, :], sn3[:, 0, :], AF.Abs, [sn], [sn])
                    tt("vector", sn3[:, 0, :], sn3[:, 0, :], ev_, MAX, [sn, tokscal], [sn])
                    recip(sn3[:, 1, :], sn3[:, 0, :], [sn], [sn])
                    tt("vector", sn3[:, 1, :], sn3[:, 1, :], wv_, MUL, [sn, tokscal], [sn])
                    rwb = sn3[:, 1, :].unsqueeze(2).to_broadcast([128, NT, 128])
                    if d == 0:
                        tt("vector", hh3, nd3[:, :, 0:128], rwb, MUL, [nd, sn], [hh])
                    else:
                        tmpn = nd3[:, :, 0:128]
                        tt("vector", tmpn, tmpn, rwb, MUL, [nd, sn], [nd])
                        tt("gpsimd", hh3, hh3, tmpn, ADD, [nd, hh], [hh])
                sqn = nd3[:, :, 0:128]
                tt("vector", sqn, hh3, hh3, MUL, [hh], [nd])
                S.op("vector", (lambda o_, i_: (lambda e: e.tensor_reduce(out=o_, in_=i_, axis=mybir.AxisListType.X, op=ADD)))(sn3[:, 2, :], sqn), R_([nd]), R_([sn]))
                act(sn3[:, 2, :], sn3[:, 2, :], AF.Sqrt, [sn, smallc], [sn], scale=1.0 / 128, bias=smallc.ap[:, 3:4])
                recip(sn3[:, 2, :], sn3[:, 2, :], [sn], [sn])
                tt("vector", hh3, hh3, sn3[:, 2, :].unsqueeze(2).to_broadcast([128, NT, 128]), MUL, [hh, sn], [hh])
                tt("vector", hh3, hh3, pv.ap[:, o_mng + l * 128:o_mng + (l + 1) * 128].unsqueeze(1).to_broadcast([128, NT, 128]), MUL, [hh, pv], [hh])
                tt("vector", ybt3, hh3, sgo3, MUL, [hh, sgo], [ybt])
                for t0 in range(0, NT, 4):
                    nt_ = min(4, NT - t0)
                    ps = bankA()
                    pb = ps.ap[:].bitcast(BF16)
                    for i_ in range(nt_):
                        tp(pb[:, i_ * 128:(i_ + 1) * 128], ybt3[:, t0 + i_, :], ident_b, [ybt, cbf], [ps])
                    yt = nxt(tmpB, "B")
                    cp("scalar", yt.ap[:, 0:nt_ * 128], pb[:, 0:nt_ * 128], [ps], [yt])
                    dma("gpsimd", yT_d[1 * 4 + h][:, t0 * 128:(t0 + nt_) * 128], yt.ap[:, 0:nt_ * 128], [yt], [yres_d])
            AR.release(m_)

        def p34(l, s, sq, last):
            m_ = AR.mark()
            xb = AR.alloc(8 * 512, F32, "xb")
            xb3 = xb.ap[:].rearrange("p (k t) -> p k t", k=8)

            mT = AR.alloc(8 * 512, BF16, "mT")
            mT3 = mT.ap[:].rearrange("p (k t) -> p k t", k=8)
            acc = AR.alloc(512, F32, "acc")
            h2 = Buf(mT.ap, "h2")
            h2.res = mT.res
            h23 = h2.ap[:].rearrange("p (k t) -> p k t", k=8)
            uT = AR.alloc(32 * 512, BF16, "uT")
            uT3 = uT.ap[:].rearrange("p (j t) -> p j t", j=32)
            yb = Buf(uT.ap[:, 0:16 * 512], "ybl")
            yb.res = uT.res
            yb3 = yb.ap[:].rearrange("p (j t) -> p j t", j=16)
            ot = Buf(uT.ap[:, 0:16 * 512].bitcast(F32), "ot")
            ot.res = uT.res
            ot3 = ot.ap[:].rearrange("p (k t) -> p k t", k=8)
            wg4 = [AR.alloc(4 * 1024, BF16, "wg4_%d" % i) for i in range(2)]
            wb4 = [AR.alloc(2048, BF16, "wb4_%d" % i) for i in range(2)]
            w2s = [AR.alloc(4096, BF16, "w2s_%d" % i) for i in range(2)]
            xsrc = xT_in[sq] if l == 0 else xT_d
            xsrc3 = xsrc.rearrange("p (k t) -> p k t", k=8)
            xdst3 = xT_d.rearrange("p (k t) -> p k t", k=8)
            cnt = 0
            for bi, (t0, n) in enumerate(BLKS):
                sc = 4 if t0 < NCTX else s
                dma("sync", xb3[:, :, 0:n], xsrc3[:, :, t0:t0 + n], [xres[bi]], [xb])
                dma("sync", yb3[:, :, 0:n], yT_d[:, :, t0:t0 + n].rearrange("j p t -> p j t"), [yres_d], [yb])
                for i in range(8):
                    wg = wg4[cnt % 2]
                    wb = wb4[cnt % 2]
                    cnt += 1
                    j0 = l * NCH_IN + CH_GATES + i * 4
                    dma("sync", wg.ap[:].rearrange("p (g w) -> p g w", g=4), win_b[j0:j0 + 4].rearrange("g p w -> p g w"), [wres], [wg])
                    dma("sync", wb.ap[:], wbr_b[l * 8 + i], [wres], [wb])
                    wg3 = wg.ap[:].rearrange("p (g k c) -> p g k c", g=4, k=8)
                    wb3 = wb.ap[:].rearrange("p (r k c) -> p r k c", r=4, k=4)
                    for r in range(4):
                        psg = bankA()
                        for kc in range(8):
                            mm(psg.ap[:, 0:n], wg3[:, r, kc, :], hT3[:, kc, t0:t0 + n], kc == 0, kc == 7, [wg, hres[bi]], [psg])
                        gsig = nxt(tmpF, "F")
                        act(gsig.ap[:, 0:n], psg.ap[:, 0:n], AF.Sigmoid, [psg], [gsig])
                        psz = bankA()
                        for kc in range(4):
                            mm(psz.ap[:, 0:n], wb3[:, r, kc, :], yb3[:, r * 4 + kc, 0:n], kc == 0, kc == 3, [wb, yb], [psz])
                        if r == 0:
                            tt("vector", acc.ap[:, 0:n], psz.ap[:, 0:n], gsig.ap[:, 0:n], MUL, [psz, gsig], [acc])
                        else:
                            t_ = nxt(tmpF, "F")
                            tt("vector", t_.ap[:, 0:n], psz.ap[:, 0:n], gsig.ap[:, 0:n], MUL, [psz, gsig], [t_])
                            if r < 3:
                                tt("gpsimd", acc.ap[:, 0:n], acc.ap[:, 0:n], t_.ap[:, 0:n], ADD, [acc, t_], [acc])
                            else:
                                tt("gpsimd", mT3[:, i, 0:n], acc.ap[:, 0:n], t_.ap[:, 0:n], ADD, [acc, t_], [mT])
                for i in range(8):
                    wo = load_w(wout_b, l * 8 + i)
                    wo3 = wo.ap[:].rearrange("p (k c) -> p k c", k=8)
                    ps = bankA()
                    for kc in range(8):
                        mm(ps.ap[:, 0:n], wo3[:, kc, :], mT3[:, kc, 0:n], kc == 0, kc == 7, [wo, mT], [ps])
                    stt(xb3[:, i, 0:n], ps.ap[:, 0:n], modc(l, 2, i, sc), xb3[:, i, 0:n], MUL, ADD, [ps, xb, modT], [xb])
                if dbg and l == NL - 1:
                    dma("gpsimd", xmid_dbg.rearrange("p (k t) -> p k t", k=8)[:, :, t0:t0 + n], xb3[:, :, 0:n], [xb], [Res()])
                norm_mod(xb3, n, l, 4, 3, sc, h23, 0, [xb], [h2])
                for j in range(32):
                    w1 = load_w(wff1_b, l * 32 + j)
                    w13 = w1.ap[:].rearrange("p (k c) -> p k c", k=8)
                    ps = bankA()
                    for kc in range(8):
                        mm(ps.ap[:, 0:n], w13[:, kc, :], h23[:, kc, 0:n], kc == 0, kc == 7, [w1, h2], [ps])
                    r_ = nxt(tmpF, "F")
                    act(r_.ap[:, 0:n], ps.ap[:, 0:n], AF.Relu, [ps], [r_])
                    tt("vector" if j % 2 else "gpsimd", uT3[:, j, 0:n], r_.ap[:, 0:n], r_.ap[:, 0:n], MUL, [r_], [uT])
                for i in range(8):
                    w2 = w2s[cnt % 2]
                    cnt += 1
                    dma("sync", w2.ap[:], wff2_b[l * 8 + i], [wres], [w2])
                    w23 = w2.ap[:].rearrange("p (k c) -> p k c", k=32)
                    ps = bankA()
                    for kc in range(32):
                        mm(ps.ap[:, 0:n], w23[:, kc, :], uT3[:, kc, 0:n], kc == 0, kc == 31, [w2, uT], [ps])
                    stt(xb3[:, i, 0:n], ps.ap[:, 0:n], modc(l, 5, i, sc), xb3[:, i, 0:n], MUL, ADD, [ps, xb, modT], [xb])
                if not last:
                    dma("gpsimd", xdst3[:, :, t0:t0 + n], xb3[:, :, 0:n], [xb], [xres[bi]])
                elif t0 >= NCTX:
                    ps = bankA()
                    for kc in range(8):
                        sq_ = nxt(tmpB, "B")
                        act(sq_.ap[:, 0:n], xb3[:, kc, 0:n], AF.Square, [xb], [sq_])
                        mm(ps.ap[:, 0:n], ones_b, sq_.ap[:, 0:n], kc == 0, kc == 7, [sq_, cbf], [ps])
                    rs = nxt(tmpR, "R")
                    act(rs.ap[:, 0:n], ps.ap[:, 0:n], AF.Sqrt, [ps, smallc], [rs], scale=1.0 / DM, bias=smallc.ap[:, 3:4])
                    recip(rs.ap[:, 0:n], rs.ap[:, 0:n], [rs], [rs])
                    for kc in range(8):
                        stt(ot3[:, kc, 0:n], xb3[:, kc, 0:n], pvc("gfin", kc), rs.ap[:, 0:n], MUL, MUL, [xb, rs, pv], [ot])
                    od = dma("gpsimd", outT[sq].rearrange("p (k t) -> p k t", k=8)[:, :, t0 - NCTX:t0 - NCTX + n], ot3[:, :, 0:n], [ot], [Res()])
                    out_deps.append(od)
            AR.release(m_)

        xres = [Res("x%d" % i) for i in range(len(BLKS))]
        out_deps = []
        xmid_dbg = dscr("xmid_dbg", [128, 8 * T], F32) if dbg else None
        tsc_dbg = dscr("tsc_dbg", [128, NT * 24], F32) if dbg else None
        dB_dbg = dscr("dB_dbg", [128, 288], F32) if dbg else None
        zt = None

        for sq in range(NSEQ):
            for l in range(NL):
                m_ = AR.mark()
                xbs = [AR.alloc(8 * 512, F32, "xb1_%d" % i) for i in range(2)]
                xsrc = xT_in[sq] if l == 0 else xT_d
                xsrc3 = xsrc.rearrange("p (k t) -> p k t", k=8)
                for bi, (t0, n) in enumerate(BLKS):
                    xb = xbs[bi % 2]
                    xb3 = xb.ap[:].rearrange("p (k t) -> p k t", k=8)
                    dma("sync", xb3[:, :, 0:n], xsrc3[:, :, t0:t0 + n], [xres[bi]], [xb])
                    sc = 4 if t0 < NCTX else sq
                    norm_mod(xb3, n, l, 1, 0, sc, hT3, t0, [xb], [hres[bi]])
                if dbg and l == NL - 1:
                    dma("gpsimd", hT_dbg[:, :], hT.ap[:], hres, [Res()])
                AR.release(m_)
                for nm, fn, br in (("A", mixer_A, 0), ("B", mixer_B, 1), ("C", mixer_C, 2), ("D", mixer_D, 3)):
                    if nm in stages:
                        fn(l, sq)
                    else:
                        z = nxt(tmpB, "B")
                        memset("vector", z.ap[:], 0.0, [z])
                        for c in range(4):
                            for (t0, n) in BLKS:
                                dma("gpsimd", yT_d[br * 4 + c][:, t0:t0 + n], z.ap[:, 0:n], [z], [yres_d])
                S.barrier()
                p34(l, sq, sq, l == NL - 1)
        S.barrier()
        S.ops["sync"].append(([(k, S.cnt[k]) for k in S.dma_pool["gpsimd"] + S.dma_pool["sync"] if S.cnt[k] > 0], None, None, 0))
        print("ops recorded:", S.nops, flush=True)
        with nc.Block() as block:
            S.replay(block)
    return nc


def make_inputs(inp, core, nseq, NL=4):
    x = inp["x"][core * nseq:(core + 1) * nseq]
    ctx = inp["ctx"][core * nseq:(core + 1) * nseq]
    xc = np.concatenate([ctx, x], axis=1)
    xT = np.ascontiguousarray(xc.reshape(nseq, T, 8, 128).transpose(0, 3, 2, 1)).reshape(nseq, 128, 8 * T)
    return {"xT": xT, "pv": host_pv(inp, core, nseq)}


_SHARED = {}


def shared_inputs(inp):
    cst, rope = host_consts()
    ich = in_chunks()
    win = np.concatenate([layout_w(inp["w_in"][l], ich, 8) for l in range(L)], axis=0)
    wff1 = np.concatenate([layout_w(inp["w_ff1"][l], simple_chunks(32), 8) for l in range(L)], axis=0)
    wout = np.concatenate([layout_w(inp["w_out"][l], simple_chunks(8), 8) for l in range(L)], axis=0)
    wbr = np.concatenate([layout_w(inp["w_branch"][l].reshape(2048, 1024), simple_chunks(8), 16) for l in range(L)], axis=0)
    wff2 = np.concatenate([layout_w(inp["w_ff2"][l], simple_chunks(8), 32) for l in range(L)], axis=0)
    return {"cst": cst, "rope": rope, "w_ada": np.ascontiguousarray(inp["w_ada"]), "win": win, "wff1": wff1, "wout": wout,
            "wbr": wbr, "wff2": wff2, "wuq": np.ascontiguousarray(inp["w_mla_uq"]), "wuk": np.ascontiguousarray(inp["w_mla_uk"]),
            "wuv": np.ascontiguousarray(inp["w_mla_uv"])}


def kernel(**inputs):
    inp = {k: np.asarray(v, dtype=np.float32) for k, v in inputs.items()}
    ncores = 8
    nseq = inp["x"].shape[0] // ncores
    sh = shared_inputs(inp)
    in_maps = []
    for c in range(ncores):
        m = dict(sh)
        m.update(make_inputs(inp, c, nseq))
        in_maps.append(m)
    nc = build(NSEQ=nseq, NL=L)
    res = run_bass_kernel_spmd(nc, in_maps, core_ids=list(range(ncores)))
    outs = []
    for c in range(ncores):
        o = np.asarray(res.results[c]["outT"]).reshape(nseq, 128, 8, NLAT)
        outs.append(o.transpose(0, 3, 2, 1).reshape(nseq, NLAT, DM))
    return np.ascontiguousarray(np.concatenate(outs, axis=0)).astype(np.float32)
```

```python
import numpy as np
from contextlib import ExitStack
import concourse.bass as bass
import concourse.mybir as mybir
from concourse.bass_utils import run_bass_kernel_spmd

F32 = mybir.dt.float32
BF16 = mybir.dt.bfloat16
AF = mybir.ActivationFunctionType
ALU = mybir.AluOpType

DM = 1024
L = 4
T = 2304
NCTX = 256
NLAT = 2048
NT = 18
EPS = 1e-6
BLKS = [(0, 256), (256, 512), (768, 512), (1280, 512), (1792, 512)]
NCH_IN = 76
CH_AI, CH_AFF, CH_AFB, CH_AQ, CH_AG = 0, 4, 8, 12, 16
CH_BK, CH_BQ, CH_BV, CH_BO, CH_BIG, CH_BLF = 20, 22, 24, 28, 32, 33
CH_CK, CH_CV, CH_CQ, CH_DCKV, CH_DKR, CH_DCQ, CH_GATES = 34, 35, 36, 40, 41, 42, 44
OFF = dict(a_i=0, a_f_fwd=512, a_f_bwd=1024, b_k=1536, b_v=1792, b_gates=2304, c_k=2320, c_v=2448, d_ckv=2576,
           d_krope=2704, a_q=2768, a_g=3280, b_q=3792, b_o=4048, c_q=4560, d_cq=5072, gates=5328)

ENGINES = ("tensor", "vector", "scalar", "gpsimd", "sync")


class Res:
    __slots__ = ("name", "w", "r")

    def __init__(self, name=""):
        self.name = name
        self.w = None
        self.r = {}


class Sched:
    def __init__(self, nc, stack, n_dma_sems=8):
        self.nc = nc
        self.ops = {e: [] for e in ENGINES}
        self.sems = {}
        self.cnt = {}
        self.known = {e: {} for e in ENGINES}
        for e in ENGINES:
            self.sems[e] = stack.enter_context(nc.semaphore("s_" + e))
            self.cnt[e] = 0
        self.dma_pool = {}
        self.dma_i = {}
        for q in ("sync", "gpsimd"):
            ks = []
            for i in range(n_dma_sems):
                k = "d_%s_%d" % (q, i)
                self.sems[k] = stack.enter_context(nc.semaphore(k))
                self.cnt[k] = 0
                ks.append(k)
            self.dma_pool[q] = ks
            self.dma_i[q] = 0
        self.nops = 0

    def _emit(self, eng, fn, reads, writes, semkey, inc, extra=()):
        waits = {}

        def need(dep, raw):
            if dep is None:
                return
            k, v = dep
            if k == eng and (eng == "tensor" or not raw):
                return
            if v > waits.get(k, 0):
                waits[k] = v

        for r in reads:
            need(r.w, True)
        for w in writes:
            need(w.w, False)
            for k, v in w.r.items():
                need((k, v), False)
        for d in extra:
            need(d, True)
        kn = self.known[eng]
        wl = []
        for k, v in waits.items():
            if kn.get(k, 0) >= v:
                continue
            kn[k] = v
            wl.append((k, v))
        self.cnt[semkey] += inc
        val = self.cnt[semkey]
        for r in reads:
            if r.r.get(semkey, 0) < val:
                r.r[semkey] = val
        for w in writes:
            w.w = (semkey, val)
            w.r = {}
        self.ops[eng].append((wl, fn, semkey, inc))
        self.nops += 1
        return (semkey, val)

    def op(self, eng, fn, reads=(), writes=()):
        return self._emit(eng, fn, reads, writes, eng, 1)

    def dma(self, q, out, in_, reads=(), writes=()):
        pool = self.dma_pool[q]
        k = pool[self.dma_i[q] % len(pool)]
        self.dma_i[q] += 1
        prev = ((k, self.cnt[k]),) if self.cnt[k] > 0 else ()
        return self._emit(q, lambda e: e.dma_start(out=out, in_=in_), reads, writes, k, 16, extra=prev)

    def barrier(self):
        allk = [(k, v) for k, v in self.cnt.items() if v > 0]
        for e in ENGINES:
            kn = self.known[e]
            wl = []
            for k, v in allk:
                if k == e or kn.get(k, 0) >= v:
                    continue
                kn[k] = v
                wl.append((k, v))
            if wl:
                self.ops[e].append((wl, None, None, 0))

    def replay(self, block):
        sems = self.sems

        def mk(engname):
            lst = self.ops[engname]

            def body(e):
                for wl, fn, semkey, inc in lst:
                    for k, v in wl:
                        e.wait_ge(sems[k], v)
                    if fn is not None:
                        fn(e).then_inc(sems[semkey], inc)
            return body

        block.tensor(mk("tensor"))
        block.vector(mk("vector"))
        block.scalar(mk("scalar"))
        block.gpsimd(mk("gpsimd"))
        block.sync(mk("sync"))


class Buf:
    __slots__ = ("ap", "res")

    def __init__(self, ap, name=""):
        self.ap = ap
        self.res = Res(name)


PV_SPEC = [("g1", L * 8), ("g2", L * 8), ("gfin", 8), ("bada", L * 48), ("lbl", L * 8), ("hng", L),
           ("convw", L * 12), ("mgb", L * 2), ("mng", L * 128), ("gqg", L), ("gkg", L), ("mqg", L * 2),
           ("mkvg", L), ("sel", 2), ("cT", 64)]


def pv_offsets():
    o = {}
    off = 0
    for n, w in PV_SPEC:
        o[n] = (off, w)
        off += w
    return o, off


CST_SPEC = [("ident", 128), ("ones", 128), ("blk64", 128), ("prot", 128), ("masks", 128), ("onehot", 1024)]


def cst_offsets():
    o = {}
    off = 0
    for n, w in CST_SPEC:
        o[n] = (off, w)
        off += w
    return o, off


def in_chunks():
    ch = []

    def grp(name, n):
        for i in range(n):
            ch.append([(OFF[name] + i * 128, 128, 0)])
    grp("a_i", 4); grp("a_f_fwd", 4); grp("a_f_bwd", 4); grp("a_q", 4); grp("a_g", 4)
    grp("b_k", 2); grp("b_q", 2); grp("b_v", 4); grp("b_o", 4)
    g = OFF["b_gates"]
    ch.append([(g + 0, 4, 0), (g + 8, 4, 4)])
    ch.append([(g + 4, 4, 0), (g + 12, 4, 4)])
    grp("c_k", 1); grp("c_v", 1)
    for c in range(4):
        ch.append([(OFF["c_q"] + c * 64, 64, 0), (OFF["c_q"] + (c + 4) * 64, 64, 64)])
    grp("d_ckv", 1)
    ch.append([(OFF["d_krope"], 64, 0)])
    grp("d_cq", 2)
    for i in range(8):
        for r in range(4):
            ch.append([(OFF["gates"] + r * 1024 + i * 128, 128, 0)])
    assert len(ch) == NCH_IN
    return ch


def layout_w(W, chunks, nkc):
    out = np.zeros((len(chunks), 128, nkc, 128), np.float32)
    Wr = W.reshape(nkc, 128, W.shape[1])
    for j, segs in enumerate(chunks):
        for (c0, w, pos) in segs:
            out[j, :, :, pos:pos + w] = Wr[:, :, c0:c0 + w].transpose(1, 0, 2)
    return out.reshape(len(chunks), 128, nkc * 128)


def simple_chunks(n):
    return [[(i * 128, 128, 0)] for i in range(n)]


def host_consts():
    co, nc_ = cst_offsets()
    c = np.zeros((128, nc_), np.float32)
    p = np.arange(128)
    c[:, co["ident"][0]:co["ident"][0] + 128] = np.eye(128, dtype=np.float32)
    c[:, co["ones"][0]:co["ones"][0] + 128] = 1.0
    blk = (p[:, None] // 64 == p[None, :] // 64).astype(np.float32)
    c[:, co["blk64"][0]:co["blk64"][0] + 128] = blk
    prot = np.zeros((128, 128), np.float32)
    for f in range(128):
        if f % 32 < 16:
            prot[f + 16, f] = -1.0
        else:
            prot[f - 16, f] = 1.0
    c[:, co["prot"][0]:co["prot"][0] + 128] = prot
    s = (p % 64)[:, None]
    t = np.arange(64)[None, :]
    m0 = co["masks"][0]
    c[:, m0:m0 + 64] = (s <= t).astype(np.float32)
    c[:, m0 + 64:m0 + 128] = (s >= t).astype(np.float32)
    o0 = co["onehot"][0]
    for r in range(8):
        c[r, o0 + r * 128:o0 + (r + 1) * 128] = 1.0
    inv = (10000.0 ** (-np.arange(16, dtype=np.float32) / 16)).astype(np.float32)
    tt = np.arange(NLAT)
    row = (tt // 64).astype(np.float32)
    col = (tt % 64).astype(np.float32)
    rope = np.zeros((128, 2 * NLAT), np.float32)
    for f in range(128):
        j = f % 64
        pos = row if j < 32 else col
        ang = (pos * inv[j % 16]).astype(np.float32)
        rope[f, :NLAT] = np.cos(ang)
        rope[f, NLAT:] = np.sin(ang)
    return c, rope


def host_pv(inp, core, nseq):
    po, npv = pv_offsets()
    pv = np.zeros((128, npv), np.float32)

    def put(name, arr):
        o, w = po[name]
        assert arr.shape == (128, w), (name, arr.shape, w)
        pv[:, o:o + w] = arr
    put("g1", inp["g_norm1"].reshape(L, 8, 128).transpose(2, 0, 1).reshape(128, L * 8))
    put("g2", inp["g_norm2"].reshape(L, 8, 128).transpose(2, 0, 1).reshape(128, L * 8))
    put("gfin", inp["g_final"].reshape(8, 128).T)
    put("bada", inp["b_ada"].reshape(L, 48, 128).transpose(2, 0, 1).reshape(128, L * 48))
    put("lbl", inp["hgrn_lb_logits"].reshape(L, 2, 4, 128).transpose(3, 0, 1, 2).reshape(128, L * 8))
    put("hng", inp["hgrn_norm_g"].T)
    put("convw", inp["mlstm_conv_w"].reshape(L, 2, 3, 2, 128).transpose(4, 0, 1, 3, 2).reshape(128, L * 12))
    mgb = np.zeros((128, L, 2), np.float32)
    b = inp["b_mlstm_gates"]
    mgb[0:4, :, 0] = b[:, 0:4].T
    mgb[4:8, :, 0] = b[:, 8:12].T
    mgb[0:4, :, 1] = b[:, 4:8].T
    mgb[4:8, :, 1] = b[:, 12:16].T
    put("mgb", mgb.reshape(128, L * 2))
    put("mng", np.broadcast_to(inp["mlstm_norm_g"].reshape(1, L * 128), (128, L * 128)))
    put("gqg", np.tile(inp["gqa_q_norm_g"].T, (2, 1)))
    put("gkg", np.tile(inp["gqa_k_norm_g"].T, (2, 1)))
    put("mqg", inp["mla_q_norm_g"].reshape(L, 2, 128).transpose(2, 0, 1).reshape(128, L * 2))
    put("mkvg", inp["mla_kv_norm_g"].T)
    sel = np.zeros((128, 2), np.float32)
    sel[0:4, 0] = 1.0
    sel[:, 1] = sel[:, 0] - 1.0
    put("sel", sel)
    cT = np.zeros((128, 8, 8), np.float32)
    cc = inp["c"][core * nseq:(core + 1) * nseq]
    cT[:, :, 0:nseq] = cc.reshape(nseq, 8, 128).transpose(2, 1, 0)
    cT[:, :, 4] = inp["c_ctx"].reshape(8, 128).T
    put("cT", cT.reshape(128, 64))
    return pv


def build(NSEQ=4, NL=4, dbg=False, stages="ABCD"):
    nc = bass.Bass("TRN2", target_bir_lowering=False)
    po, NPV = pv_offsets()
    co, NCST = cst_offsets()

    def din(name, shape, dt=F32):
        return nc.dram_tensor(name, list(shape), dt, kind="ExternalInput").ap()

    xT_in = din("xT", [NSEQ, 128, 8 * T])
    pv_in = din("pv", [128, NPV])
    cst_in = din("cst", [128, NCST])
    rope_in = din("rope", [128, 2 * NLAT])
    wada_in = din("w_ada", [L, DM, 6 * DM])
    win_in = din("win", [L * NCH_IN, 128, 1024])
    wff1_in = din("wff1", [L * 32, 128, 1024])
    wout_in = din("wout", [L * 8, 128, 1024])
    wbr_in = din("wbr", [L * 8, 128, 2048])
    wff2_in = din("wff2", [L * 8, 128, 4096])
    wuq_in = din("wuq", [L, 256, 768])
    wuk_in = din("wuk", [L, 128, 512])
    wuv_in = din("wuv", [L, 128, 512])
    outT = nc.dram_tensor("outT", [NSEQ, 128, 8 * NLAT], F32, kind="ExternalOutput").ap()

    def dscr(name, shape, dt):
        if dbg:
            return nc.dram_tensor(name, list(shape), dt, kind="ExternalOutput").ap()
        return nc.dram_tensor(name, list(shape), dt).ap()

    win_b = nc.dram_tensor("win_b", [L * NCH_IN, 128, 1024], BF16).ap()
    wff1_b = nc.dram_tensor("wff1_b", [L * 32, 128, 1024], BF16).ap()
    wout_b = nc.dram_tensor("wout_b", [L * 8, 128, 1024], BF16).ap()
    wbr_b = nc.dram_tensor("wbr_b", [L * 8, 128, 2048], BF16).ap()
    wff2_b = nc.dram_tensor("wff2_b", [L * 8, 128, 4096], BF16).ap()
    xT_d = nc.dram_tensor("xT_d", [128, 8 * T], F32).ap()
    yT_d = dscr("yT_d", [16, 128, T], BF16)
    hT_dbg = dscr("hT_dbg", [128, 8 * T], BF16) if dbg else None

    with ExitStack() as st:
        S = Sched(nc, st)

        def sb(name, shape, dt=F32):
            return Buf(st.enter_context(nc.sbuf_tensor(name, list(shape), dt)), name)

        hT = sb("hT", [128, 8 * T], BF16)
        hT3 = hT.ap[:].rearrange("p (k t) -> p k t", k=8)
        hres = [Res("h%d" % i) for i in range(len(BLKS))]
        cst = sb("cst_sb", [128, NCST])
        cbf = sb("cbf", [128, 512], BF16)
        ropeb = sb("ropeb", [128, 2 * NLAT], BF16)
        pv = sb("pv_sb", [128, NPV])
        modT = sb("modT", [128, L * 48 * 8])
        mod4 = modT.ap[:].rearrange("p (l j s) -> p l j s", l=L, j=48)
        lbt = sb("lbt", [128, 3 * L * 8])
        smallc = sb("smallc", [128, 8])
        wslots = [sb("wslot%d" % i, [128, 1024], BF16) for i in range(8)]
        tmpF = [sb("tmpF%d" % i, [128, 512]) for i in range(6)]
        tmpB = [sb("tmpB%d" % i, [128, 512], BF16) for i in range(6)]
        tmpR = [sb("tmpR%d" % i, [128, 512]) for i in range(3)]
        tmpE = [sb("tmpE%d" % i, [128, 512], BF16) for i in range(3)]
        small = [sb("small%d" % i, [128, 64]) for i in range(8)]
        ARENA_B = int(nc.sbuf_bytes_remaining) - 2048
        ARENA_B -= ARENA_B % 64
        arena = sb("arena", [128, ARENA_B // 2], BF16)
        banks = [Buf(st.enter_context(nc.psum_tensor("bank%d" % i, [128, 512], F32)), "bank%d" % i) for i in range(8)]

        rr = {"w": 0, "F": 0, "B": 0, "E": 0, "s": 0, "A": 0, "R": 0}

        def nxt(lst, key):
            b = lst[rr[key] % len(lst)]
            rr[key] += 1
            return b

        def bankA():
            return nxt(banks[0:4], "A")
        bankO = banks[4:8]
        rr["S"] = 0

        def bankS():
            return nxt(banks, "S")

        class Arena:
            def __init__(self):
                self.off = 0

            def mark(self):
                return self.off

            def release(self, m):
                S.barrier()
                self.off = m

            def alloc(self, nelem, dt, name="a"):
                nb = nelem * (4 if dt == F32 else 2)
                nb = (nb + 63) // 64 * 64
                assert self.off + nb <= ARENA_B, ("arena overflow", name, self.off, nb, ARENA_B)
                v = arena.ap[:, self.off // 2:(self.off + nb) // 2]
                self.off += nb
                if dt == F32:
                    v = v.bitcast(F32)
                return Buf(v[:, 0:nelem], name)
        AR = Arena()

        def pvc(name, i=0, n=1):
            o, w = po[name]
            return pv.ap[:, o + i:o + i + n]

        def cstv(name):
            o, w = co[name]
            return cst.ap[:, o:o + w]

        def R_(x):
            return [b.res if isinstance(b, Buf) else b for b in x]

        def mm(out, lhsT, rhs, start, stop, R, W):
            S.op("tensor", lambda e: e.matmul(out, lhsT=lhsT, rhs=rhs, start=start, stop=stop), R_(R), R_(W))

        def tp(out, in_, ident, R, W):
            S.op("tensor", lambda e: e.transpose(out=out, in_=in_, identity=ident), R_(R), R_(W))

        def act(out, in_, func, R, W, scale=1.0, bias=None):
            if bias is None:
                S.op("scalar", lambda e: e.activation(out=out, in_=in_, func=func, scale=scale), R_(R), R_(W))
            else:
                S.op("scalar", lambda e: e.activation(out=out, in_=in_, func=func, scale=scale, bias=bias), R_(R), R_(W))

        def tt(eng, out, in0, in1, op, R, W):
            S.op(eng, lambda e: e.tensor_tensor(out=out, in0=in0, in1=in1, op=op), R_(R), R_(W))

        def ts(eng, out, in0, s1, s2, op0, op1, R, W):
            if s2 is None:
                S.op(eng, lambda e: e.tensor_scalar(out=out, in0=in0, scalar1=s1, scalar2=None, op0=op0), R_(R), R_(W))
            else:
                S.op(eng, lambda e: e.tensor_scalar(out=out, in0=in0, scalar1=s1, scalar2=s2, op0=op0, op1=op1), R_(R), R_(W))

        def stt(out, in0, scalar, in1, op0, op1, R, W):
            S.op("vector", lambda e: e.scalar_tensor_tensor(out=out, in0=in0, scalar=scalar, in1=in1, op0=op0, op1=op1), R_(R), R_(W))

        def cp(eng, out, in_, R, W):
            if eng == "scalar":
                S.op("scalar", lambda e: e.copy(out=out, in_=in_), R_(R), R_(W))
            else:
                S.op(eng, lambda e: e.tensor_copy(out=out, in_=in_), R_(R), R_(W))

        def scan(out, d0, d1, init, op0, op1, R, W):
            S.op("vector", lambda e: e.tensor_tensor_scan(out=out, data0=d0, data1=d1, initial=init, op0=op0, op1=op1), R_(R), R_(W))

        def recip(out, in_, R, W):
            S.op("vector", lambda e: e.reciprocal(out=out, in_=in_), R_(R), R_(W))

        def memset(eng, ap, val, W):
            S.op(eng, lambda e: e.memset(ap, val), [], R_(W))

        def dma(q, out, in_, R, W):
            return S.dma(q, out, in_, R_(R), R_(W))

        MUL, ADD, SUB, MAX = ALU.mult, ALU.add, ALU.subtract, ALU.max

        dma("sync", cst.ap[:], cst_in[:, :], [], [cst])
        dma("sync", pv.ap[:], pv_in[:, :], [], [pv])
        memset("vector", smallc.ap[:, 0:1], 0.0, [smallc])
        memset("vector", smallc.ap[:, 1:2], float(np.log(0.125)), [smallc])
        memset("vector", smallc.ap[:, 2:3], 1.0, [smallc])
        memset("vector", smallc.ap[:, 3:4], EPS, [smallc])
        cp("vector", cbf.ap[:, 0:512], cst.ap[:, 0:512], [cst], [cbf])
        ident_f = cstv("ident")
        ident_b = cbf.ap[:, 0:128]
        ones_b = cbf.ap[:, 128:256]
        blk64_b = cbf.ap[:, 256:384]
        prot_b = cbf.ap[:, 384:512]
        masks = cstv("masks")
        onehot = cstv("onehot")
        zcol = smallc.ap[:, 0:1]
        m0 = AR.mark()
        rstage = AR.alloc(2 * NLAT, F32, "rstage")
        dma("sync", rstage.ap[:], rope_in[:, :], [], [rstage])
        cp("vector", ropeb.ap[:, 0:NLAT], rstage.ap[:, 0:NLAT], [rstage], [ropeb])
        cp("gpsimd", ropeb.ap[:, NLAT:], rstage.ap[:, NLAT:], [rstage], [ropeb])
        AR.release(m0)
        cosb = ropeb.ap[:, 0:NLAT]
        sinb = ropeb.ap[:, NLAT:]

        wres = Res("wconv")
        m0 = AR.mark()
        stf = [AR.alloc(4096, F32, "stf%d" % i) for i in range(3)]
        stb = [AR.alloc(4096, BF16, "stb%d" % i) for i in range(3)]
        ci = 0
        for (src, dst, per_l, W) in ((win_in, win_b, NCH_IN, 1024), (wff1_in, wff1_b, 32, 1024), (wout_in, wout_b, 8, 1024),
                                     (wbr_in, wbr_b, 8, 2048), (wff2_in, wff2_b, 8, 4096)):
            g = 4096 // W
            for j0 in range(0, per_l * NL, g):
                f = stf[ci % 3]
                b = stb[ci % 3]
                dma("sync", f.ap[:].rearrange("p (g w) -> p g w", g=g), src[j0:j0 + g].rearrange("g p w -> p g w"), [], [f])
                eng = ("vector", "gpsimd", "scalar")[ci % 3]
                cp(eng, b.ap[:], f.ap[:], [f], [b])
                dma("gpsimd", dst[j0:j0 + g].rearrange("g p w -> p g w"), b.ap[:].rearrange("p (g w) -> p g w", g=g), [b], [wres])
                ci += 1
        AR.release(m0)

        m0 = AR.mark()
        sT = AR.alloc(64, F32, "sT")
        act(sT.ap[:], pvc("cT", 0, 64), AF.Silu, [pv], [sT])
        sT3 = sT.ap[:].rearrange("p (k s) -> p k s", k=8)
        wst = [AR.alloc(8 * 512, F32, "wada%d" % i) for i in range(2)]
        for l in range(NL):
            for ct in range(12):
                w = wst[(l * 12 + ct) % 2]
                dma("sync", w.ap[:].rearrange("p (k n) -> p k n", k=8),
                    wada_in[l].rearrange("(k p) n -> p k n", p=128)[:, :, ct * 512:(ct + 1) * 512], [], [w])
                w3 = w.ap[:].rearrange("p (k n) -> p k n", k=8)
                ps = bankA()
                for fc in range(4):
                    for kc in range(8):
                        mm(ps.ap[:, fc * 8:(fc + 1) * 8], w3[:, kc, fc * 128:(fc + 1) * 128], sT3[:, kc, :], kc == 0, kc == 7, [w, sT], [ps])
                o, _ = po["bada"]
                bb = pv.ap[:, o + l * 48 + ct * 4:o + l * 48 + ct * 4 + 4].unsqueeze(2).to_broadcast([128, 4, 8])
                tt("vector", mod4[:, l, ct * 4:(ct + 1) * 4, :], ps.ap[:, 0:32].rearrange("p (a s) -> p a s", a=4), bb, ADD, [ps, pv], [modT])
            for (jm, gname) in ((8, "g1"), (32, "g2")):
                o, _ = po[gname]
                gb = pv.ap[:, o + l * 8:o + l * 8 + 8].unsqueeze(2).to_broadcast([128, 8, 8])
                stt(mod4[:, l, jm:jm + 8, :], mod4[:, l, jm:jm + 8, :], 1.0, gb, ADD, MUL, [modT, pv], [modT])
        AR.release(m0)

        def modc(l, m, kc, s):
            return mod4[:, l, m * 8 + kc, s:s + 1]

        lb3 = lbt.ap[:].rearrange("p (a l j) -> p a l j", a=3, l=L)
        ex = small[0]
        act(ex.ap[:, 0:L * 8], pvc("lbl", 0, L * 8), AF.Exp, [pv], [ex])
        ex3 = ex.ap[:, 0:L * 8].rearrange("p (l j) -> p l j", l=L)
        ssum = small[1]
        tt("vector", ssum.ap[:, 0:8], ex3[:, 0, :], ex3[:, 1, :], ADD, [ex], [ssum])
        tt("vector", ssum.ap[:, 0:8], ssum.ap[:, 0:8], ex3[:, 2, :], ADD, [ex, ssum], [ssum])
        tt("vector", ssum.ap[:, 0:8], ssum.ap[:, 0:8], ex3[:, 3, :], ADD, [ex, ssum], [ssum])
        recip(ssum.ap[:, 0:8], ssum.ap[:, 0:8], [ssum], [ssum])
        memset("vector", lb3[:, 0, 0, :], 0.0, [lbt])
        for l in range(1, L):
            tt("vector", lb3[:, 0, l, :], lb3[:, 0, l - 1, :], ex3[:, l, :], ADD, [ex, lbt], [lbt])
        for l in range(1, L):
            tt("vector", lb3[:, 0, l, :], lb3[:, 0, l, :], ssum.ap[:, 0:8], MUL, [ssum, lbt], [lbt])
        for l in range(L):
            ts("vector", lb3[:, 2, l, :], lb3[:, 0, l, :], -1.0, None, ADD, None, [lbt], [lbt])
            ts("vector", lb3[:, 1, l, :], lb3[:, 2, l, :], -1.0, None, MUL, None, [lbt], [lbt])
        S.barrier()

        def load_w(wb, idx, nelem=1024, slot=None):
            s_ = slot if slot is not None else nxt(wslots, "w")
            dma("sync", s_.ap[:, 0:nelem], wb[idx], [wres], [s_])
            return s_

        def proj_fm(wslot, t0, n, bi, width=128):
            ps = bankA()
            w3 = wslot.ap[:].rearrange("p (k c) -> p k c", k=8)
            for kc in range(8):
                mm(ps.ap[0:width, 0:n], w3[:, kc, 0:width], hT3[:, kc, t0:t0 + n], kc == 0, kc == 7, [wslot, hres[bi]], [ps])
            return ps

        def proj_tm(wslot, tile, ps, c0, width=128):
            w3 = wslot.ap[:].rearrange("p (k c) -> p k c", k=8)
            bi = 0 if tile < 2 else 1 + (tile - 2) // 4
            for kc in range(8):
                mm(ps.ap[:, c0:c0 + width], hT3[:, kc, tile * 128:(tile + 1) * 128], w3[:, kc, 0:width], kc == 0, kc == 7, [wslot, hres[bi]], [ps])

        def norm_mod(xb3, n, l, mA, mB, s, out3, t0, R, W):
            ps = bankA()
            for kc in range(8):
                sq = nxt(tmpB, "B")
                act(sq.ap[:, 0:n], xb3[:, kc, 0:n], AF.Square, R, [sq])
                mm(ps.ap[:, 0:n], ones_b, sq.ap[:, 0:n], kc == 0, kc == 7, [sq, cbf], [ps])
            rs = nxt(tmpR, "R")
            act(rs.ap[:, 0:n], ps.ap[:, 0:n], AF.Sqrt, [ps, smallc], [rs], scale=1.0 / DM, bias=smallc.ap[:, 3:4])
            recip(rs.ap[:, 0:n], rs.ap[:, 0:n], [rs], [rs])
            for kc in range(8):
                t_ = nxt(tmpF, "F")
                tt("vector", t_.ap[:, 0:n], xb3[:, kc, 0:n], rs.ap[:, 0:n], MUL, R + [rs], [t_])
                act(out3[:, kc, t0:t0 + n], t_.ap[:, 0:n], AF.Identity, [t_, modT], W, scale=modc(l, mA, kc, s), bias=modc(l, mB, kc, s))

        def rms_fm(srcs, rows, n, blkmat, denom, gaps, outs, R, W, keep=None):
            kfs = []
            ps2 = bankA()
            for i, (src, sres) in enumerate(srcs):
                kf = nxt(tmpF, "F")
                cp("scalar", kf.ap[0:rows, 0:n], src, [sres], [kf])
                sq = nxt(tmpB, "B")
                tt("vector", sq.ap[0:rows, 0:n], kf.ap[0:rows, 0:n], kf.ap[0:rows, 0:n], MUL, [kf], [sq])
                mm(ps2.ap[0:rows, 0:n], blkmat[0:rows, 0:rows], sq.ap[0:rows, 0:n], i == 0, i == len(srcs) - 1, [sq, cbf], [ps2])
                kfs.append(kf)
            rs = nxt(tmpR, "R")
            act(rs.ap[0:rows, 0:n], ps2.ap[0:rows, 0:n], AF.Sqrt, [ps2, smallc], [rs], scale=1.0 / denom, bias=smallc.ap[0:rows, 3:4])
            recip(rs.ap[0:rows, 0:n], rs.ap[0:rows, 0:n], [rs], [rs])
            for kf, g, o in zip(kfs, gaps, outs):
                stt(o, kf.ap[0:rows, 0:n], g, rs.ap[0:rows, 0:n], MUL, MUL, [kf, rs, pv] + R, W)

        def rope(src_b, sres, rows, n, t0, out, W):
            lt = t0 - NCTX
            ps3 = bankA()
            mm(ps3.ap[0:rows, 0:n], prot_b[0:rows, 0:rows], src_b, True, True, [sres, cbf], [ps3])
            t1 = nxt(tmpF, "F")
            tt("gpsimd", t1.ap[0:rows, 0:n], src_b, cosb[0:rows, lt:lt + n], MUL, [sres, ropeb], [t1])
            t2 = nxt(tmpF, "F")
            tt("vector", t2.ap[0:rows, 0:n], ps3.ap[0:rows, 0:n], sinb[0:rows, lt:lt + n], MUL, [ps3, ropeb], [t2])
            tt("vector", out, t1.ap[0:rows, 0:n], t2.ap[0:rows, 0:n], ADD, [t1, t2], W)

        def attention(pairs_fn, v_fn, nkt, n, dv, scale, y_fn, R, yres):
            nqs = n // 128

            def s_stage(kt):
                psS_ = bankA()
                prs = pairs_fn(kt)
                for i, (k_ap, q_ap) in enumerate(prs):
                    mm(psS_.ap[:, 0:n], k_ap, q_ap, i == 0, i == len(prs) - 1, R, [psS_])
                return psS_
            ps_next = s_stage(0)
            for kt in range(nkt):
                psS = ps_next
                if kt + 1 < nkt:
                    ps_next = s_stage(kt + 1)
                E = nxt(tmpE, "E")
                act(E.ap[:, 0:n], psS.ap[:, 0:n], AF.Exp, [psS], [E], scale=scale)
                for qs in range(nqs):
                    mm(bankO[qs].ap[:, 0:dv + 1], E.ap[:, qs * 128:(qs + 1) * 128], v_fn(kt), kt == 0, kt == nkt - 1, [E] + R, [bankO[qs]])
            for qs in range(nqs):
                rd = nxt(small, "s")
                recip(rd.ap[:, 0:1], bankO[qs].ap[:, dv:dv + 1], [bankO[qs]], [rd])
                ts("vector", y_fn(qs), bankO[qs].ap[:, 0:dv], rd.ap[:, 0:1], None, MUL, None, [bankO[qs], rd], [yres])

        def store_yT(ytok, yres, n, t0, branch):
            nqs = n // 128
            y3 = ytok.ap[:].rearrange("p (q f) -> p q f", f=512)
            for c in range(4):
                ps = bankA()
                pb = ps.ap[:].bitcast(BF16)
                for qs in range(nqs):
                    tp(pb[:, qs * 128:(qs + 1) * 128], y3[:, qs, c * 128:(c + 1) * 128], ident_b, [yres, cbf], [ps])
                yt = nxt(tmpB, "B")
                cp("scalar", yt.ap[:, 0:n], pb[:, 0:n], [ps], [yt])
                dma("gpsimd", yT_d[branch * 4 + c][:, t0:t0 + n], yt.ap[:, 0:n], [yt], [yres_d])

        yres_d = Res("yT_d")

        def mixer_C(l, s):
            m_ = AR.mark()
            kT = AR.alloc(T, BF16, "kT_c")
            vC = AR.alloc(NT * 2 * 65, BF16, "vC")
            vC4 = vC.ap[:].rearrange("p (t h d) -> p t h d", t=NT, h=2)
            ytok = AR.alloc(4 * 512, BF16, "ytokC")
            y3 = ytok.ap[:].rearrange("p (q f) -> p q f", f=512)
            qTs = [AR.alloc(4 * 512, BF16, "qT%d" % i) for i in range(2)]
            memset("gpsimd", vC.ap[:], 1.0, [vC])
            wk = load_w(win_b, l * NCH_IN + CH_CK)
            for bi, (t0, n) in enumerate(BLKS):
                ps = proj_fm(wk, t0, n, bi)
                if t0 < NCTX:
                    rms_fm([(ps.ap[:, 0:n], ps)], 128, n, blk64_b, 64.0, [pvc("gkg", l)], [kT.ap[:, t0:t0 + n]], [], [kT])
                else:
                    kn = nxt(tmpB, "B")
                    rms_fm([(ps.ap[:, 0:n], ps)], 128, n, blk64_b, 64.0, [pvc("gkg", l)], [kn.ap[:, 0:n]], [], [kn])
                    rope(kn.ap[:, 0:n], kn, 128, n, t0, kT.ap[:, t0:t0 + n], [kT])
            wv = load_w(win_b, l * NCH_IN + CH_CV)
            for tl in range(NT):
                ps = bankA()
                proj_tm(wv, tl, ps, 0)
                cp("vector", vC4[:, tl, :, 0:64], ps.ap[:, 0:128].rearrange("p (h d) -> p h d", h=2), [ps], [vC])
            wq = [load_w(win_b, l * NCH_IN + CH_CQ + c) for c in range(4)]
            for bi, (t0, n) in enumerate(BLKS):
                qT = qTs[bi % 2]
                q3 = qT.ap[:].rearrange("p (c t) -> p c t", c=4)
                for c in range(4):
                    ps = proj_fm(wq[c], t0, n, bi)
                    if t0 < NCTX:
                        rms_fm([(ps.ap[:, 0:n], ps)], 128, n, blk64_b, 64.0, [pvc("gqg", l)], [q3[:, c, 0:n]], [], [qT])
                    else:
                        qn = nxt(tmpB, "B")
                        rms_fm([(ps.ap[:, 0:n], ps)], 128, n, blk64_b, 64.0, [pvc("gqg", l)], [qn.ap[:, 0:n]], [], [qn])
                        rope(qn.ap[:, 0:n], qn, 128, n, t0, q3[:, c, 0:n], [qT])
                nkt = 2 if bi == 0 else NT
                for hq in range(8):
                    kvh = hq // 4
                    c = hq % 4
                    b0 = kvh * 64

                    def pairs(kt, b0=b0, c=c, n=n, q3=q3):
                        return [(kT.ap[b0:b0 + 64, kt * 128:(kt + 1) * 128], q3[b0:b0 + 64, c, 0:n])]

                    def vfn(kt, kvh=kvh):
                        return vC4[:, kt, kvh, :]

                    def yfn(qs, hq=hq):
                        return y3[:, qs, hq * 64:(hq + 1) * 64]
                    attention(pairs, vfn, nkt, n, 64, 0.125, yfn, [kT, qT, vC], ytok)
                store_yT(ytok, ytok, n, t0, 2)
            AR.release(m_)

        def mixer_D(l, s):
            m_ = AR.mark()
            wuq_f = AR.alloc(2 * 768, F32, "wuq_f")
            wuq = AR.alloc(2 * 768, BF16, "wuq")
            wuk_f = AR.alloc(512, F32, "wuk_f")
            wuv_f = AR.alloc(512, F32, "wuv_f")
            wukv = AR.alloc(1024, BF16, "wukv")
            dma("sync", wuq_f.ap[:].rearrange("p (k n) -> p k n", k=2), wuq_in[l].rearrange("(k p) n -> p k n", p=128), [], [wuq_f])
            dma("sync", wuk_f.ap[:], wuk_in[l], [], [wuk_f])
            dma("sync", wuv_f.ap[:], wuv_in[l], [], [wuv_f])
            cp("vector", wuq.ap[:], wuq_f.ap[:], [wuq_f], [wuq])
            cp("vector", wukv.ap[:, 0:512], wuk_f.ap[:], [wuk_f], [wukv])
            cp("vector", wukv.ap[:, 512:1024], wuv_f.ap[:], [wuv_f], [wukv])
            wuq3 = wuq.ap[:].rearrange("p (k n) -> p k n", k=2)
            ckvn = AR.alloc(T, BF16, "ckvn")
            knT = AR.alloc(4 * T, BF16, "knT")
            kn3 = knT.ap[:].rearrange("p (h t) -> p h t", h=4)
            krT = AR.alloc(T, BF16, "krT")
            vD = AR.alloc(NT * 4 * 129, BF16, "vD")
            vD4 = vD.ap[:].rearrange("p (t h d) -> p t h d", t=NT, h=4)
            ytok = AR.alloc(4 * 512, BF16, "ytokD")
            y3 = ytok.ap[:].rearrange("p (q f) -> p q f", f=512)
            qn = [AR.alloc(4 * 512, BF16, "qnD%d" % i) for i in range(2)]
            qr = [AR.alloc(4 * 512, BF16, "qrD%d" % i) for i in range(2)]
            cqn = AR.alloc(2 * 512, BF16, "cqn")
            cq3 = cqn.ap[:].rearrange("p (k t) -> p k t", k=2)
            memset("gpsimd", vD.ap[:], 1.0, [vD])
            wc = load_w(win_b, l * NCH_IN + CH_DCKV)
            wr = load_w(win_b, l * NCH_IN + CH_DKR)
            for bi, (t0, n) in enumerate(BLKS):
                ps = proj_fm(wc, t0, n, bi)
                rms_fm([(ps.ap[:, 0:n], ps)], 128, n, ones_b, 128.0, [pvc("mkvg", l)], [ckvn.ap[:, t0:t0 + n]], [], [ckvn])
                for h in range(4):
                    ps = bankA()
                    mm(ps.ap[:, 0:n], wukv.ap[:, h * 128:(h + 1) * 128], ckvn.ap[:, t0:t0 + n], True, True, [wukv, ckvn], [ps])
                    cp("scalar" if h % 2 else "vector", kn3[:, h, t0:t0 + n], ps.ap[:, 0:n], [ps], [knT])
                ps = proj_fm(wr, t0, n, bi, width=64)
                if t0 < NCTX:
                    cp("scalar", krT.ap[0:64, t0:t0 + n], ps.ap[0:64, 0:n], [ps], [krT])
                else:
                    kb = nxt(tmpB, "B")
                    cp("scalar", kb.ap[0:64, 0:n], ps.ap[0:64, 0:n], [ps], [kb])
                    rope(kb.ap[0:64, 0:n], kb, 64, n, t0, krT.ap[0:64, t0:t0 + n], [krT])
            for tl in range(NT):
                ps = bankA()
                mm(ps.ap[:, 0:512], ckvn.ap[:, tl * 128:(tl + 1) * 128], wukv.ap[:, 512:1024], True, True, [wukv, ckvn], [ps])
                cp("vector" if tl % 2 else "scalar", vD4[:, tl, :, 0:128], ps.ap[:, 0:512].rearrange("p (h d) -> p h d", h=4), [ps], [vD])
            wq = [load_w(win_b, l * NCH_IN + CH_DCQ + c) for c in range(2)]
            o_mq, _ = po["mqg"]
            for bi, (t0, n) in enumerate(BLKS):
                pss = [proj_fm(wq[c], t0, n, bi) for c in range(2)]
                rms_fm([(p_.ap[:, 0:n], p_) for p_ in pss], 128, n, ones_b, 256.0,
                       [pv.ap[:, o_mq + l * 2 + c:o_mq + l * 2 + c + 1] for c in range(2)], [cq3[:, c, 0:n] for c in range(2)], [], [cqn])
                qn3 = qn[bi % 2].ap[:].rearrange("p (h t) -> p h t", h=4)
                qr3 = qr[bi % 2].ap[:].rearrange("p (h t) -> p h t", h=4)
                for h in range(4):
                    ps = bankA()
                    for kc in range(2):
                        mm(ps.ap[:, 0:n], wuq3[:, kc, h * 192:h * 192 + 128], cq3[:, kc, 0:n], kc == 0, kc == 1, [wuq, cqn], [ps])
                    cp("scalar", qn3[:, h, 0:n], ps.ap[:, 0:n], [ps], [qn[bi % 2]])
                    ps = bankA()
                    for kc in range(2):
                        mm(ps.ap[0:64, 0:n], wuq3[:, kc, h * 192 + 128:h * 192 + 192], cq3[:, kc, 0:n], kc == 0, kc == 1, [wuq, cqn], [ps])
                    if t0 < NCTX:
                        cp("vector", qr3[0:64, h, 0:n], ps.ap[0:64, 0:n], [ps], [qr[bi % 2]])
                    else:
                        qb = nxt(tmpB, "B")
                        cp("scalar", qb.ap[0:64, 0:n], ps.ap[0:64, 0:n], [ps], [qb])
                        rope(qb.ap[0:64, 0:n], qb, 64, n, t0, qr3[0:64, h, 0:n], [qr[bi % 2]])
                nkt = 2 if bi == 0 else NT
                for h in range(4):
                    def pairs(kt, h=h, n=n, qn3=qn3, qr3=qr3):
                        return [(kn3[:, h, kt * 128:(kt + 1) * 128], qn3[:, h, 0:n]),
                                (krT.ap[0:64, kt * 128:(kt + 1) * 128], qr3[0:64, h, 0:n])]

                    def vfn(kt, h=h):
                        return vD4[:, kt, h, :]

                    def yfn(qs, h=h):
                        return y3[:, qs, h * 128:(h + 1) * 128]
                    attention(pairs, vfn, nkt, n, 128, float(192 ** -0.5), yfn, [knT, krT, qn[bi % 2], qr[bi % 2], vD], ytok)
                store_yT(ytok, ytok, n, t0, 3)
            AR.release(m_)

        def vorder(d):
            return list(range(36)) if d == 0 else [3, 2, 1, 0] + list(range(35, 3, -1))

        def mixer_A(l, s):
            m_ = AR.mark()
            qs_ = AR.alloc(T, F32, "qs")
            sg = AR.alloc(T, BF16, "sg")
            lf = AR.alloc(T, F32, "lf")
            kk = AR.alloc(T, F32, "kk")
            vtok = AR.alloc(NT * 128, BF16, "vtok")
            v3 = vtok.ap[:].rearrange("p (t v) -> p t v", t=NT)
            oacc = AR.alloc(T, F32, "oacc")
            Fc = AR.alloc(T, F32, "Fc")
            tmp = AR.alloc(T, F32, "tmpA")
            eq = AR.alloc(T, F32, "eq")
            qt = AR.alloc(T, BF16, "qt")
            ktT = AR.alloc(T, BF16, "ktT")
            ktok = AR.alloc(NT * 128, BF16, "ktok")
            k3 = ktok.ap[:].rearrange("p (t v) -> p t v", t=NT)
            Sst = AR.alloc(128, F32, "Sst")
            Sb = [AR.alloc(128, BF16, "Sb%d" % i) for i in range(2)]
            tS = AR.alloc(128, F32, "tS")
            ATs = [[AR.alloc(64, BF16, "AT%d_%d" % (dd, i)) for i in range(3)] for dd in range(2)]
            for dd in range(2):
                for i in range(3):
                    memset("vector", ATs[dd][i].ap[:], 0.0, [ATs[dd][i]])
            qtX = AR.alloc(T, BF16, "qtX")
            ktX = AR.alloc(T, BF16, "ktX")
            sm = AR.alloc(6 * 36, F32, "smA")
            sm3 = sm.ap[:].rearrange("p (a c) -> p a c", a=6)
            for hd in range(4):
                wv = load_w(win_b, l * NCH_IN + CH_AI + hd)
                wq = load_w(win_b, l * NCH_IN + CH_AQ + hd)
                wg = load_w(win_b, l * NCH_IN + CH_AG + hd)
                for bi, (t0, n) in enumerate(BLKS):
                    ps = proj_fm(wq, t0, n, bi)
                    act(qs_.ap[:, t0:t0 + n], ps.ap[:, 0:n], AF.Silu, [ps], [qs_])
                    ps = proj_fm(wg, t0, n, bi)
                    act(sg.ap[:, t0:t0 + n], ps.ap[:, 0:n], AF.Silu, [ps], [sg])
                for tl in range(NT):
                    ps = bankA()
                    proj_tm(wv, tl, ps, 0)
                    cp("vector", v3[:, tl, :], ps.ap[:, 0:128], [ps], [vtok])
                for d in range(2):
                    wf = load_w(win_b, l * NCH_IN + (CH_AFF if d == 0 else CH_AFB) + hd)
                    j = d * 4 + hd
                    lbv = lb3[:, 0, l, j:j + 1]
                    omlb = lb3[:, 1, l, j:j + 1]
                    nomlb = lb3[:, 2, l, j:j + 1]
                    for bi, (t0, n) in enumerate(BLKS):
                        ps = proj_fm(wf, t0, n, bi)
                        act(tmp.ap[:, t0:t0 + n], ps.ap[:, 0:n], AF.Sigmoid, [ps], [tmp])
                    act(lf.ap[:], tmp.ap[:], AF.Ln, [tmp, lbt], [lf], scale=omlb, bias=lbv)
                    act(kk.ap[:], tmp.ap[:], AF.Identity, [tmp, lbt], [kk], scale=nomlb, bias=omlb)
                    scan(Fc.ap[:], lf.ap[:], zcol.to_broadcast([128, T]), 0.0, ADD, ADD, [lf, smallc], [Fc])
                    if d == 0:
                        vc = Fc
                    else:
                        tt("vector", Fc.ap[:], lf.ap[:], Fc.ap[:], SUB, [lf, Fc], [Fc])
                        vc = Fc
                    vc3 = vc.ap[:].rearrange("p (c t) -> p c t", t=64)
                    lf3 = lf.ap[:].rearrange("p (c t) -> p c t", t=64)
                    fv = 0 if d == 0 else 63
                    lv = 63 if d == 0 else 0
                    tt("vector", sm3[:, 0, :], vc3[:, :, fv], lf3[:, :, fv], SUB, [vc, lf], [sm])
                    tt("vector", sm3[:, 1, :], vc3[:, :, 32], sm3[:, 0, :], SUB, [vc, sm], [sm])
                    tt("vector", sm3[:, 2, :], vc3[:, :, lv], vc3[:, :, 32], SUB, [vc], [sm])
                    tt("vector", sm3[:, 3, :], vc3[:, :, lv], sm3[:, 0, :], SUB, [vc, sm], [sm])
                    act(sm3[:, 1:4, :], sm3[:, 1:4, :], AF.Exp, [sm], [sm])
                    tmp3 = tmp.ap[:].rearrange("p (c t) -> p c t", t=64)
                    tt("vector", tmp3, vc3, vc3[:, :, 32:33].to_broadcast([128, 36, 64]), SUB, [vc], [tmp])
                    act(eq.ap[:], tmp.ap[:], AF.Exp, [tmp], [eq])
                    act(tmp.ap[:], tmp.ap[:], AF.Exp, [tmp], [tmp], scale=-1.0)
                    tt("vector", qt.ap[:], qs_.ap[:], eq.ap[:], MUL, [qs_, eq], [qt])
                    tt("gpsimd", ktT.ap[:], kk.ap[:], tmp.ap[:], MUL, [kk, tmp], [ktT])
                    tmp32 = tmp.ap[:].rearrange("p (c t) -> p c t", t=32)
                    vc32 = vc.ap[:].rearrange("p (c t) -> p c t", t=32)
                    tt("vector", tmp32, vc32, vc32[:, :, 16:17].to_broadcast([128, 72, 32]), SUB, [vc], [tmp])
                    act(eq.ap[:], tmp.ap[:], AF.Exp, [tmp], [eq])
                    act(tmp.ap[:], tmp.ap[:], AF.Exp, [tmp], [tmp], scale=-1.0)
                    tt("vector", qtX.ap[:], qs_.ap[:], eq.ap[:], MUL, [qs_, eq], [qtX])
                    tt("gpsimd", ktX.ap[:], kk.ap[:], tmp.ap[:], MUL, [kk, tmp], [ktX])
                    for tl in range(NT):
                        ps = bankA()
                        pb = ps.ap[:].bitcast(BF16)
                        tp(pb[:, 0:128], ktT.ap[:, tl * 128:(tl + 1) * 128], ident_b, [ktT, cbf], [ps])
                        cp("scalar" if tl % 2 else "vector", k3[:, tl, :], pb[:, 0:128], [ps], [ktok])
                    memset("vector", Sst.ap[:], 0.0, [Sst])
                    memset("vector", Sb[0].ap[:], 0.0, [Sb[0]])
                    order = vorder(d)
                    mk_ = masks[:, d * 64:(d + 1) * 64]

                    def a_stage(ci_, c, d=d, mk_=mk_):
                        b = (c % 2) * 64
                        tk0 = c * 64
                        psA = bankS()
                        AT = ATs[d][ci_ % 3]
                        if d == 0:
                            mm(psA.ap[b:b + 64, 0:32], ktX.ap[:, tk0:tk0 + 64], qtX.ap[:, tk0:tk0 + 32], True, True, [ktX, qtX], [psA])
                            mm(psA.ap[b:b + 64, 32:64], ktX.ap[:, tk0:tk0 + 64], qtX.ap[:, tk0 + 32:tk0 + 64], True, True, [ktX, qtX], [psA])
                            mm(psA.ap[b:b + 32, 32:64], ktT.ap[:, tk0:tk0 + 32], qt.ap[:, tk0 + 32:tk0 + 64], True, True, [ktT, qt], [psA])
                        else:
                            mm(psA.ap[b:b + 64, 0:32], ktT.ap[:, tk0:tk0 + 64], qt.ap[:, tk0:tk0 + 32], True, True, [ktT, qt], [psA])
                            mm(psA.ap[b:b + 32, 0:32], ktX.ap[:, tk0:tk0 + 32], qtX.ap[:, tk0:tk0 + 32], True, True, [ktX, qtX], [psA])
                            mm(psA.ap[b:b + 64, 32:64], ktX.ap[:, tk0:tk0 + 64], qtX.ap[:, tk0 + 32:tk0 + 64], True, True, [ktX, qtX], [psA])
                        tt("vector", AT.ap[b:b + 64, :], psA.ap[b:b + 64, 0:64], mk_[b:b + 64, :], MUL, [psA, cst], [AT])
                        return AT
                    def d_stage(c):
                        tl_ = c // 2
                        b_ = (c % 2) * 64
                        psD_ = bankS()
                        mm(psD_.ap[:, 0:128], k3[b_:b_ + 64, tl_, :], v3[b_:b_ + 64, tl_, :], True, True, [ktok, vtok], [psD_])
                        return psD_
                    nst = len(order)
                    AT_next = a_stage(0, order[0])
                    psD_cur = d_stage(order[0]) if nst > 1 else None
                    for ci_, c in enumerate(order):
                        tl = c // 2
                        b = (c % 2) * 64
                        tk0 = c * 64
                        sbc = Sb[ci_ % 2]
                        AT = AT_next
                        psD = psD_cur
                        psD_cur = None
                        if ci_ + 1 < nst - 1:
                            psD_cur = d_stage(order[ci_ + 1])
                        if ci_ < nst - 1:
                            AT_next = a_stage(ci_ + 1, order[ci_ + 1])
                            act(tS.ap[:], psD.ap[:, 0:128], AF.Copy, [psD, sm], [tS], scale=sm3[:, 2, c:c + 1])
                            stt(Sst.ap[:], Sst.ap[:], sm3[:, 3, c:c + 1], tS.ap[:], MUL, ADD, [Sst, tS, sm], [Sst])
                            cn = order[ci_ + 1]
                            act(Sb[(ci_ + 1) % 2].ap[:], Sst.ap[:], AF.Copy, [Sst, sm], [Sb[(ci_ + 1) % 2]], scale=sm3[:, 1, cn:cn + 1])
                        psO = bankS()
                        mm(psO.ap[:, 0:64], sbc.ap[:], qt.ap[:, tk0:tk0 + 64], True, False, [sbc, qt], [psO])
                        mm(psO.ap[:, 0:64], v3[b:b + 64, tl, :], AT.ap[b:b + 64, :], False, True, [vtok, AT], [psO])
                        if d == 0:
                            cp("scalar", oacc.ap[:, tk0:tk0 + 64], psO.ap[:, 0:64], [psO], [oacc])
                        else:
                            tt("vector", oacc.ap[:, tk0:tk0 + 64], oacc.ap[:, tk0:tk0 + 64], psO.ap[:, 0:64], ADD, [psO, oacc], [oacc])
                for bi, (t0, n) in enumerate(BLKS):
                    sq = nxt(tmpB, "B")
                    act(sq.ap[:, 0:n], oacc.ap[:, t0:t0 + n], AF.Square, [oacc], [sq])
                    ps = bankA()
                    mm(ps.ap[:, 0:n], ones_b, sq.ap[:, 0:n], True, True, [sq, cbf], [ps])
                    rs = nxt(tmpR, "R")
                    act(rs.ap[:, 0:n], ps.ap[:, 0:n], AF.Sqrt, [ps, smallc], [rs], scale=1.0 / 128, bias=smallc.ap[:, 3:4])
                    recip(rs.ap[:, 0:n], rs.ap[:, 0:n], [rs], [rs])
                    t1 = nxt(tmpF, "F")
                    stt(t1.ap[:, 0:n], oacc.ap[:, t0:t0 + n], pvc("hng", l), rs.ap[:, 0:n], MUL, MUL, [oacc, rs, pv], [t1])
                    yb = nxt(tmpB, "B")
                    tt("vector", yb.ap[:, 0:n], t1.ap[:, 0:n], sg.ap[:, t0:t0 + n], MUL, [t1, sg], [yb])
                    dma("gpsimd", yT_d[0 * 4 + hd][:, t0:t0 + n], yb.ap[:, 0:n], [yb], [yres_d])
            AR.release(m_)

        def mixer_B(l, s):
            m_ = AR.mark()
            tokscal = AR.alloc(NT * 24, F32, "tokscal")
            tsc3 = tokscal.ap[:].rearrange("p (t r) -> p t r", t=NT)
            dB = AR.alloc(8 * 36, F32, "dB")
            dB3 = dB.ap[:].rearrange("p (r c) -> p r c", r=8)
            m1 = AR.mark()
            A0 = AR.alloc(T, F32, "A0")
            A1 = AR.alloc(T, F32, "A1")
            A2 = AR.alloc(T, F32, "A2")
            A3 = AR.alloc(T, F32, "A3")
            Rr = AR.alloc(36, F32, "Rr")
            Rb = AR.alloc(8 * 36, F32, "Rb")
            Rb3 = Rb.ap[:].rearrange("p (r c) -> p r c", r=8)
            sel = pvc("sel", 0)
            selm1 = pvc("sel", 1)
            o_mgb, _ = po["mgb"]
            wig = load_w(win_b, l * NCH_IN + CH_BIG)
            wlf = load_w(win_b, l * NCH_IN + CH_BLF)
            P8 = slice(0, 8)
            for bi, (t0, n) in enumerate(BLKS):
                ps = proj_fm(wig, t0, n, bi, width=8)
                act(A0.ap[P8, t0:t0 + n], ps.ap[P8, 0:n], AF.Identity, [ps, pv], [A0], bias=pv.ap[P8, o_mgb + l * 2:o_mgb + l * 2 + 1])
                ps = proj_fm(wlf, t0, n, bi, width=8)
                t_ = nxt(tmpF, "F")
                ts("vector", t_.ap[P8, 0:n], ps.ap[P8, 0:n], pv.ap[P8, o_mgb + l * 2 + 1:o_mgb + l * 2 + 2], -1.0, ADD, MUL, [ps, pv], [t_])
                act(t_.ap[P8, 0:n], t_.ap[P8, 0:n], AF.Exp, [t_], [t_])
                act(A1.ap[P8, t0:t0 + n], t_.ap[P8, 0:n], AF.Ln, [t_, smallc], [A1], bias=smallc.ap[P8, 2:3])
            ts("vector", A1.ap[P8, :], A1.ap[P8, :], -1.0, None, MUL, None, [A1], [A1])
            scan(A2.ap[P8, :], A1.ap[P8, :], zcol[P8, :].to_broadcast([8, T]), 0.0, ADD, ADD, [A1, smallc], [A2])
            Kt = nxt(small, "s")
            ts("vector", Kt.ap[P8, 0:1], A2.ap[P8, T - 1:T], selm1[P8, :], -1.0, MUL, MUL, [A2, pv], [Kt])
            tt("vector", A1.ap[P8, :], A2.ap[P8, :], A1.ap[P8, :], SUB, [A1, A2], [A1])
            ts("vector", A3.ap[P8, :], A1.ap[P8, :], selm1[P8, :], None, MUL, None, [A1, pv], [A3])
            stt(A2.ap[P8, :], A2.ap[P8, :], sel[P8, :], A3.ap[P8, :], MUL, ADD, [A2, A3, pv], [A2])
            ts("vector", A2.ap[P8, NCTX:T], A2.ap[P8, NCTX:T], Kt.ap[P8, 0:1], None, ADD, None, [A2, Kt], [A2])
            tt("vector", A0.ap[P8, :], A0.ap[P8, :], A2.ap[P8, :], SUB, [A0, A2], [A0])
            scan(A1.ap[P8, :], A0.ap[P8, :], A0.ap[P8, :], -1e30, MAX, MAX, [A0], [A1])
            scan(A3.ap[P8, 0:NCTX][:, ::-1], A0.ap[P8, 0:NCTX][:, ::-1], A0.ap[P8, 0:NCTX][:, ::-1], -1e30, MAX, MAX, [A0], [A3])
            scan(A3.ap[P8, NCTX:T][:, ::-1], A0.ap[P8, NCTX:T][:, ::-1], A0.ap[P8, NCTX:T][:, ::-1], A3.ap[P8, 0:1], MAX, MAX, [A0, A3], [A3])
            ts("vector", A3.ap[P8, :], A3.ap[P8, :], selm1[P8, :], -1.0, MUL, MUL, [A3, pv], [A3])
            stt(A1.ap[P8, :], A1.ap[P8, :], sel[P8, :], A3.ap[P8, :], MUL, ADD, [A1, A3, pv], [A1])
            G3 = A1.ap[P8, :].rearrange("p (c t) -> p c t", t=64)
            ts("vector", Rr.ap[P8, :], G3[:, :, 63], selm1[P8, :], -1.0, MUL, MUL, [A1, pv], [Rr])
            stt(Rr.ap[P8, :], G3[:, :, 0], sel[P8, :], Rr.ap[P8, :], MUL, ADD, [A1, Rr, pv], [Rr])
            Rbc = Rr.ap[P8, :].unsqueeze(2).to_broadcast([8, 36, 64])
            a3 = A0.ap[P8, :].rearrange("p (c t) -> p c t", t=64)
            A33 = A3.ap[P8, :].rearrange("p (c t) -> p c t", t=64)
            tt("vector", A33, a3, Rbc, SUB, [A0, Rr], [A3])
            act(A3.ap[P8, :], A3.ap[P8, :], AF.Exp, [A3, smallc], [A3], bias=smallc.ap[P8, 1:2])
            tt("vector", a3, G3, Rbc, SUB, [A1, Rr], [A0])
            act(A0.ap[P8, :], A0.ap[P8, :], AF.Exp, [A0], [A0], scale=-1.0)
            tt("vector", A2.ap[P8, :], A2.ap[P8, :], A1.ap[P8, :], ADD, [A1, A2], [A2])
            act(A2.ap[P8, :], A2.ap[P8, :], AF.Exp, [A2], [A2], scale=-1.0)
            for tl in range(NT):
                ps = bankA()
                for qi, Aq in enumerate((A3, A0, A2)):
                    tp(ps.ap[:, qi * 8:(qi + 1) * 8], Aq.ap[P8, tl * 128:(tl + 1) * 128], ident_f[0:8, 0:8], [Aq, cst], [ps])
                cp("vector", tsc3[:, tl, :], ps.ap[:, 0:24], [ps], [tokscal])
            Df = AR.alloc(36, F32, "Df")
            Db = AR.alloc(36, F32, "Db")
            memset("vector", Df.ap[P8, :], 0.0, [Df])
            tt("vector", Df.ap[P8, 0:35], Rr.ap[P8, 0:35], Rr.ap[P8, 1:36], SUB, [Rr], [Df])
            tt("vector", Db.ap[P8, 1:36], Rr.ap[P8, 1:36], Rr.ap[P8, 0:35], SUB, [Rr], [Db])
            tt("vector", Db.ap[P8, 0:1], Rr.ap[P8, 0:1], Rr.ap[P8, 35:36], SUB, [Rr], [Db])
            memset("vector", Db.ap[P8, 4:5], 0.0, [Db])
            ts("vector", Db.ap[P8, :], Db.ap[P8, :], selm1[P8, :], -1.0, MUL, MUL, [Db, pv], [Db])
            stt(Df.ap[P8, :], Df.ap[P8, :], sel[P8, :], Db.ap[P8, :], MUL, ADD, [Df, Db, pv], [Df])
            act(Df.ap[P8, :], Df.ap[P8, :], AF.Exp, [Df], [Df])
            ps = bankA()
            for r in range(8):
                mm(ps.ap[:, r * 36:(r + 1) * 36], onehot[0:8, r * 128:(r + 1) * 128], Df.ap[P8, :], True, True, [Df, cst], [ps])
            cp("vector", dB.ap[:, :], ps.ap[:, 0:288], [ps], [dB])
            if dbg:
                dma("gpsimd", tsc_dbg[:, :], tokscal.ap[:], [tokscal], [Res()])
                dma("gpsimd", dB_dbg[:, :], dB.ap[:], [dB], [Res()])
            AR.release(m1)
            qTb = AR.alloc(2 * T, BF16, "qTb")
            kTb = AR.alloc(2 * T, BF16, "kTb")
            q3 = qTb.ap[:].rearrange("p (c t) -> p c t", c=2)
            k3 = kTb.ap[:].rearrange("p (c t) -> p c t", c=2)
            vtok = AR.alloc(NT * 4 * 129, BF16, "vtokB")
            v4 = vtok.ap[:].rearrange("p (t h d) -> p t h d", t=NT, h=4)
            kttok = AR.alloc(NT * 2 * 256, BF16, "kttok")
            kt5 = kttok.ap[:].rearrange("p (t d h k) -> p t d h k", t=NT, d=2, h=4)
            memset("gpsimd", vtok.ap[:], 1.0, [vtok])
            m2 = AR.mark()
            pre = AR.alloc(2 * T, F32, "pre")
            pre3 = pre.ap[:].rearrange("p (c t) -> p c t", c=2)
            cv = AR.alloc(2 * T, F32, "cv")
            cv3 = cv.ap[:].rearrange("p (c t) -> p c t", c=2)
            o_cw, _ = po["convw"]
            for qk, (chb, dst, dst3) in enumerate(((CH_BQ, qTb, q3), (CH_BK, kTb, k3))):
                ws = [load_w(win_b, l * NCH_IN + chb + c) for c in range(2)]
                for c in range(2):
                    for bi, (t0, n) in enumerate(BLKS):
                        ps = proj_fm(ws[c], t0, n, bi)
                        cp("scalar" if bi % 2 else "vector", pre3[:, c, t0:t0 + n], ps.ap[:, 0:n], [ps], [pre])
                    def cw(j, c=c, qk=qk):
                        i_ = o_cw + ((l * 2 + qk) * 2 + c) * 3 + j
                        return pv.ap[:, i_:i_ + 1]
                    ts("vector", cv3[:, c, :], pre3[:, c, :], cw(1), None, MUL, None, [pre, pv], [cv])
                    for (a, b_) in ((0, NCTX), (NCTX, T)):
                        stt(cv3[:, c, a + 1:b_], pre3[:, c, a:b_ - 1], cw(0), cv3[:, c, a + 1:b_], MUL, ADD, [pre, cv, pv], [cv])
                        stt(cv3[:, c, a:b_ - 1], pre3[:, c, a + 1:b_], cw(2), cv3[:, c, a:b_ - 1], MUL, ADD, [pre, cv, pv], [cv])
                    act(dst3[:, c, :], cv3[:, c, :], AF.Silu, [cv], [dst])
            AR.release(m2)
            wvs = [load_w(win_b, l * NCH_IN + CH_BV + h) for h in range(4)]
            for tl in range(NT):
                ps = bankA()
                for h in range(4):
                    proj_tm(wvs[h], tl, ps, h * 128)
                cp("vector" if tl % 2 else "scalar", v4[:, tl, :, 0:128], ps.ap[:, 0:512].rearrange("p (h d) -> p h d", h=4), [ps], [vtok])
                psk = bankA()
                pkb = psk.ap[:].bitcast(BF16)
                for c in range(2):
                    tp(pkb[:, c * 128:(c + 1) * 128], k3[:, c, tl * 128:(tl + 1) * 128], ident_b, [kTb, cbf], [psk])
                for d in range(2):
                    tt("vector", kt5[:, tl, d, :, :], pkb[:, 0:256].rearrange("p (h k) -> p h k", h=4),
                       tsc3[:, tl, d * 4:(d + 1) * 4].unsqueeze(2).to_broadcast([128, 4, 64]), MUL, [psk, tokscal], [kttok])
            nd = AR.alloc(NT * 129, F32, "nd")
            nd3 = nd.ap[:].rearrange("p (t d) -> p t d", t=NT)
            hh = AR.alloc(NT * 128, F32, "hh")
            hh3 = hh.ap[:].rearrange("p (t d) -> p t d", t=NT)
            sgo = AR.alloc(NT * 128, BF16, "sgo")
            sgo3 = sgo.ap[:].rearrange("p (t d) -> p t d", t=NT)
            ybt = AR.alloc(NT * 128, BF16, "ybt")
            ybt3 = ybt.ap[:].rearrange("p (t d) -> p t d", t=NT)
            Cst = AR.alloc(129, F32, "Cst")
            Cb = [AR.alloc(129, BF16, "Cb%d" % i) for i in range(2)]
            STs = [AR.alloc(64, BF16, "ST%d" % i) for i in range(3)]
            sn = AR.alloc(4 * NT, F32, "sn")
            sn3 = sn.ap[:].rearrange("p (a t) -> p a t", a=4)
            o_mng, _ = po["mng"]
            for h in range(4):
                hb = (h % 2) * 64
                ch = h // 2
                wo = load_w(win_b, l * NCH_IN + CH_BO + h)
                for tl in range(NT):
                    ps = bankA()
                    proj_tm(wo, tl, ps, 0)
                    act(sgo3[:, tl, :], ps.ap[:, 0:128], AF.Sigmoid, [ps], [sgo])
                for d in range(2):
                    r = d * 4 + h
                    memset("vector", Cst.ap[hb:hb + 64, :], 0.0, [Cst])
                    memset("vector", Cb[0].ap[hb:hb + 64, :], 0.0, [Cb[0]])
                    order = vorder(d)

                    def s_stage(ci_, c, d=d, r=r, hb=hb, ch=ch):
                        tl = c // 2
                        b = (c % 2) * 64
                        tk0 = c * 64
                        psS = bankS()
                        mm(psS.ap[b:b + 64, 0:64], k3[hb:hb + 64, ch, tk0:tk0 + 64], q3[hb:hb + 64, ch, tk0:tk0 + 64], True, True, [kTb, qTb], [psS])
                        ST = STs[ci_ % 3]
                        stt(ST.ap[b:b + 64, :], psS.ap[b:b + 64, 0:64], tsc3[b:b + 64, tl, r:r + 1], masks[b:b + 64, d * 64:(d + 1) * 64], MUL, MUL,
                            [psS, tokscal, cst], [ST])
                        return ST
                    ST_next = s_stage(0, order[0])
                    for ci_, c in enumerate(order):
                        tl = c // 2
                        b = (c % 2) * 64
                        tk0 = c * 64
                        cbc = Cb[ci_ % 2]
                        ST = ST_next
                        psD = None
                        if ci_ < len(order) - 1:
                            psD = bankS()
                            mm(psD.ap[hb:hb + 64, 0:129], kt5[b:b + 64, tl, d, h, :], v4[b:b + 64, tl, h, :], True, True, [kttok, vtok], [psD])
                            ST_next = s_stage(ci_ + 1, order[ci_ + 1])
                        psN = bankS()
                        mm(psN.ap[b:b + 64, 0:129], q3[hb:hb + 64, ch, tk0:tk0 + 64], cbc.ap[hb:hb + 64, :], True, False, [qTb, cbc], [psN])
                        mm(psN.ap[b:b + 64, 0:129], ST.ap[b:b + 64, :], v4[b:b + 64, tl, h, :], False, True, [ST, vtok], [psN])
                        cp("scalar", nd3[b:b + 64, tl, :], psN.ap[b:b + 64, 0:129], [psN], [nd])
                        if psD is not None:
                            dcol = dB3[hb:hb + 64, r, c:c + 1]
                            ts("vector", Cst.ap[hb:hb + 64, :], Cst.ap[hb:hb + 64, :], dcol, None, MUL, None, [Cst, dB], [Cst])
                            stt(Cst.ap[hb:hb + 64, :], psD.ap[hb:hb + 64, 0:129], dcol, Cst.ap[hb:hb + 64, :], MUL, ADD, [psD, Cst, dB], [Cst])
                            cp("scalar", Cb[(ci_ + 1) % 2].ap[hb:hb + 64, :], Cst.ap[hb:hb + 64, :], [Cst], [Cb[(ci_ + 1) % 2]])
                    wv_ = tsc3[:, :, 8 + r]
                    ev_ = tsc3[:, :, 16 + r]
                    tt("vector", sn3[:, 0, :], nd3[:, :, 128], wv_, MUL, [nd, tokscal], [sn])
                    act(sn3[:, 0, :], sn3[:, 0, :], AF.Abs, [sn], [sn])
                    tt("vector", sn3[:, 0, :], sn3[:, 0, :], ev_, MAX, [sn, tokscal], [sn])
                    recip(sn3[:, 1, :], sn3[:, 0, :], [sn], [sn])
                    tt("vector", sn3[:, 1, :], sn3[:, 1, :], wv_, MUL, [sn, tokscal], [sn])
                    rwb = sn3[:, 1, :].unsqueeze(2).to_broadcast([128, NT, 128])
                    if d == 0:
                        tt("vector", hh3, nd3[:, :, 0:128], rwb, MUL, [nd, sn], [hh])
                    else:
                        tmpn = nd3[:, :, 0:128]
                        tt("vector", tmpn, tmpn, rwb, MUL, [nd, sn], [nd])
                        tt("gpsimd", hh3, hh3, tmpn, ADD, [nd, hh], [hh])
                sqn = nd3[:, :, 0:128]
                tt("vector", sqn, hh3, hh3, MUL, [hh], [nd])
                S.op("vector", (lambda o_, i_: (lambda e: e.tensor_reduce(out=o_, in_=i_, axis=mybir.AxisListType.X, op=ADD)))(sn3[:, 2, :], sqn), R_([nd]), R_([sn]))
                act(sn3[:, 2, :], sn3[:, 2, :], AF.Sqrt, [sn, smallc], [sn], scale=1.0 / 128, bias=smallc.ap[:, 3:4])
                recip(sn3[:, 2, :], sn3[:, 2, :], [sn], [sn])
                tt("vector", hh3, hh3, sn3[:, 2, :].unsqueeze(2).to_broadcast([128, NT, 128]), MUL, [hh, sn], [hh])
                tt("vector", hh3, hh3, pv.ap[:, o_mng + l * 128:o_mng + (l + 1) * 128].unsqueeze(1).to_broadcast([128, NT, 128]), MUL, [hh, pv], [hh])
                tt("vector", ybt3, hh3, sgo3, MUL, [hh, sgo], [ybt])
                for t0 in range(0, NT, 4):
                    nt_ = min(4, NT - t0)
                    ps = bankA()
                    pb = ps.ap[:].bitcast(BF16)
                    for i_ in range(nt_):
                        tp(pb[:, i_ * 128:(i_ + 1) * 128], ybt3[:, t0 + i_, :], ident_b, [ybt, cbf], [ps])
                    yt = nxt(tmpB, "B")
                    cp("scalar", yt.ap[:, 0:nt_ * 128], pb[:, 0:nt_ * 128], [ps], [yt])
                    dma("gpsimd", yT_d[1 * 4 + h][:, t0 * 128:(t0 + nt_) * 128], yt.ap[:, 0:nt_ * 128], [yt], [yres_d])
            AR.release(m_)

        def p34(l, s, sq, last):
            m_ = AR.mark()
            xb = AR.alloc(8 * 512, F32, "xb")
            xb3 = xb.ap[:].rearrange("p (k t) -> p k t", k=8)

            mT = AR.alloc(8 * 512, BF16, "mT")
            mT3 = mT.ap[:].rearrange("p (k t) -> p k t", k=8)
            acc = AR.alloc(512, F32, "acc")
            h2 = Buf(mT.ap, "h2")
            h2.res = mT.res
            h23 = h2.ap[:].rearrange("p (k t) -> p k t", k=8)
            uT = AR.alloc(32 * 512, BF16, "uT")
            uT3 = uT.ap[:].rearrange("p (j t) -> p j t", j=32)
            yb = Buf(uT.ap[:, 0:16 * 512], "ybl")
            yb.res = uT.res
            yb3 = yb.ap[:].rearrange("p (j t) -> p j t", j=16)
            ot = Buf(uT.ap[:, 0:16 * 512].bitcast(F32), "ot")
            ot.res = uT.res
            ot3 = ot.ap[:].rearrange("p (k t) -> p k t", k=8)
            wg4 = [AR.alloc(4 * 1024, BF16, "wg4_%d" % i) for i in range(2)]
            wb4 = [AR.alloc(2048, BF16, "wb4_%d" % i) for i in range(2)]
            w2s = [AR.alloc(4096, BF16, "w2s_%d" % i) for i in range(2)]
            xsrc = xT_in[sq] if l == 0 else xT_d
            xsrc3 = xsrc.rearrange("p (k t) -> p k t", k=8)
            xdst3 = xT_d.rearrange("p (k t) -> p k t", k=8)
            cnt = 0
            for bi, (t0, n) in enumerate(BLKS):
                sc = 4 if t0 < NCTX else s
                dma("sync", xb3[:, :, 0:n], xsrc3[:, :, t0:t0 + n], [xres[bi]], [xb])
                dma("sync", yb3[:, :, 0:n], yT_d[:, :, t0:t0 + n].rearrange("j p t -> p j t"), [yres_d], [yb])
                for i in range(8):
                    wg = wg4[cnt % 2]
                    wb = wb4[cnt % 2]
                    cnt += 1
                    j0 = l * NCH_IN + CH_GATES + i * 4
                    dma("sync", wg.ap[:].rearrange("p (g w) -> p g w", g=4), win_b[j0:j0 + 4].rearrange("g p w -> p g w"), [wres], [wg])
                    dma("sync", wb.ap[:], wbr_b[l * 8 + i], [wres], [wb])
                    wg3 = wg.ap[:].rearrange("p (g k c) -> p g k c", g=4, k=8)
                    wb3 = wb.ap[:].rearrange("p (r k c) -> p r k c", r=4, k=4)
                    for r in range(4):
                        psg = bankA()
                        for kc in range(8):
                            mm(psg.ap[:, 0:n], wg3[:, r, kc, :], hT3[:, kc, t0:t0 + n], kc == 0, kc == 7, [wg, hres[bi]], [psg])
                        gsig = nxt(tmpF, "F")
                        act(gsig.ap[:, 0:n], psg.ap[:, 0:n], AF.Sigmoid, [psg], [gsig])
                        psz = bankA()
                        for kc in range(4):
                            mm(psz.ap[:, 0:n], wb3[:, r, kc, :], yb3[:, r * 4 + kc, 0:n], kc == 0, kc == 3, [wb, yb], [psz])
                        if r == 0:
                            tt("vector", acc.ap[:, 0:n], psz.ap[:, 0:n], gsig.ap[:, 0:n], MUL, [psz, gsig], [acc])
                        else:
                            t_ = nxt(tmpF, "F")
                            tt("vector", t_.ap[:, 0:n], psz.ap[:, 0:n], gsig.ap[:, 0:n], MUL, [psz, gsig], [t_])
                            if r < 3:
                                tt("gpsimd", acc.ap[:, 0:n], acc.ap[:, 0:n], t_.ap[:, 0:n], ADD, [acc, t_], [acc])
                            else:
                                tt("gpsimd", mT3[:, i, 0:n], acc.ap[:, 0:n], t_.ap[:, 0:n], ADD, [acc, t_], [mT])
                for i in range(8):
                    wo = load_w(wout_b, l * 8 + i)
                    wo3 = wo.ap[:].rearrange("p (k c) -> p k c", k=8)
                    ps = bankA()
                    for kc in range(8):
                        mm(ps.ap[:, 0:n], wo3[:, kc, :], mT3[:, kc, 0:n], kc == 0, kc == 7, [wo, mT], [ps])
                    stt(xb3[:, i, 0:n], ps.ap[:, 0:n], modc(l, 2, i, sc), xb3[:, i, 0:n], MUL, ADD, [ps, xb, modT], [xb])
                if dbg and l == NL - 1:
                    dma("gpsimd", xmid_dbg.rearrange("p (k t) -> p k t", k=8)[:, :, t0:t0 + n], xb3[:, :, 0:n], [xb], [Res()])
                norm_mod(xb3, n, l, 4, 3, sc, h23, 0, [xb], [h2])
                for j in range(32):
                    w1 = load_w(wff1_b, l * 32 + j)
                    w13 = w1.ap[:].rearrange("p (k c) -> p k c", k=8)
                    ps = bankA()
                    for kc in range(8):
                        mm(ps.ap[:, 0:n], w13[:, kc, :], h23[:, kc, 0:n], kc == 0, kc == 7, [w1, h2], [ps])
                    r_ = nxt(tmpF, "F")
                    act(r_.ap[:, 0:n], ps.ap[:, 0:n], AF.Relu, [ps], [r_])
                    tt("vector" if j % 2 else "gpsimd", uT3[:, j, 0:n], r_.ap[:, 0:n], r_.ap[:, 0:n], MUL, [r_], [uT])
                for i in range(8):
                    w2 = w2s[cnt % 2]
                    cnt += 1
                    dma("sync", w2.ap[:], wff2_b[l * 8 + i], [wres], [w2])
                    w23 = w2.ap[:].rearrange("p (k c) -> p k c", k=32)
                    ps = bankA()
                    for kc in range(32):
                        mm(ps.ap[:, 0:n], w23[:, kc, :], uT3[:, kc, 0:n], kc == 0, kc == 31, [w2, uT], [ps])
                    stt(xb3[:, i, 0:n], ps.ap[:, 0:n], modc(l, 5, i, sc), xb3[:, i, 0:n], MUL, ADD, [ps, xb, modT], [xb])
                if not last:
                    dma("gpsimd", xdst3[:, :, t0:t0 + n], xb3[:, :, 0:n], [xb], [xres[bi]])
                elif t0 >= NCTX:
                    ps = bankA()
                    for kc in range(8):
                        sq_ = nxt(tmpB, "B")
                        act(sq_.ap[:, 0:n], xb3[:, kc, 0:n], AF.Square, [xb], [sq_])
                        mm(ps.ap[:, 0:n], ones_b, sq_.ap[:, 0:n], kc == 0, kc == 7, [sq_, cbf], [ps])
                    rs = nxt(tmpR, "R")
                    act(rs.ap[:, 0:n], ps.ap[:, 0:n], AF.Sqrt, [ps, smallc], [rs], scale=1.0 / DM, bias=smallc.ap[:, 3:4])
                    recip(rs.ap[:, 0:n], rs.ap[:, 0:n], [rs], [rs])
                    for kc in range(8):
                        stt(ot3[:, kc, 0:n], xb3[:, kc, 0:n], pvc("gfin", kc), rs.ap[:, 0:n], MUL, MUL, [xb, rs, pv], [ot])
                    od = dma("gpsimd", outT[sq].rearrange("p (k t) -> p k t", k=8)[:, :, t0 - NCTX:t0 - NCTX + n], ot3[:, :, 0:n], [ot], [Res()])
                    out_deps.append(od)
            AR.release(m_)

        xres = [Res("x%d" % i) for i in range(len(BLKS))]
        out_deps = []
        xmid_dbg = dscr("xmid_dbg", [128, 8 * T], F32) if dbg else None
        tsc_dbg = dscr("tsc_dbg", [128, NT * 24], F32) if dbg else None
        dB_dbg = dscr("dB_dbg", [128, 288], F32) if dbg else None
        zt = None

        for sq in range(NSEQ):
            for l in range(NL):
                m_ = AR.mark()
                xbs = [AR.alloc(8 * 512, F32, "xb1_%d" % i) for i in range(2)]
                xsrc = xT_in[sq] if l == 0 else xT_d
                xsrc3 = xsrc.rearrange("p (k t) -> p k t", k=8)
                for bi, (t0, n) in enumerate(BLKS):
                    xb = xbs[bi % 2]
                    xb3 = xb.ap[:].rearrange("p (k t) -> p k t", k=8)
                    dma("sync", xb3[:, :, 0:n], xsrc3[:, :, t0:t0 + n], [xres[bi]], [xb])
                    sc = 4 if t0 < NCTX else sq
                    norm_mod(xb3, n, l, 1, 0, sc, hT3, t0, [xb], [hres[bi]])
                if dbg and l == NL - 1:
                    dma("gpsimd", hT_dbg[:, :], hT.ap[:], hres, [Res()])
                AR.release(m_)
                for nm, fn, br in (("A", mixer_A, 0), ("B", mixer_B, 1), ("C", mixer_C, 2), ("D", mixer_D, 3)):
                    if nm in stages:
                        fn(l, sq)
                    else:
                        z = nxt(tmpB, "B")
                        memset("vector", z.ap[:], 0.0, [z])
                        for c in range(4):
                            for (t0, n) in BLKS:
                                dma("gpsimd", yT_d[br * 4 + c][:, t0:t0 + n], z.ap[:, 0:n], [z], [yres_d])
                S.barrier()
                p34(l, sq, sq, l == NL - 1)
        S.barrier()
        S.ops["sync"].append(([(k, S.cnt[k]) for k in S.dma_pool["gpsimd"] + S.dma_pool["sync"] if S.cnt[k] > 0], None, None, 0))
        print("ops recorded:", S.nops, flush=True)
        with nc.Block() as block:
            S.replay(block)
    return nc


def make_inputs(inp, core, nseq, NL=4):
    x = inp["x"][core * nseq:(core + 1) * nseq]
    ctx = inp["ctx"][core * nseq:(core + 1) * nseq]
    xc = np.concatenate([ctx, x], axis=1)
    xT = np.ascontiguousarray(xc.reshape(nseq, T, 8, 128).transpose(0, 3, 2, 1)).reshape(nseq, 128, 8 * T)
    return {"xT": xT, "pv": host_pv(inp, core, nseq)}


_SHARED = {}


def shared_inputs(inp):
    cst, rope = host_consts()
    ich = in_chunks()
    win = np.concatenate([layout_w(inp["w_in"][l], ich, 8) for l in range(L)], axis=0)
    wff1 = np.concatenate([layout_w(inp["w_ff1"][l], simple_chunks(32), 8) for l in range(L)], axis=0)
    wout = np.concatenate([layout_w(inp["w_out"][l], simple_chunks(8), 8) for l in range(L)], axis=0)
    wbr = np.concatenate([layout_w(inp["w_branch"][l].reshape(2048, 1024), simple_chunks(8), 16) for l in range(L)], axis=0)
    wff2 = np.concatenate([layout_w(inp["w_ff2"][l], simple_chunks(8), 32) for l in range(L)], axis=0)
    return {"cst": cst, "rope": rope, "w_ada": np.ascontiguousarray(inp["w_ada"]), "win": win, "wff1": wff1, "wout": wout,
            "wbr": wbr, "wff2": wff2, "wuq": np.ascontiguousarray(inp["w_mla_uq"]), "wuk": np.ascontiguousarray(inp["w_mla_uk"]),
            "wuv": np.ascontiguousarray(inp["w_mla_uv"])}


def kernel(**inputs):
    inp = {k: np.asarray(v, dtype=np.float32) for k, v in inputs.items()}
    ncores = 8
    nseq = inp["x"].shape[0] // ncores
    sh = shared_inputs(inp)
    in_maps = []
    for c in range(ncores):
        m = dict(sh)
        m.update(make_inputs(inp, c, nseq))
        in_maps.append(m)
    nc = build(NSEQ=nseq, NL=L)
    res = run_bass_kernel_spmd(nc, in_maps, core_ids=list(range(ncores)))
    outs = []
    for c in range(ncores):
        o = np.asarray(res.results[c]["outT"]).reshape(nseq, 128, 8, NLAT)
        outs.append(o.transpose(0, 3, 2, 1).reshape(nseq, NLAT, DM))
    return np.ascontiguousarray(np.concatenate(outs, axis=0)).astype(np.float32)
```
